# Optimizing a Trainium2 kernel written in Bass

```python
import numpy as np
import jax, jax.numpy as jnp
from jax import lax

D_MODEL = 1024
BATCH = 2
SEQ = 8192
DEPTH = 2

HEAD_DIM = 64
GM_GROUPS = 4
GM_WIDTH = GM_GROUPS * HEAD_DIM
GM_CHUNK = 128
NSA_HEADS = 8
NSA_KV_GROUPS = 2
NSA_HPG = NSA_HEADS // NSA_KV_GROUPS
NSA_WIDTH = NSA_HEADS * HEAD_DIM
NSA_KV_WIDTH = NSA_KV_GROUPS * HEAD_DIM
CMP_BLOCK = 32
CMP_STRIDE = 16
CMP_HIDDEN = 128
SEL_BLOCK = 64
SEL_TOPK = 16
N_LOCAL_SEL = 2
WINDOW = 512
Q_BLOCK = 128
MEM_HEADS = 4
MEM_WIDTH = MEM_HEADS * HEAD_DIM
MEM_LEN = 256
MIX_WIDTH = GM_WIDTH + NSA_WIDTH + MEM_WIDTH
ROPE_THETA = 10000.0
LN_EPS = 1e-5
ALPHA = (2.0 * DEPTH) ** 0.25
BETA = (8.0 * DEPTH) ** -0.25
NEG_INF = -1e30
FORCE_SCORE = 1e4

IN_SPLITS = (GM_WIDTH, GM_WIDTH, GM_WIDTH,
             NSA_WIDTH,
             NSA_KV_WIDTH, NSA_KV_WIDTH,
             NSA_KV_WIDTH, NSA_KV_WIDTH,
             NSA_KV_WIDTH, NSA_KV_WIDTH,
             NSA_HEADS * 3,
             NSA_WIDTH,
             MEM_WIDTH, MEM_WIDTH)
IN_COLS = int(sum(IN_SPLITS))
SPLIT_POINTS = tuple(int(c) for c in np.cumsum(IN_SPLITS)[:-1])

kernel_name = 'hymba_gmlp_nsa_memory_deepnorm'


def layer_norm(x, g, b):
    xf = x.astype(jnp.float32)
    mu = jnp.mean(xf, axis=-1, keepdims=True)
    var = jnp.mean(jnp.square(xf - mu), axis=-1, keepdims=True)
    y = (xf - mu) * lax.rsqrt(var + LN_EPS)
    return (y * g.astype(jnp.float32) + b.astype(jnp.float32)).astype(x.dtype)


def rope(x, pos):
    half = x.shape[-1] // 2
    inv_freq = ROPE_THETA ** (-jnp.arange(half, dtype=jnp.float32) * 2.0 / x.shape[-1])
    ang = pos.astype(jnp.float32)[:, None] * inv_freq[None, :]
    cos = jnp.cos(ang)[:, None, :].astype(x.dtype)
    sin = jnp.sin(ang)[:, None, :].astype(x.dtype)
    x1, x2 = x[..., :half], x[..., half:]
    return jnp.concatenate([x1 * cos - x2 * sin, x2 * cos + x1 * sin], axis=-1)


def masked_softmax(s, mask):
    p = jax.nn.softmax(jnp.where(mask, s, NEG_INF), axis=-1)
    return p * mask.astype(jnp.float32)


def gmlp_mixer(u, v, z, ln_g, ln_b, ws, bs):
    B, S, _ = u.shape
    u = jax.nn.gelu(u, approximate=False)
    v = jax.nn.gelu(v, approximate=False).reshape(B, S, GM_GROUPS, HEAD_DIM)
    v = layer_norm(v, ln_g, ln_b)
    n_chunk = S // GM_CHUNK
    v = v.reshape(B, n_chunk, GM_CHUNK, GM_GROUPS, HEAD_DIM)
    causal = jnp.tril(jnp.ones((GM_CHUNK, GM_CHUNK), ws.dtype))
    s = jnp.einsum('gij,bnjgc->bnigc', ws * causal[None], v) + bs.T[:, :, None]
    s = s.reshape(B, S, GM_WIDTH)
    return u * s * jax.nn.silu(z)


def nsa_mixer(q, kc, vc, ks, vs, kw, vw, gate_logits, z,
              cmp_pos_k, cmp_k_w1, cmp_k_w2, cmp_pos_v, cmp_v_w1, cmp_v_w2):
    B, S, _ = q.shape
    G, Hg, dh = NSA_KV_GROUPS, NSA_HPG, HEAD_DIM
    dtype = q.dtype
    pos = jnp.arange(S)
    q = rope(q.reshape(B, S, NSA_HEADS, dh), pos)
    kc = rope(kc.reshape(B, S, G, dh), pos)
    ks = rope(ks.reshape(B, S, G, dh), pos)
    kw = rope(kw.reshape(B, S, G, dh), pos)
    vc = vc.reshape(B, S, G, dh)
    vs = vs.reshape(B, S, G, dh)
    vw = vw.reshape(B, S, G, dh)

    n_cmp = (S - CMP_BLOCK) // CMP_STRIDE + 1
    cidx = np.arange(n_cmp)[:, None] * CMP_STRIDE + np.arange(CMP_BLOCK)[None, :]

    def compress(t, pos_emb, w1, w2):
        blk = t[:, cidx] + pos_emb[None, None, :, None, :]
        blk = blk.transpose(0, 3, 1, 2, 4).reshape(B, G, n_cmp, CMP_BLOCK * dh)
        return jax.nn.gelu(blk @ w1) @ w2

    k_cmp = compress(kc, cmp_pos_k, cmp_k_w1, cmp_k_w2)
    v_cmp = compress(vc, cmp_pos_v, cmp_v_w1, cmp_v_w2)

    n_sel = S // SEL_BLOCK
    topk = min(SEL_TOPK, n_sel)
    k_sel = ks.transpose(0, 2, 1, 3).reshape(B, G, n_sel, SEL_BLOCK, dh)
    v_sel = vs.transpose(0, 2, 1, 3).reshape(B, G, n_sel, SEL_BLOCK, dh)
    c_start = np.arange(n_cmp) * CMP_STRIDE
    s_start = np.arange(n_sel) * SEL_BLOCK
    overlap = jnp.asarray(((c_start[:, None] < s_start[None, :] + SEL_BLOCK) &
                           (c_start[:, None] + CMP_BLOCK > s_start[None, :])).astype(np.float32))

    k_win = jnp.pad(kw.transpose(0, 2, 1, 3), ((0, 0), (0, 0), (WINDOW, 0), (0, 0)))
    v_win = jnp.pad(vw.transpose(0, 2, 1, 3), ((0, 0), (0, 0), (WINDOW, 0), (0, 0)))

    n_qb = S // Q_BLOCK
    q_blocks = q.reshape(B, n_qb, Q_BLOCK, G, Hg, dh).transpose(1, 0, 3, 4, 2, 5)
    scale = dh ** -0.5
    gather = jax.vmap(jax.vmap(lambda kb, ix: kb[ix]))

    def block_fn(args):
        qblk, bi = args
        t = bi * Q_BLOCK + jnp.arange(Q_BLOCK)
        s_c = jnp.einsum('bghqd,bgnd->bghqn', qblk, k_cmp).astype(jnp.float32) * scale
        cmask = (jnp.arange(n_cmp) * CMP_STRIDE + CMP_BLOCK - 1)[None, :] <= t[:, None]
        p_c = masked_softmax(s_c, cmask)
        o_c = jnp.einsum('bghqn,bgnd->bghqd', p_c.astype(dtype), v_cmp)
        imp = jnp.einsum('bghqn,nj->bgqj', p_c, overlap)
        blk = jnp.arange(n_sel)[None, :]
        t_blk = (t // SEL_BLOCK)[:, None]
        valid = blk <= t_blk
        forced = (blk == 0) | (valid & (blk > t_blk - N_LOCAL_SEL))
        imp = jnp.where(forced, FORCE_SCORE, jnp.where(valid, imp, -1.0))
        _, top_idx = lax.top_k(imp, topk)
        kg = gather(k_sel, top_idx)
        vg = gather(v_sel, top_idx)
        s_s = jnp.einsum('bghqd,bgqkld->bghqkl', qblk, kg).astype(jnp.float32) * scale
        key_pos = top_idx[..., None] * SEL_BLOCK + jnp.arange(SEL_BLOCK)
        smask = key_pos <= t[None, None, :, None, None]
        n_keys = topk * SEL_BLOCK
        p_s = masked_softmax(s_s.reshape(B, G, Hg, Q_BLOCK, n_keys),
                             smask.reshape(B, G, 1, Q_BLOCK, n_keys))
        o_s = jnp.einsum('bghqm,bgqmd->bghqd', p_s.astype(dtype),
                         vg.reshape(B, G, Q_BLOCK, n_keys, dh))
        start = bi * Q_BLOCK
        kwb = lax.dynamic_slice_in_dim(k_win, start, WINDOW + Q_BLOCK, axis=2)
        vwb = lax.dynamic_slice_in_dim(v_win, start, WINDOW + Q_BLOCK, axis=2)
        s_w = jnp.einsum('bghqd,bgmd->bghqm', qblk, kwb).astype(jnp.float32) * scale
        kpos = start - WINDOW + jnp.arange(WINDOW + Q_BLOCK)
        wmask = ((kpos[None, :] <= t[:, None]) & (kpos[None, :] > t[:, None] - WINDOW)
                 & (kpos[None, :] >= 0))
        p_w = masked_softmax(s_w, wmask)
        o_w = jnp.einsum('bghqm,bgmd->bghqd', p_w.astype(dtype), vwb)
        return jnp.stack([o_c, o_s, o_w], axis=-1)

    outs = lax.map(block_fn, (q_blocks, jnp.arange(n_qb)))
    outs = outs.transpose(1, 0, 4, 2, 3, 5, 6).reshape(B, S, NSA_HEADS, dh, 3)
    gates = jax.nn.sigmoid(gate_logits.astype(jnp.float32)).reshape(B, S, NSA_HEADS, 3)
    o = jnp.einsum('bshdc,bshc->bshd', outs, gates.astype(dtype)).reshape(B, S, NSA_WIDTH)
    return o * jax.nn.silu(z)


def memory_mixer(q, z, mem, w_mem_kv):
    B, S, _ = q.shape
    kv = (mem @ w_mem_kv).reshape(mem.shape[0], mem.shape[1], 2, MEM_HEADS, HEAD_DIM)
    k, v = kv[:, :, 0], kv[:, :, 1]
    q = q.reshape(B, S, MEM_HEADS, HEAD_DIM)
    s = jnp.einsum('bshd,bmhd->bhsm', q, k).astype(jnp.float32) * HEAD_DIM ** -0.5
    p = jax.nn.softmax(s, axis=-1)
    o = jnp.einsum('bhsm,bmhd->bshd', p.astype(q.dtype), v).reshape(B, S, MEM_WIDTH)
    return o * jax.nn.silu(z)


def hybrid_layer(x, mem, w_in, gm_ln_g, gm_ln_b, gm_ws, gm_bs,
                 cmp_pos_k, cmp_k_w1, cmp_k_w2, cmp_pos_v, cmp_v_w1, cmp_v_w2,
                 w_mem_kv, w_out, ln_g, ln_b):
    h = x @ w_in
    (gm_u, gm_v, gm_z, nq, nkc, nvc, nks, nvs, nkw, nvw, ngate, nz,
     mq, mz) = jnp.split(h, SPLIT_POINTS, axis=-1)
    y_gm = gmlp_mixer(gm_u, gm_v, gm_z, gm_ln_g, gm_ln_b, gm_ws, gm_bs)
    y_nsa = nsa_mixer(nq, nkc, nvc, nks, nvs, nkw, nvw, ngate, nz,
                      cmp_pos_k, cmp_k_w1, cmp_k_w2, cmp_pos_v, cmp_v_w1, cmp_v_w2)
    y_mem = memory_mixer(mq, mz, mem, w_mem_kv)
    y = jnp.concatenate([y_gm, y_nsa, y_mem], axis=-1) @ w_out
    return layer_norm(ALPHA * x + y, ln_g, ln_b)


def setup_inputs(seed: int = 0) -> dict:
    key = jax.random.key(seed)
    ks = jax.random.split(key, 20)
    f32 = jnp.float32
    nrm = lambda k, shape, s: jax.random.normal(k, shape, f32) * s
    L = DEPTH
    return {
        'x': nrm(ks[0], (BATCH, SEQ, D_MODEL), 1.0),
        'mem': nrm(ks[1], (BATCH, MEM_LEN, D_MODEL), 1.0),
        'w_in': nrm(ks[2], (L, D_MODEL, IN_COLS), D_MODEL ** -0.5),
        'gm_ln_g': 1.0 + nrm(ks[3], (L, GM_GROUPS, HEAD_DIM), 0.01),
        'gm_ln_b': nrm(ks[4], (L, GM_GROUPS, HEAD_DIM), 0.01),
        'gm_ws': nrm(ks[5], (L, GM_GROUPS, GM_CHUNK, GM_CHUNK), GM_CHUNK ** -0.5),
        'gm_bs': 1.0 + nrm(ks[6], (L, GM_GROUPS, GM_CHUNK), 0.1),
        'cmp_pos_k': nrm(ks[7], (L, CMP_BLOCK, HEAD_DIM), 0.1),
        'cmp_k_w1': nrm(ks[8], (L, CMP_BLOCK * HEAD_DIM, CMP_HIDDEN), (CMP_BLOCK * HEAD_DIM) ** -0.5),
        'cmp_k_w2': nrm(ks[9], (L, CMP_HIDDEN, HEAD_DIM), CMP_HIDDEN ** -0.5),
        'cmp_pos_v': nrm(ks[10], (L, CMP_BLOCK, HEAD_DIM), 0.1),
        'cmp_v_w1': nrm(ks[11], (L, CMP_BLOCK * HEAD_DIM, CMP_HIDDEN), (CMP_BLOCK * HEAD_DIM) ** -0.5),
        'cmp_v_w2': nrm(ks[12], (L, CMP_HIDDEN, HEAD_DIM), CMP_HIDDEN ** -0.5),
        'w_mem_kv': nrm(ks[13], (L, D_MODEL, 2 * MEM_WIDTH), D_MODEL ** -0.5),
        'w_out': nrm(ks[14], (L, MIX_WIDTH, D_MODEL), BETA * MIX_WIDTH ** -0.5),
        'ln_g': 1.0 + nrm(ks[15], (L, D_MODEL), 0.01),
        'ln_b': nrm(ks[16], (L, D_MODEL), 0.01),
    }


def reference(x, mem, w_in, gm_ln_g, gm_ln_b, gm_ws, gm_bs,
              cmp_pos_k, cmp_k_w1, cmp_k_w2, cmp_pos_v, cmp_v_w1, cmp_v_w2,
              w_mem_kv, w_out, ln_g, ln_b):
    for l in range(DEPTH):
        x = hybrid_layer(x, mem, w_in[l], gm_ln_g[l], gm_ln_b[l], gm_ws[l], gm_bs[l],
                         cmp_pos_k[l], cmp_k_w1[l], cmp_k_w2[l],
                         cmp_pos_v[l], cmp_v_w1[l], cmp_v_w2[l],
                         w_mem_kv[l], w_out[l], ln_g[l], ln_b[l])
    return x
```

```python
import os
import numpy as np
import ml_dtypes
from contextlib import ExitStack
import concourse.bass as bass
import concourse.mybir as mybir
from concourse.bass_utils import run_bass_kernel_spmd

F32 = mybir.dt.float32
BF16 = mybir.dt.bfloat16
AF = mybir.ActivationFunctionType
ALU = mybir.AluOpType
AX = mybir.AxisListType

D = 1024
SEQ = 8192
NB = 2
DEPTH = 2
NK = 16
NT = NK * 128
ALPHA = (2.0 * DEPTH) ** 0.25
LN_EPS = 1e-5
SCALE = 0.125
BIG = 30000.0
CCW = 12352
O_KC, O_VC, O_KS, O_KW, O_VS, O_VW = 0, 2048, 4096, 6144, 8192, 10272

C_U, C_V, C_Z, C_Q, C_KC, C_VC, C_KS, C_VS, C_KW, C_VW, C_G, C_NZ, C_MQ, C_MZ = (
    0, 256, 512, 768, 1280, 1408, 1536, 1664, 1792, 1920, 2048, 2072, 2584, 2840)


def _swap64(cols):
    cols = np.asarray(cols).reshape(-1, 64)
    return np.concatenate([cols[:, 32:], cols[:, :32]], axis=1).reshape(-1)


def _col_groups():
    ar = np.arange
    groups = []
    for nm, c0 in (("kc", C_KC), ("ks", C_KS), ("kw", C_KW)):
        p = ar(c0, c0 + 128)
        groups.append((nm, "F", p))
    groups.append(("vg", "T", np.concatenate([ar(C_VS, C_VS + 128), ar(C_VW, C_VW + 128)] + [ar(C_G, C_G + 24)] * 5 + [ar(C_G, C_G + 8)])))
    groups.append(("vcmq", "F", np.concatenate([ar(C_VC, C_VC + 128), ar(C_MQ, C_MQ + 256)])))
    groups.append(("uv", "T", np.concatenate([ar(C_U, C_U + 256), ar(C_V, C_V + 256)])))
    groups.append(("zz", "T", np.concatenate([ar(C_Z, C_Z + 256), ar(C_MZ, C_MZ + 256)])))
    groups.append(("nz", "T", ar(C_NZ, C_NZ + 512)))
    for hg in range(4):
        p = np.concatenate([ar(C_Q + hg * 64, C_Q + hg * 64 + 64), ar(C_Q + (4 + hg) * 64, C_Q + (4 + hg) * 64 + 64)])
        groups.append((f"q{hg}", "F", p))
    return groups


COLG = _col_groups()
WPC = int(sum(len(g[2]) for g in COLG))


class Sched:
    ENG = ['tensor', 'vector', 'scalar', 'gpsimd', 'sync']
    EPOCH = 30000
    NEPOCH = 4
    NDS = 8

    def __init__(self, nc, es):
        self.nc = nc
        self.es = es
        self.ops = {e: [] for e in self.ENG}
        self.cnt = {e: 0 for e in self.ENG}
        self.sems = {}
        for e in self.ENG:
            for ep in range(self.NEPOCH):
                self.sems[(e, 'e', ep)] = es.enter_context(nc.semaphore(f"s_{e}_{ep}"))
        self.dq = {}
        for q in ['sync', 'gpsimd']:
            self.dq[q] = {'i': 0, 'use': [0] * self.NDS}
            for k in range(self.NDS):
                self.sems[(q, 'd', k)] = es.enter_context(nc.semaphore(f"d_{q}_{k}"))
        self.ncc = 0
        self.waited = {e: {} for e in self.ENG}
        self.lastw = {}
        self.readers = {}

    def op(self, eng, fn, reads=(), writes=(), dma=False):
        deps = []
        for r in reads:
            if r in self.lastw:
                deps.append(self.lastw[r])
        for w in writes:
            if w in self.lastw:
                deps.append(self.lastw[w])
            deps.extend(self.readers.get(w, []))
        if dma == 'cc':
            self.ncc += 1
            sk = ('cc', 'c', self.ncc)
            self.sems[sk] = self.es.enter_context(self.nc.semaphore(f"cc_{self.ncc}"))
            tok = (sk, 1)
        elif dma:
            q = self.dq[eng]
            k = q['i'] % self.NDS
            q['i'] += 1
            q['use'][k] += 1
            sk = (eng, 'd', k)
            val = 16 * q['use'][k]
            if q['use'][k] > 1:
                deps.append((sk, val - 16))
            tok = (sk, val)
        else:
            self.cnt[eng] += 1
            ep = (self.cnt[eng] - 1) // self.EPOCH
            sk = (eng, 'e', ep)
            tok = (sk, self.cnt[eng] - ep * self.EPOCH)
        need = {}
        for (dk, v) in deps:
            if eng == 'tensor' and dk[0] == 'tensor' and dk[1] == 'e':
                continue
            if self.waited[eng].get(dk, 0) >= v:
                continue
            need[dk] = max(need.get(dk, 0), v)
        for dk, v in need.items():
            self.waited[eng][dk] = v
        self.ops[eng].append((list(need.items()), fn, tok))
        for w in writes:
            self.lastw[w] = tok
            self.readers[w] = []
        for r in reads:
            self.readers.setdefault(r, []).append(tok)
        return tok

    def barrier(self):
        toks = set()
        for t in self.lastw.values():
            toks.add(t)
        for rl in self.readers.values():
            toks.update(rl)
        best = {}
        for (dk, v) in toks:
            best[dk] = max(best.get(dk, 0), v)
        for e in self.ENG:
            need = []
            for dk, v in best.items():
                if self.waited[e].get(dk, 0) >= v:
                    continue
                self.waited[e][dk] = v
                need.append((dk, v))
            if need:
                self.ops[e].append((need, None, None))
        self.lastw = {}
        self.readers = {}

    def emit(self):
        with self.nc.Block() as block:
            for e in self.ENG:
                ops = self.ops[e]
                if not ops:
                    continue

                def body(engobj, ops=ops):
                    for waits, fn, tok in ops:
                        for dk, v in waits:
                            engobj.wait_ge(self.sems[dk], v)
                        if fn is None:
                            continue
                        ins = fn(engobj)
                        if tok[0][1] == 'c':
                            ins.then_inc(self.sems[tok[0]])
                        else:
                            ins.then_inc(self.sems[tok[0]], 16 if tok[0][1] == 'd' else 1)
                getattr(block, e)(body)
        self.ops = {e: [] for e in self.ENG}


class _Stop(Exception):
    pass


def _kstop(n):
    import os
    if int(os.environ.get('KSTOP', '99')) <= n:
        raise _Stop()


KBK = int(os.environ.get('KBK', '0'))
KBG = int(os.environ.get('KBG', '0'))


def bc(ap, shape):
    return ap.to_broadcast(list(shape))


def build(nlayers=DEPTH, stop_after=None, debug=False):
    nc = bass.Bass("TRN2", target_bir_lowering=False)
    dt_in = lambda n, s, d=F32: nc.dram_tensor(n, list(s), d, kind="ExternalInput").ap()
    x_in = dt_in("x_own", [NT, D])
    mem_in = dt_in("mem_b", [256, D])
    wp_in = dt_in("w_in_p", [DEPTH, D, WPC])
    gm_ln_g = dt_in("gm_ln_g", [DEPTH, 1, 256])
    gm_ln_b = dt_in("gm_ln_b", [DEPTH, 1, 256])
    gm_ws = dt_in("gm_ws", [DEPTH, 4, 128, 128])
    gm_bsT = dt_in("gm_bsT", [DEPTH, 128, 4])
    posk_in = dt_in("cmp_pos_kT", [DEPTH, 64, 32])
    posv_in = dt_in("cmp_pos_vT", [DEPTH, 64, 32])
    w1k_in = dt_in("cmp_k_w1", [DEPTH, 2048, 128])
    w1v_in = dt_in("cmp_v_w1", [DEPTH, 2048, 128])
    w2k_in = dt_in("cmp_k_w2", [DEPTH, 128, 64])
    w2v_in = dt_in("cmp_v_w2", [DEPTH, 128, 64])
    wmem_in = dt_in("w_mem_kv", [DEPTH, D, 512])
    wout_in = dt_in("w_out", [DEPTH, D, D])
    lng_in = dt_in("ln_g", [DEPTH, 1, D])
    lnb_in = dt_in("ln_b", [DEPTH, 1, D])
    ropeC_in = dt_in("ropeC", [128, NT])
    ropeS_in = dt_in("ropeS", [128, NT])
    eind_in = dt_in("eind", [64, SEQ], BF16)
    zc_in = dt_in("zc", [128, 256], BF16)
    ov_in = dt_in("ovc", [128, 4, 128], BF16)
    bcmp_in = dt_in("bcmp", [128, 4, 128], BF16)
    dmb_in = dt_in("dmb", [128, 4, 128], BF16)
    wmb_in = dt_in("wmb", [128, 8, 128], BF16)
    pm_in = dt_in("pswap", [128, 128], BF16)
    vm_in = dt_in("vmc", [128, NK, 1, 128], BF16)
    out_d = nc.dram_tensor("out", [NT, D], F32, kind="ExternalOutput").ap()
    xmid = nc.dram_tensor("xmid", [NT, D], F32)
    CCN = [4096, 4096, 2080, 2080]
    cc_src = [nc.dram_tensor(f"cc_src{i}", [128, CCN[i]], BF16) for i in range(4)]
    cc_dst = [nc.dram_tensor(f"cc_dst{i}", [512, CCN[i]], BF16) for i in range(4)]
    groups = [[0, 1, 2, 3], [4, 5, 6, 7]]
    dbg = {}
    if debug:
        dbg["qt"] = nc.dram_tensor("dbg_qt", [128, NK * 512], BF16, kind="ExternalOutput").ap()
        for bi, n_ in enumerate([4096, 4096, 2080, 2080]):
            dbg[f"cc{bi}"] = nc.dram_tensor(f"dbg_cc{bi}", [512, n_], BF16, kind="ExternalOutput").ap()
        dbg["ygm"] = nc.dram_tensor("dbg_ygm", [128, NK * 512], BF16, kind="ExternalOutput").ap()
        dbg["sz"] = nc.dram_tensor("dbg_sz", [128, NK * 512], BF16, kind="ExternalOutput").ap()
        dbg["gates"] = nc.dram_tensor("dbg_gates", [128, NK * 24], F32, kind="ExternalOutput").ap()
        dbg["kcmp"] = nc.dram_tensor("dbg_kcmp", [128, 512], BF16, kind="ExternalOutput").ap()
        dbg["rhsc"] = nc.dram_tensor("dbg_rhsc", [128, 8 * 193], BF16, kind="ExternalOutput").ap()
        dbg["mix"] = nc.dram_tensor("dbg_mix", [128, NK * 512], BF16, kind="ExternalOutput").ap()
        dbg["acc"] = nc.dram_tensor("dbg_acc", [128, NK * 2 * 1292], F32, kind="ExternalOutput").ap()
        dbg["nb"] = nc.dram_tensor("dbg_nb", [128, NK * 2 * 128], BF16, kind="ExternalOutput").ap()

    with ExitStack() as es:
        S = Sched(nc, es)
        sbP = lambda n, s, d: es.enter_context(nc.sbuf_tensor(n, list(s), d))
        psb = [es.enter_context(nc.psum_tensor(f"psb{i}", [128, 512], F32)) for i in range(7)]
        psT = es.enter_context(nc.psum_tensor("psT", [128, 1024], BF16))
        rr = {'i': 0}

        def nbank(pool=(4, 5, 6)):
            b = pool[rr['i'] % len(pool)]
            rr['i'] += 1
            return b

        QT = sbP("QT", [128, NK, 4, 128], BF16)
        YGM = sbP("YGM", [128, NK, 512], BF16)
        SZ = sbP("SZ", [128, NK, 512], BF16)
        GATES = sbP("GATES", [128, NK, 24], F32)
        identF = sbP("identF", [128, 128], F32)
        identB = sbP("identB", [128, 128], BF16)
        ZC = sbP("ZC", [128, 256], BF16)
        BCMP = sbP("BCMP", [128, 4, 128], BF16)
        DMB = sbP("DMB", [128, 4, 128], BF16)
        WMB = sbP("WMB", [128, 8, 128], BF16)

        V = lambda fn, r, w: S.op('vector', fn, reads=r, writes=w)
        A = lambda fn, r, w: S.op('scalar', fn, reads=r, writes=w)
        G = lambda fn, r, w: S.op('gpsimd', fn, reads=r, writes=w)
        T = lambda fn, r, w: S.op('tensor', fn, reads=r, writes=w)
        DMA = lambda fn, r, w, q='sync': S.op(q, fn, reads=r, writes=w, dma=True)

        G(lambda e: e.memset(identF[:], 0.0), [], ['identF'])
        G(lambda e: e.affine_select(out=identF[:], in_=identF[:], pattern=[[-1, 128]], compare_op=ALU.not_equal,
                                    fill=1.0, base=0, channel_multiplier=1), ['identF'], ['identF'])
        G(lambda e: e.tensor_copy(out=identB[:], in_=identF[:]), ['identF'], ['identB'])
        PMB = sbP("PMB", [128, 128], BF16)
        DMA(lambda e: e.dma_start(out=PMB[:], in_=pm_in[:, :]), [], ['PMB'])
        DMA(lambda e: e.dma_start(out=ZC[:], in_=zc_in[:, :]), [], ['ZC'])
        DMA(lambda e: e.dma_start(out=BCMP[:], in_=bcmp_in[:, :, :]), [], ['BCMP'])
        DMA(lambda e: e.dma_start(out=DMB[:], in_=dmb_in[:, :, :]), [], ['DMB'])
        DMA(lambda e: e.dma_start(out=WMB[:], in_=wmb_in[:, :, :]), [], ['WMB'])

        for l in range(nlayers):
            xsrc = x_in if l == 0 else xmid.ap()
            xdst = out_d if l == nlayers - 1 else xmid.ap()
            with ExitStack() as pa:
                sb = lambda n, s, d: pa.enter_context(nc.sbuf_tensor(f"{n}_A{l}", list(s), d))
                xT = sb("xT", [128, 8, NT], BF16)
                memT = sb("memT", [128, 8, 256], BF16)
                wst = sb("wst", [128, 8, 512], F32)
                wbf = [sb(f"wbf{i}", [128, 8, 512], BF16) for i in range(2)]
                ropeC = sb("ropeC", [128, NT], F32)
                ropeS = sb("ropeS", [128, NT], F32)
                xin = [sb(f"xin{i}", [128, D], F32) for i in range(3)]
                memK = sb("memK", [128, 2, 256], BF16)
                memV = sb("memV", [128, 2, 4, 65], BF16)
                mqT = sb("mqT", [128, 2, NT], BF16)
                wcT = sb("wcT", [128, 4, 128], BF16)
                wsf = sb("wsf", [128, 4, 128], F32)
                bsT = sb("bsT", [128, 4], F32)
                glng = sb("glng", [128, 256], F32)
                glnb = sb("glnb", [128, 256], F32)
                kst = [sb(f"kst{i}", [128, NT], BF16) for i in range(2)]
                vst = [sb(f"vst{i}", [128, NK, 2, 65], BF16) for i in range(2)]
                guL = [sb(f"gu{i}", [128, 256], BF16) for i in range(3)]
                gvL = [sb(f"gv{i}", [128, 4, 64], F32) for i in range(3)]
                gcL = [sb(f"gc{i}", [128, 4, 64], F32) for i in range(2)]
                gsqL = [sb(f"gsq{i}", [128, 4, 64], F32) for i in range(2)]
                mhalf = sb("mhalf", [128, 4], F32)
                G(lambda e: e.memset(mhalf[:], -0.5), [], ['mhalf'])
                vlnL = [sb(f"vln{i}", [128, 256], BF16) for i in range(2)]
                st4L = [sb(f"st4{i}", [128, 16], F32) for i in range(2)]
                pbf = [sb(f"pbf{i}", [128, 512], BF16) for i in range(2)]
                tA = [sb(f"tA{i}", [128, 512], F32) for i in range(2)]
                tB = [sb(f"tB{i}", [128, 512], F32) for i in range(2)]
                pmL = [sb(f"pm{i}", [128, 2, 2, 2, 128], BF16) for i in range(2)]
                om = sb("om", [128, 4, 64], F32)

                try:
                    DMA(lambda e: e.dma_start(out=ropeC[:], in_=ropeC_in[:, :]), [], ['ropeC'])
                    DMA(lambda e: e.dma_start(out=ropeS[:], in_=ropeS_in[:, :]), [], ['ropeS'])
                    for i in range(2):
                        G(lambda e, i=i: e.memset(vst[i][:, :, :, 64:65], 1.0), [], [('vst1', i)])
                    G(lambda e: e.memset(memV[:, :, :, 64:65], 1.0), [], ['memV1'])

                    cp = 0
                    for kt in range(NK + 2):
                        buf = xin[kt % 3]
                        bk = ('xin', kt % 3)
                        if kt < NK:
                            DMA(lambda e, kt=kt, buf=buf: e.dma_start(out=buf[:], in_=xsrc[kt * 128:(kt + 1) * 128, :]), [], [bk])
                        else:
                            m = kt - NK
                            DMA(lambda e, m=m, buf=buf: e.dma_start(out=buf[:], in_=mem_in[m * 128:(m + 1) * 128, :]), [], [bk])
                        for half in range(2):
                            b = nbank((0, 1, 2, 3))
                            for cc in range(4):
                                c = half * 4 + cc
                                T(lambda e, b=b, cc=cc, c=c, buf=buf: e.transpose(out=psb[b][:, cc * 128:(cc + 1) * 128],
                                                                                   in_=buf[:, c * 128:(c + 1) * 128], identity=identF[:]),
                                  [bk, 'identF'], [('ps', b)])
                            if kt < NK:
                                dst = xT[:, half * 4:(half + 1) * 4, kt * 128:(kt + 1) * 128]
                                dk = ('xT', kt)
                            else:
                                dst = memT[:, half * 4:(half + 1) * 4, (kt - NK) * 128:(kt - NK + 1) * 128]
                                dk = ('memT', kt - NK, half)
                            src = psb[b][:, :].rearrange("p (a b) -> p a b", a=4)
                            if cp % 2 == 0:
                                V(lambda e, dst=dst, src=src: e.tensor_copy(out=dst, in_=src), [('ps', b)], [(dk, half)])
                            else:
                                A(lambda e, dst=dst, src=src: e.copy(out=dst, in_=src), [('ps', b)], [(dk, half)])
                            cp += 1
                    _kstop(1)
                    xT_keys = [(('xT', kt), h) for kt in range(NK) for h in range(2)]
                    memT_keys = [(('memT', m, h), h) for m in range(2) for h in range(2)]

                    wi = {'i': 0}

                    def load_w(src_ap, ncols):
                        i = wi['i'] % 2
                        wi['i'] += 1
                        DMA(lambda e: e.dma_start(out=wst[:, :, 0:ncols], in_=src_ap.rearrange("(c p) n -> p c n", p=128)),
                            [], ['wst'])
                        V(lambda e: e.tensor_copy(out=wbf[i][:, 0:4, 0:ncols], in_=wst[:, 0:4, 0:ncols]), ['wst'], [('wbf', i, 0)])
                        A(lambda e: e.copy(out=wbf[i][:, 4:8, 0:ncols], in_=wst[:, 4:8, 0:ncols]), ['wst'], [('wbf', i, 1)])
                        return wbf[i], [('wbf', i, 0), ('wbf', i, 1)]

                    wm, wmk = load_w(wmem_in[l], 512)
                    for pr in range(2):
                        b = nbank((0, 1, 2, 3))
                        for c in range(8):
                            T(lambda e, b=b, c=c, pr=pr: e.matmul(psb[b][:, 0:256], lhsT=wm[:, c, pr * 128:(pr + 1) * 128],
                                                                 rhs=memT[:, c, :], start=(c == 0), stop=(c == 7)),
                              wmk + memT_keys, [('ps', b)])
                        V(lambda e, b=b, pr=pr: e.tensor_copy(out=memK[:, pr, :], in_=psb[b][:, 0:256]), [('ps', b)], [('memK', pr)])
                    for mt in range(2):
                        b = nbank((0, 1, 2, 3))
                        for c in range(8):
                            T(lambda e, b=b, c=c, mt=mt: e.matmul(psb[b][:, 0:256], lhsT=memT[:, c, mt * 128:(mt + 1) * 128],
                                                                 rhs=wm[:, c, 256:512], start=(c == 0), stop=(c == 7)),
                              wmk + memT_keys, [('ps', b)])
                        V(lambda e, b=b, mt=mt: e.tensor_copy(out=memV[:, mt, :, 0:64],
                                                             in_=psb[b][:, 0:256].rearrange("p (h d) -> p h d", h=4)),
                          [('ps', b), 'memV1'], [('memV', mt)])

                    _kstop(2)
                    DMA(lambda e: e.dma_start(out=wsf[:], in_=gm_ws[l].rearrange("g i j -> i g j")), [], ['wsf'])
                    DMA(lambda e: e.dma_start(out=bsT[:], in_=gm_bsT[l]), [], ['bsT'])
                    DMA(lambda e: e.dma_start(out=glng[:], in_=gm_ln_g[l].partition_broadcast(128)), [], ['glng'])
                    DMA(lambda e: e.dma_start(out=glnb[:], in_=gm_ln_b[l].partition_broadcast(128)), [], ['glnb'])
                    for g in range(4):
                        G(lambda e, g=g: e.affine_select(out=wsf[:, g, :], in_=wsf[:, g, :], pattern=[[-1, 128]],
                                                         compare_op=ALU.is_ge, fill=0.0, base=0, channel_multiplier=1),
                          ['wsf'], ['wsf'])
                    b = nbank((0, 1, 2, 3))
                    for g in range(4):
                        T(lambda e, g=g, b=b: e.transpose(out=psb[b][:, g * 128:(g + 1) * 128], in_=wsf[:, g, :], identity=identF[:]),
                          ['wsf', 'identF'], [('ps', b)])
                    V(lambda e, b=b: e.tensor_copy(out=wcT[:], in_=psb[b][:, :].rearrange("p (g i) -> p g i", g=4)), [('ps', b)], ['wcT'])

                    _kstop(3)
                    col0 = 0
                    for gi_, (gname, kind, cols) in enumerate(COLG):
                        _kstop(4 + gi_)
                        ncols = len(cols)
                        w, wk = load_w(wp_in[l][:, col0:col0 + ncols], ncols)
                        col0 += ncols
                        if gname == "uv":
                            uvb = {}

                            def uv_s0(kt):
                                p3 = kt % 3
                                gu_, gv_ = guL[p3], gvL[p3]
                                b = nbank((0, 1, 2, 3))
                                for c in range(8):
                                    T(lambda e, b=b, c=c, w=w, ncols=ncols: e.matmul(psb[b][:, 0:ncols], lhsT=xT[:, c, kt * 128:(kt + 1) * 128], rhs=w[:, c, 0:ncols],
                                                                   start=(c == 0), stop=(c == 7)), wk + [(('xT', kt), 0), (('xT', kt), 1)], [('ps', b)])
                                P = psb[b]
                                pk = ('ps', b)
                                A(lambda e: e.activation(out=gu_[:], in_=P[:, 0:256], func=AF.Gelu), [pk], [('gu', p3)])
                                A(lambda e: e.activation(out=gv_[:].rearrange("p g c -> p (g c)"), in_=P[:, 256:512], func=AF.Gelu), [pk], [('gv', p3)])

                            def uv_s1(kt):
                                pp = kt % 2
                                p3 = kt % 3
                                gv_, gc_, gsq_, st4_ = gvL[p3], gcL[pp], gsqL[pp], st4L[pp]
                                V(lambda e: e.tensor_reduce(out=st4_[:, 0:4], in_=gv_[:], axis=AX.X, op=ALU.add), [('gv', p3)], [('st_sum', pp)])
                                V(lambda e: e.tensor_scalar(out=st4_[:, 4:8], in0=st4_[:, 0:4], scalar1=-1.0 / 64, scalar2=None, op0=ALU.mult),
                                  [('st_sum', pp)], [('st_nm', pp)])
                                V(lambda e: e.tensor_tensor(out=gc_[:], in0=gv_[:], in1=bc(st4_[:, 4:8].unsqueeze(2), [128, 4, 64]), op=ALU.add),
                                  [('gv', p3), ('st_nm', pp)], [('gc', pp)])
                                G(lambda e: e.tensor_tensor(out=gsq_[:], in0=gc_[:], in1=gc_[:], op=ALU.mult), [('gc', pp)], [('gsq', pp)])
                                V(lambda e: e.tensor_reduce(out=st4_[:, 8:12], in_=gsq_[:], axis=AX.X, op=ALU.add), [('gsq', pp)], [('st_ss', pp)])
                                V(lambda e: e.tensor_scalar(out=st4_[:, 8:12], in0=st4_[:, 8:12], scalar1=1.0 / 64, scalar2=LN_EPS,
                                                            op0=ALU.mult, op1=ALU.add), [('st_ss', pp)], [('st_ss', pp)])
                                G(lambda e: e.tensor_tensor(out=st4_[:, 12:16], in0=st4_[:, 8:12], in1=mhalf[:, 0:4], op=ALU.pow), [('st_ss', pp), 'mhalf'], [('st_rs', pp)])

                            def uv_s2(kt):
                                pp = kt % 2
                                gc_, vln_, st4_ = gcL[pp], vlnL[pp], st4L[pp]
                                V(lambda e: e.tensor_tensor(out=gc_[:], in0=gc_[:], in1=bc(st4_[:, 12:16].unsqueeze(2), [128, 4, 64]), op=ALU.mult),
                                  [('gc', pp), ('st_rs', pp)], [('gc', pp)])
                                G(lambda e: e.tensor_tensor(out=gc_[:].rearrange("p g c -> p (g c)"), in0=gc_[:].rearrange("p g c -> p (g c)"),
                                                            in1=glng[:], op=ALU.mult), [('gc', pp), 'glng'], [('gc', pp)])
                                G(lambda e: e.tensor_tensor(out=vln_[:], in0=gc_[:].rearrange("p g c -> p (g c)"), in1=glnb[:], op=ALU.add),
                                  [('gc', pp), 'glnb'], [('vln', pp)])
                                b2 = nbank((4, 5, 6))
                                uvb[kt] = b2
                                for g in range(4):
                                    T(lambda e, g=g: e.matmul(psb[b2][:, g * 64:(g + 1) * 64], lhsT=wcT[:, g, :], rhs=vln_[:, g * 64:(g + 1) * 64],
                                                              start=True, stop=True), [('vln', pp), 'wcT'], [('ps', b2)])

                            def uv_s3(kt):
                                pp = kt % 2
                                p3 = kt % 3
                                gu_, gsq_ = guL[p3], gsqL[pp]
                                b2 = uvb[kt]
                                V(lambda e: e.tensor_tensor(out=gsq_[:], in0=psb[b2][:, 0:256].rearrange("p (g c) -> p g c", g=4),
                                                            in1=bc(bsT[:, :].unsqueeze(2), [128, 4, 64]), op=ALU.add), [('ps', b2), 'bsT'], [('gsq', pp)])
                                V(lambda e: e.tensor_tensor(out=YGM[:, kt, 0:256], in0=gsq_[:].rearrange("p g c -> p (g c)"), in1=gu_[:], op=ALU.mult),
                                  [('gsq', pp), ('gu', p3)], [('YGMa', kt)])
                            for step in range(NK + 2):
                                if step < NK:
                                    uv_s0(step)
                                if 0 <= step - 2 < NK:
                                    uv_s3(step - 2)
                                if 0 <= step - 1 < NK:
                                    uv_s2(step - 1)
                                if step < NK:
                                    uv_s1(step)
                        elif kind == "T":
                            for kt in range(NK):
                                b = nbank((0, 1, 2, 3))
                                for c in range(8):
                                    T(lambda e, b=b, c=c, kt=kt, w=w, ncols=ncols: e.matmul(
                                        psb[b][:, 0:ncols], lhsT=xT[:, c, kt * 128:(kt + 1) * 128], rhs=w[:, c, 0:ncols],
                                        start=(c == 0), stop=(c == 7)),
                                      wk + [(('xT', kt), 0), (('xT', kt), 1)], [('ps', b)])
                                P = psb[b]
                                pk = ('ps', b)
                                if gname == "uv":
                                    pass
                                elif gname == "zz":
                                    tb = tA[kt % 2]
                                    A(lambda e, P=P, tb=tb: e.activation(out=tb[:, 0:256], in_=P[:, 0:256], func=AF.Silu), [pk], [('tA', kt % 2)])
                                    A(lambda e, P=P, kt=kt: e.activation(out=YGM[:, kt, 256:512], in_=P[:, 256:512], func=AF.Silu), [pk], [('YGMb', kt)])
                                    G(lambda e, kt=kt, tb=tb: e.tensor_tensor(out=YGM[:, kt, 0:256], in0=YGM[:, kt, 0:256], in1=tb[:, 0:256], op=ALU.mult),
                                      [('tA', kt % 2), ('YGMa', kt)], [('YGMa', kt)])
                                elif gname == "nz":
                                    A(lambda e, P=P, kt=kt: e.activation(out=SZ[:, kt, :], in_=P[:, 0:512], func=AF.Silu), [pk], [('SZ', kt)])
                                elif gname == "vg":
                                    for i in range(2):
                                        V(lambda e, P=P, kt=kt, i=i: e.tensor_copy(out=vst[i][:, kt, :, 0:64],
                                                                                 in_=P[:, i * 128:(i + 1) * 128].rearrange("p (g d) -> p g d", g=2)),
                                          [pk, ('vst1', i)], [('vst', i, kt)])
                                    A(lambda e, P=P, kt=kt: e.activation(out=GATES[:, kt, :], in_=P[:, 256:280], func=AF.Sigmoid),
                                      [pk, ('vst', 0, kt), ('vst', 1, kt)], [('GATES', kt)])
                        else:
                            nch = ncols // 128
                            roped = gname[0] in ("q", "k")
                            for tg in range(4):
                                xk = [(('xT', kt), h) for kt in range(tg * 4, tg * 4 + 4) for h in range(2)]
                                bl = []
                                for ch in range(nch):
                                    b = nbank((0, 1, 2, 3))
                                    bl.append(b)
                                    for c in range(8):
                                        T(lambda e, b=b, c=c, ch=ch, tg=tg, w=w: e.matmul(
                                            psb[b][:, :], lhsT=w[:, c, ch * 128:(ch + 1) * 128], rhs=xT[:, c, tg * 512:(tg + 1) * 512],
                                            start=(c == 0), stop=(c == 7)), wk + xk, [('ps', b)])
                                tsl = slice(tg * 512, (tg + 1) * 512)
                                if roped:
                                    b1 = bl[0]
                                    b2 = nbank((0, 1, 2, 3))
                                    ta, tb = tA[tg % 2], tB[tg % 2]
                                    pb_ = pbf[tg % 2]
                                    A(lambda e, b1=b1, pb_=pb_: e.copy(out=pb_[:], in_=psb[b1][:, :]), [('ps', b1)], [('pbf', tg % 2)])
                                    T(lambda e, b2=b2, pb_=pb_: e.matmul(psb[b2][:, :], lhsT=PMB[:], rhs=pb_[:], start=True, stop=True),
                                      [('pbf', tg % 2), 'PMB'], [('ps', b2)])
                                    V(lambda e, b1=b1, ta=ta, tsl=tsl: e.tensor_tensor(out=ta[:], in0=psb[b1][:, :], in1=ropeC[:, tsl], op=ALU.mult),
                                      [('ps', b1), ('pbf', tg % 2), 'ropeC'], [('tA', tg % 2)])
                                    V(lambda e, b2=b2, tb=tb, tsl=tsl: e.tensor_tensor(out=tb[:], in0=psb[b2][:, :], in1=ropeS[:, tsl], op=ALU.mult),
                                      [('ps', b2), 'ropeS'], [('tB', tg % 2)])
                                    if gname[0] == "q":
                                        hg = int(gname[1])
                                        dst = QT[:, tg * 4:(tg + 1) * 4, hg, :]
                                        dkey = ('QT', tg, hg)
                                        G(lambda e, ta=ta, tb=tb, dst=dst: e.tensor_tensor(out=dst, in0=ta[:].rearrange("p (a b) -> p a b", a=4),
                                                                                          in1=tb[:].rearrange("p (a b) -> p a b", a=4), op=ALU.add),
                                          [('tA', tg % 2), ('tB', tg % 2)], [dkey])
                                    else:
                                        kb_ = {"kc": 0, "ks": 1, "kw": 0}[gname]
                                        G(lambda e, ta=ta, tb=tb, kb_=kb_, tsl=tsl: e.tensor_tensor(out=kst[kb_][:, tsl], in0=ta[:], in1=tb[:], op=ALU.add),
                                          [('tA', tg % 2), ('tB', tg % 2)], [('kst', kb_, tg)])
                                else:
                                    A(lambda e, b=bl[0], tsl=tsl: e.copy(out=kst[1][:, tsl], in_=psb[b][:, :]), [('ps', bl[0])], [('kst', 1, tg)])
                                    V(lambda e, b=bl[1], tsl=tsl: e.tensor_copy(out=mqT[:, 0, tsl], in_=psb[b][:, :]), [('ps', bl[1])], [('mqT', 0, tg)])
                                    V(lambda e, b=bl[2], tsl=tsl: e.tensor_copy(out=mqT[:, 1, tsl], in_=psb[b][:, :]), [('ps', bl[2])], [('mqT', 1, tg)])
                        if gname in ("kc", "ks", "kw", "vcmq"):
                            si, kb_, bi, o = {"kc": (0, 0, 0, 0), "vcmq": (1, 1, 0, 2048), "ks": (2, 1, 1, 0), "kw": (3, 0, 1, 2048)}[gname]
                            DMA(lambda e, kb_=kb_, bi=bi, o=o: e.dma_start(out=cc_src[bi].ap()[:, o:o + NT], in_=kst[kb_][:]),
                                [('kst', kb_, tg) for tg in range(4)], [('cc_src', si)])
                        if gname == "vcmq":
                            for i in range(2):
                                DMA(lambda e, i=i: e.dma_start(out=cc_src[2 + i].ap()[:, :], in_=vst[i][:].rearrange("p k g e -> p (k g e)")),
                                    [('vst', i, kt) for kt in range(NK)], [('cc_src', 4 + i)])
                            if not os.environ.get("KNOCC"):
                                ccr = [[('cc_src', 0), ('cc_src', 1)], [('cc_src', 2), ('cc_src', 3)], [('cc_src', 4)], [('cc_src', 5)]]
                                for bi in range(4):
                                    S.op('gpsimd', lambda e, bi=bi: e.collective_compute("AllGather", ALU.bypass, replica_groups=groups,
                                                                                       ins=[cc_src[bi].ap().opt()], outs=[cc_dst[bi].ap().opt()]),
                                         reads=ccr[bi], writes=[('cc_dst', bi)], dma='cc')
                        if gname == "zz":
                            def mem_qk(kt):
                                tg = kt // 4
                                pm_ = pmL[kt % 2]
                                for half, b in ((0, 4), (1, 5)):
                                    rs = slice(half * 64, half * 64 + 64)
                                    for mt in range(2):
                                        for hh in range(2):
                                            T(lambda e, b=b, mt=mt, hh=hh, rs=rs: e.matmul(
                                                psb[b][:, (mt * 2 + hh) * 128:(mt * 2 + hh + 1) * 128],
                                                lhsT=memK[rs, hh, mt * 128:(mt + 1) * 128],
                                                rhs=mqT[rs, hh, kt * 128:(kt + 1) * 128], start=True, stop=True),
                                              [('memK', 0), ('memK', 1), ('mqT', 0, tg), ('mqT', 1, tg)], [('ps', b)])
                                    A(lambda e, b=b, half=half: e.activation(out=pm_[:, half, :, :, :].rearrange("p m h q -> p (m h q)"),
                                                                            in_=psb[b][:, :], func=AF.Exp, scale=SCALE),
                                      [('ps', b)], [('pm', kt % 2, half)])

                            def mem_pv(kt):
                                pm_ = pmL[kt % 2]
                                b = 6
                                for h in range(4):
                                    for mt in range(2):
                                        T(lambda e, h=h, mt=mt: e.matmul(psb[b][:, h * 65:(h + 1) * 65], lhsT=pm_[:, h % 2, mt, h // 2, :],
                                                                        rhs=memV[:, mt, h, :], start=(mt == 0), stop=(mt == 1)),
                                          [('pm', kt % 2, 0), ('pm', kt % 2, 1), ('memV', 0), ('memV', 1)], [('ps', b)])
                                ov = psb[b][:, 0:260].rearrange("p (h e) -> p h e", h=4)
                                V(lambda e: e.reciprocal(out=st4L[0][:, 0:4], in_=ov[:, :, 64]), [('ps', b)], [('st_sum', 0)])
                                V(lambda e: e.tensor_tensor(out=om[:], in0=ov[:, :, 0:64], in1=bc(st4L[0][:, 0:4].unsqueeze(2), [128, 4, 64]), op=ALU.mult),
                                  [('ps', b), ('st_sum', 0)], ['om'])
                                G(lambda e: e.tensor_tensor(out=YGM[:, kt, 256:512], in0=om[:].rearrange("p h d -> p (h d)"),
                                                            in1=YGM[:, kt, 256:512], op=ALU.mult), ['om', ('YGMb', kt)], [('YGMb', kt)])
                            for kt in range(NK + 1):
                                if kt < NK:
                                    mem_qk(kt)
                                if kt >= 1:
                                    mem_pv(kt - 1)
                    if debug and l == 0:
                        DMA(lambda e: e.dma_start(out=dbg["qt"][:, :], in_=QT[:].rearrange("p k h q -> p (k h q)")),
                            [('QT', tg, hg) for tg in range(4) for hg in range(4)], ['dbg_qt'])
                        DMA(lambda e: e.dma_start(out=dbg["ygm"][:, :], in_=YGM[:].rearrange("p k c -> p (k c)")),
                            [('YGMa', kt) for kt in range(NK)] + [('YGMb', kt) for kt in range(NK)], ['dbg_ygm'])
                        DMA(lambda e: e.dma_start(out=dbg["sz"][:, :], in_=SZ[:].rearrange("p k c -> p (k c)")),
                            [('SZ', kt) for kt in range(NK)], ['dbg_sz'])
                        DMA(lambda e: e.dma_start(out=dbg["gates"][:, :], in_=GATES[:].rearrange("p k c -> p (k c)")),
                            [('GATES', kt) for kt in range(NK)], ['dbg_gates'])
                        for bi in range(4):
                            DMA(lambda e, bi=bi: e.dma_start(out=dbg[f"cc{bi}"][:, :], in_=cc_dst[bi].ap()[:, :]), [('cc_dst', bi)], [f'dbg_cc{bi}'])
                except _Stop:
                    pass
                S.barrier()
                S.emit()
            if stop_after == "A":
                break
            with ExitStack() as lb:
                sbL = lambda n, s, d: lb.enter_context(nc.sbuf_tensor(f"{n}_L{l}", list(s), d))
                KcmpT = sbL("KcmpT", [128, 512], BF16)
                RHSc = sbL("RHSc", [128, 4, 2, 193], BF16)
                Wout = sbL("Wout", [128, 8, D], BF16)
                wos = sbL("wos", [128, 8, 128], F32)
                lngt = sbL("lngt", [128, D], F32)
                lnbt = sbL("lnbt", [128, D], F32)
                with ExitStack() as p0:
                    sb = lambda n, s, d: p0.enter_context(nc.sbuf_tensor(f"{n}_B0{l}", list(s), d))
                    tT = [sb("kcT", [128, 4, NT], BF16), sb("vcT", [128, 4, NT], BF16)]
                    w1s = sb("w1s", [128, 32, 128], F32)
                    w1b = [sb("w1k", [128, 32, 128], BF16), sb("w1v", [128, 32, 128], BF16)]
                    posf = sb("posf", [128, 2, 32], F32)
                    posb = sb("posb", [128, 2, 32], BF16)
                    w2f = sb("w2f", [128, 2, 64], F32)
                    w2kp = sb("w2kp", [128, 2, 128], BF16)
                    w2vb = sb("w2vb", [128, 64], BF16)
                    hid = sb("hid", [128, 4, 512], BF16)
                    pos_ins = [posk_in, posv_in]
                    for t in range(2):
                        for hf in range(2):
                            DMA(lambda e, t=t, hf=hf: e.dma_start(out=posf[hf * 64:(hf + 1) * 64, t, :], in_=pos_ins[t][l]), [], [('posf', t, hf)])
                    V(lambda e: e.tensor_copy(out=posb[:], in_=posf[:]), [('posf', t, hf) for t in range(2) for hf in range(2)], ['posb'])
                    DMA(lambda e: e.dma_start(out=w2f[:, 0, :], in_=w2k_in[l]), [], [('w2f', 0)])
                    DMA(lambda e: e.dma_start(out=w2f[:, 1, :], in_=w2v_in[l]), [], [('w2f', 1)])
                    G(lambda e: e.memset(w2kp[:], 0.0), [], ['w2kp'])
                    G(lambda e: e.tensor_copy(out=w2kp[:, 0, 0:64], in_=w2f[:, 0, :]), ['w2kp', ('w2f', 0)], ['w2kp'])
                    G(lambda e: e.tensor_copy(out=w2kp[:, 1, 64:128], in_=w2f[:, 0, :]), ['w2kp', ('w2f', 0)], ['w2kp'])
                    G(lambda e: e.tensor_copy(out=w2vb[:], in_=w2f[:, 1, :]), [('w2f', 1)], ['w2vb'])
                    w1_ins = [w1k_in, w1v_in]
                    for t in range(2):
                        if t == 1:
                            for t2_ in range(2):
                                DMA(lambda e, t2_=t2_: e.dma_start(out=tT[t2_][:], in_=cc_dst[0].ap()[:, t2_ * 2048:(t2_ + 1) * 2048].rearrange("(r p) c -> p r c", r=4)),
                                    [], [('tT', t2_)])
                        for hf in range(2):
                            DMA(lambda e, t=t, hf=hf: e.dma_start(out=w1s[hf * 64:(hf + 1) * 64, :, :],
                                                                 in_=w1_ins[t][l].rearrange("(l d) m -> d l m", d=64)), [], [('w1s', hf)])
                        V(lambda e, t=t: e.tensor_copy(out=w1b[t][:, 0:16, :], in_=w1s[:, 0:16, :]), [('w1s', 0), ('w1s', 1)], [('w1b', t, 0)])
                        A(lambda e, t=t: e.copy(out=w1b[t][:, 16:32, :], in_=w1s[:, 16:32, :]), [('w1s', 0), ('w1s', 1)], [('w1b', t, 1)])
                    DMA(lambda e: e.dma_start(out=lngt[:], in_=lng_in[l].partition_broadcast(128)), [], ['lngt'])
                    DMA(lambda e: e.dma_start(out=lnbt[:], in_=lnb_in[l].partition_broadcast(128)), [], ['lnbt'])
                    for cs in range(8):
                        DMA(lambda e, cs=cs: e.dma_start(out=wos[:], in_=wout_in[l][:, cs * 128:(cs + 1) * 128].rearrange("(c p) n -> p c n", p=128)),
                            [], ['wos'])
                        if cs % 2 == 0:
                            V(lambda e, cs=cs: e.tensor_copy(out=Wout[:, :, cs * 128:(cs + 1) * 128], in_=wos[:]), ['wos'], [('Wout', cs)])
                        else:
                            A(lambda e, cs=cs: e.copy(out=Wout[:, :, cs * 128:(cs + 1) * 128], in_=wos[:]), ['wos'], [('Wout', cs)])
                    biasS = sb("biasS", [128, 4], F32)
                    for t in range(2):
                        for g in range(2):
                            gr = slice(g * 64, g * 64 + 64)
                            bb = 4 + g
                            for li in range(32):
                                T(lambda e, bb=bb, t=t, gr=gr, li=li: e.matmul(psb[bb][:, t * 8:(t + 1) * 8], lhsT=w1b[t][gr, li, :],
                                                                              rhs=bc(posb[gr, t, li:li + 1], [64, 8]), start=(li == 0), stop=(li == 31)),
                                  [('w1b', t, 0), ('w1b', t, 1), 'posb'], [('ps', bb)])
                            V(lambda e, bb=bb, t=t, g=g: e.tensor_copy(out=biasS[:, t * 2 + g:t * 2 + g + 1], in_=psb[bb][:, t * 8:t * 8 + 1]),
                              [('ps', bb)], [('biasS', t * 2 + g)])
                    for t in range(2):
                        for g in range(2):
                            b = t * 2 + g
                            gr = slice(g * 64, g * 64 + 64)
                            rk = [('w1b', t, 0), ('w1b', t, 1), ('tT', t)]
                            pk = [('ps', b)]
                            pf = psb[b]
                            svm = tT[t][gr, :, :].rearrange("p r (k m l) -> p m r k l", k=16, m=8)
                            for li in range(32):
                                if li < 16:
                                    T(lambda e, pf=pf, svm=svm, li=li, t=t, gr=gr: e.matmul(pf[:, :], lhsT=w1b[t][gr, li, :], rhs=svm[:, :, :, :, li],
                                                                                          start=(li == 0), stop=False), rk, pk)
                                else:
                                    T(lambda e, pf=pf, svm=svm, li=li, t=t, gr=gr: e.matmul(pf[:, 0:448], lhsT=w1b[t][gr, li, :],
                                                                                          rhs=svm[:, 1:8, :, :, li - 16], start=False, stop=False), rk, pk)
                                    T(lambda e, pf=pf, svm=svm, li=li, t=t, gr=gr: e.matmul(pf[:, 448:496], lhsT=w1b[t][gr, li, :],
                                                                                          rhs=svm[:, 0, 1:4, :, li - 16], start=False, stop=False), rk, pk)
                                    T(lambda e, pf=pf, svm=svm, li=li, t=t, gr=gr: e.matmul(pf[:, 496:511], lhsT=w1b[t][gr, li, :],
                                                                                          rhs=svm[:, 0, 0, 1:16, li - 16], start=False, stop=(li == 31)), rk, pk)
                            A(lambda e, b=b: e.activation(out=hid[:, b, :].rearrange("p (rk m) -> p m rk", m=8),
                                                          in_=psb[b][:, :].rearrange("p (m rk) -> p m rk", m=8), func=AF.Gelu, bias=biasS[:, b:b + 1]),
                              pk + [('biasS', b)], [('hid', b)])
                    T(lambda e: e.matmul(psb[4][:, :], lhsT=w2kp[:, 0, :], rhs=hid[:, 0, :], start=True, stop=False), ['w2kp', ('hid', 0)], [('ps', 4)])
                    T(lambda e: e.matmul(psb[4][:, :], lhsT=w2kp[:, 1, :], rhs=hid[:, 1, :], start=False, stop=True), ['w2kp', ('hid', 1)], [('ps', 4)])
                    V(lambda e: e.tensor_copy(out=KcmpT[:], in_=psb[4][:, :]), [('ps', 4)], ['KcmpT'])
                    for g in range(2):
                        for rp in range(4):
                            T(lambda e, g=g, rp=rp: e.matmul(psb[5][:, (rp * 2 + g) * 64:(rp * 2 + g + 1) * 64], lhsT=hid[:, 2 + g, rp * 128:(rp + 1) * 128],
                                                             rhs=w2vb[:], start=True, stop=True), ['w2vb', ('hid', 2 + g)], [('ps', 5)])
                    G(lambda e: e.memset(RHSc[:, :, :, 64:65], 1.0), [], ['RHSc1'])
                    V(lambda e: e.tensor_copy(out=RHSc[:, :, :, 0:64], in_=psb[5][:, :].rearrange("p (r g d) -> p r g d", r=4, g=2)),
                      [('ps', 5), 'RHSc1'], ['RHScV'])
                    for g in range(2):
                        DMA(lambda e, g=g: e.dma_start(out=RHSc[:, :, g, 65:193], in_=ov_in[:, :, :]), [], [('RHScO', g)])
                    if debug and l == 0:
                        DMA(lambda e: e.dma_start(out=dbg["kcmp"][:, :], in_=KcmpT[:]), ['KcmpT'], ['dbg_kcmp'])
                        DMA(lambda e: e.dma_start(out=dbg["rhsc"][:, :], in_=RHSc[:].rearrange("p r g e -> p (r g e)")),
                            ['RHScV', 'RHSc1', ('RHScO', 0), ('RHScO', 1)], ['dbg_rhsc'])
                    S.barrier()
                    S.emit()
                if stop_after == "B0":
                    break
                with ExitStack() as p1:
                    sb = lambda n, s, d: p1.enter_context(nc.sbuf_tensor(f"{n}_B1{l}", list(s), d))
                    Kaug = [sb(f"Kaug{g}", [128, 64, 128], BF16) for g in range(2)]
                    Kwin = sb("Kwin", [128, 64, 128], BF16)
                    Vs = sb("Vs", [128, 64, 2, 65], BF16)
                    Vw = sb("Vw", [128, 64, 2, 65], BF16)
                    Qaug = [sb(f"Qaug{g}", [128, 3, 512], BF16) for g in range(2)]
                    Pt = [sb(f"Pt{i}", [128, 512], BF16) for i in range(4)]
                    vmk = [sb(f"vmk{i}", [128, 1, 128], BF16) for i in range(2)]
                    imp = sb("imp", [128, 128], F32)
                    impm = sb("impm", [128, 128], F32)
                    wkt = sb("wkt", [128, 128], F32)
                    selm = sb("selm", [128, 128], F32)
                    m8 = sb("m8", [128, 16], F32)
                    thr = sb("thr", [128, 1], F32)
                    NBt = sb("NBt", [128, 192], BF16)
                    rsA = sb("rsA", [128, 12], F32)
                    riA = sb("riA", [128, 12], F32)
                    fA = sb("fA", [128, 12], F32)
                    oacc = sb("oacc", [128, 4, 64], F32)
                    t2 = sb("t2", [128, 4, 64], F32)
                    t3 = sb("t3", [128, 4, 64], F32)
                    mixn = sb("mixn", [128, 512], BF16)
                    mixT = sb("mixT", [128, 8, 128], BF16)
                    xblk = sb("xblk", [128, D], F32)
                    zb = sb("zb", [128, D], F32)
                    stt = sb("stt", [128, 12], F32)
                    mv = sb("mv", [128, 4], F32)
                    zeroB = sb("zeroB", [128, 386], BF16)
                    G(lambda e: e.memset(zeroB[:], 0.0), [], ['zeroB'])
                    mhalfB = sb("mhalfB", [128, 1], F32)
                    G(lambda e: e.memset(mhalfB[:], -0.5), [], ['mhalfB'])
                    G(lambda e: e.memset(NBt[:], 0.0), [], ['NBa'])
                    accS = sb("accS", [128, 1292], F32)

                    try:
                        ksrc = cc_dst[1].ap()[:, 0:2048].rearrange("(r p) c -> p r c", r=4)
                        DMA(lambda e: e.dma_start(out=Kaug[0][0:64, :, :].rearrange("p (r k) c -> p r (k c)", r=4), in_=ksrc[0:64]), [], [('Kaug', 0, 'k')])
                        DMA(lambda e: e.dma_start(out=Kaug[1][64:128, :, :].rearrange("p (r k) c -> p r (k c)", r=4), in_=ksrc[64:128]), [], [('Kaug', 1, 'k')])
                        DMA(lambda e: e.dma_start(out=Kaug[0][64:128, :, :].rearrange("p s c -> p (s c)"), in_=eind_in[:, :]), [], [('Kaug', 0, 'e')])
                        DMA(lambda e: e.dma_start(out=Kaug[1][0:64, :, :].rearrange("p s c -> p (s c)"), in_=eind_in[:, :]), [], [('Kaug', 1, 'e')], q='gpsimd')
                        DMA(lambda e: e.dma_start(out=Kwin[:].rearrange("p (r k) c -> p r (k c)", r=4),
                                                  in_=cc_dst[1].ap()[:, 2048:4096].rearrange("(r p) c -> p r c", r=4)), [], ['Kwin'], q='gpsimd')
                        DMA(lambda e: e.dma_start(out=Vs[:].rearrange("p (r k) g e -> p r (k g e)", r=4),
                                                  in_=cc_dst[2].ap()[:, :].rearrange("(r p) c -> p r c", r=4)), [], ['Vs'])
                        DMA(lambda e: e.dma_start(out=Vw[:].rearrange("p (r k) g e -> p r (k g e)", r=4),
                                                  in_=cc_dst[3].ap()[:, :].rearrange("(r p) c -> p r c", r=4)), [], ['Vw'], q='gpsimd')
                        Wk = [('Wout', cs) for cs in range(8)]
                        G(lambda e: e.memset(Qaug[0][64:128, 2, :], 0.0), [], [('Qz', 0)])
                        G(lambda e: e.memset(Qaug[1][0:64, 2, :], 0.0), [], [('Qz', 1)])
                        KaugK = [[('Kaug', g, 'k'), ('Kaug', g, 'e')] for g in range(2)]

                        tiles = []

                        def mk_group(k, g):
                            M = 8 * (k + 1)
                            sk = 128 - 8 * k
                            gr = slice(g * 64, g * 64 + 64)
                            mr = slice(64, 128) if g == 0 else slice(0, 64)
                            qk = ('Qq', g)
                            vk = ('vmk', k % 2)
                            vm_ = vmk[k % 2]
                            ob = [psb[i][:, 0:386].rearrange("p (h e) -> p h e", h=2) for i in range(2)]

                            def group_begin():
                                if g == 0:
                                    DMA(lambda e: e.dma_start(out=vmk[k % 2][:], in_=vm_in[:, k, :, :]), [], [('vmk', k % 2)])

                            def q_copy():
                                V(lambda e: e.tensor_copy(out=Qaug[g][gr, :, :],
                                                          in_=bc(QT[gr, k, :, :].rearrange("p h q -> p (h q)").unsqueeze(1), [64, 3, 512])),
                                  [], [qk])
                            qcopies.append(q_copy)

                            def zero_acc(banks):
                                def f():
                                    for rep_ in range(2):
                                        for b_, n_ in banks:
                                            T(lambda e, b_=b_, n_=n_: e.matmul(psb[b_][:, 0:n_], lhsT=zeroB[:, 0:128], rhs=zeroB[:, 0:n_], start=True, stop=False),
                                              ['zeroB'], [('ps', b_)])
                                return f

                            for rp in range(4):
                                def qk_c(b, rp=rp):
                                    T(lambda e: e.matmul(psb[b][0:M, :], lhsT=KcmpT[:, rp * 128:rp * 128 + M], rhs=Qaug[g][:, 2, :],
                                                         start=True, stop=False), [qk, ('Qz', g)], [('ps', b)])
                                    T(lambda e: e.matmul(psb[b][0:M, :].rearrange("p (h q) -> p h q", h=4), lhsT=ZC[:, sk:sk + M],
                                                         rhs=bc(BCMP[:, rp, :].unsqueeze(1), [128, 4, 128]), start=False, stop=True), [], [('ps', b)])

                                def exp_c(b, pi):
                                    A(lambda e: e.activation(out=Pt[pi][0:M, :], in_=psb[b][0:M, :], func=AF.Exp, scale=SCALE), [('ps', b)], [('Pt', pi)])

                                def pv_c(pi, rp=rp):
                                    for h in range(4):
                                        bo, co = h // 2, (h % 2) * 193
                                        T(lambda e, h=h, bo=bo, co=co: e.matmul(psb[bo][:, co:co + 193], lhsT=Pt[pi][0:M, h * 128:(h + 1) * 128],
                                                                               rhs=RHSc[0:M, rp, g, :], start=False, stop=(rp == 3)),
                                          [('Pt', pi)], [('ps', bo)])
                                t = dict(qk=qk_c, exp=exp_c, pv=pv_c, pre_qk=[], pre_pv=[], post_pv=[])
                                if rp == 0:
                                    t['pre_qk'].append(group_begin)
                                    t['pre_pv'].append(zero_acc([(0, 386), (1, 386)]))
                                tiles.append(t)

                            def imp_chain():
                                for i in range(2):
                                    V(lambda e, i=i: e.tensor_scalar(out=rsA[:, 2 * i:2 * i + 2], in0=ob[i][:, :, 64], scalar1=1e-30, scalar2=None, op0=ALU.max),
                                      [('ps', i)], [('rsA', i)])
                                V(lambda e: e.reciprocal(out=riA[:, 0:4], in_=rsA[:, 0:4]), [('rsA', 0), ('rsA', 1)], ['riAc'])
                                V(lambda e: e.scalar_tensor_tensor(out=imp[:], in0=ob[0][:, 0, 65:193], scalar=riA[:, 0:1], in1=vm_[:, 0, :],
                                                                   op0=ALU.mult, op1=ALU.add), [('ps', 0), 'riAc', vk], ['imp'])
                                for h in range(1, 4):
                                    V(lambda e, h=h: e.scalar_tensor_tensor(out=imp[:], in0=ob[h // 2][:, h % 2, 65:193], scalar=riA[:, h:h + 1], in1=imp[:],
                                                                            op0=ALU.mult, op1=ALU.add), [('ps', h // 2), 'riAc', 'imp'], ['imp'])
                                V(lambda e: e.max(out=m8[:, 0:8], in_=imp[:]), ['imp'], ['m8a'])
                                V(lambda e: e.match_replace(out=wkt[:], in_to_replace=m8[:, 0:8], in_values=imp[:], imm_value=-1e30), ['imp', 'm8a'], ['wkt'])
                                V(lambda e: e.max(out=m8[:, 8:16], in_=wkt[:]), ['wkt'], ['m8b'])
                                V(lambda e: e.tensor_scalar(out=selm[:], in0=imp[:], scalar1=m8[:, 15:16], scalar2=None, op0=ALU.is_ge), ['imp', 'm8b'], ['selm'])
                                V(lambda e: e.tensor_scalar(out=NBt[:, 64:192], in0=selm[:], scalar1=-1.0, scalar2=BIG, op0=ALU.add, op1=ALU.mult), ['selm'], ['NBa'])
                                V(lambda e: e.tensor_copy(out=accS[:, 0:386], in_=psb[0][:, 0:386]), [('ps', 0)], [('accS', 0)])
                                V(lambda e: e.tensor_copy(out=accS[:, 386:772], in_=psb[1][:, 0:386]), [('ps', 1)], [('accS', 1)])
                            tiles[-1]['post_pv'].append(imp_chain)

                            wt = [(rp, dk) for dk in range(2) for rp in range(4) if k - 1 + dk >= 0]
                            for idx, (rp, dk) in enumerate(wt):
                                slot = rp * 16 + k - 1 + dk

                                def qk_w(b, slot=slot, rp=rp, dk=dk):
                                    T(lambda e: e.matmul(psb[b][:, :], lhsT=Kwin[:, slot, :], rhs=Qaug[g][:, 2, :], start=True, stop=False),
                                      [qk, ('Qz', g), 'Kwin'], [('ps', b)])
                                    T(lambda e: e.matmul(psb[b][:, :].rearrange("p (h q) -> p h q", h=4), lhsT=identB[:],
                                                         rhs=bc(WMB[:, rp * 2 + dk, :].unsqueeze(1), [128, 4, 128]), start=False, stop=True), [], [('ps', b)])

                                def exp_f(b, pi):
                                    A(lambda e: e.activation(out=Pt[pi][:], in_=psb[b][:, :], func=AF.Exp, scale=SCALE), [('ps', b)], [('Pt', pi)])

                                def pv_w(pi, slot=slot, last=(idx == len(wt) - 1)):
                                    for h in range(4):
                                        T(lambda e, h=h: e.matmul(psb[3][:, h * 65:(h + 1) * 65], lhsT=Pt[pi][:, h * 128:(h + 1) * 128], rhs=Vw[:, slot, g, :],
                                                                  start=False, stop=last), [('Pt', pi), 'Vw'], [('ps', 3)])
                                t = dict(qk=qk_w, exp=exp_f, pv=pv_w, pre_qk=[], pre_pv=[], post_pv=[])
                                if idx == 0:
                                    t['pre_pv'].append(zero_acc([(3, 260)]))
                                    firstwin.append(t)
                                tiles.append(t)

                            def mask_to_q():
                                if g == 0:
                                    for ver, (w0, w1) in enumerate([(0, 128), (64, 192)]):
                                        T(lambda e, ver=ver, w0=w0, w1=w1: e.transpose(out=psT[:, ver * 128:(ver + 1) * 128], in_=NBt[:, w0:w1], identity=identB[:]),
                                          ['NBa'], ['psT'])
                                else:
                                    for ver, (w0, w1) in enumerate([(64, 128), (128, 192)]):
                                        T(lambda e, ver=ver, w0=w0, w1=w1: e.transpose(out=psT[0:64, ver * 128:(ver + 1) * 128], in_=NBt[:, w0:w1], identity=identB[:]),
                                          ['NBa'], ['psT'])
                                for ver in range(2):
                                    V(lambda e, ver=ver: e.tensor_copy(out=Qaug[g][mr, ver, :].rearrange("p (h q) -> p h q", h=4),
                                                                       in_=bc(psT[mr, ver * 128:(ver + 1) * 128].unsqueeze(1), [64, 4, 128])),
                                      ['psT'], [('Qm', g, ver)])
                            st_ = [(rp, kp) for kp in range(k + 1) for rp in range(4)]
                            for idx, (rp, kp) in enumerate(st_):
                                slot = rp * 16 + kp
                                ver = 0 if rp < 2 else 1
                                diag = (kp == k)

                                def qk_s(b, slot=slot, ver=ver, diag=diag, rp=rp):
                                    T(lambda e: e.matmul(psb[b][:, :], lhsT=Kaug[g][:, slot, :], rhs=Qaug[g][:, ver, :], start=True, stop=(not diag)),
                                      [qk, ('Qm', g, ver)] + KaugK[g], [('ps', b)])
                                    if diag:
                                        T(lambda e: e.matmul(psb[b][:, :].rearrange("p (h q) -> p h q", h=4), lhsT=identB[:],
                                                             rhs=bc(DMB[:, rp, :].unsqueeze(1), [128, 4, 128]), start=False, stop=True), [], [('ps', b)])

                                def pv_s(pi, slot=slot, last=(idx == len(st_) - 1)):
                                    for h in range(4):
                                        T(lambda e, h=h: e.matmul(psb[2][:, h * 65:(h + 1) * 65], lhsT=Pt[pi][:, h * 128:(h + 1) * 128], rhs=Vs[:, slot, g, :],
                                                                  start=False, stop=last), [('Pt', pi), 'Vs'], [('ps', 2)])
                                t = dict(qk=qk_s, exp=exp_f, pv=pv_s, pre_qk=[], pre_pv=[], post_pv=[])
                                if idx == 0:
                                    t['pre_qk'].append(mask_to_q)
                                    t['pre_pv'].append(zero_acc([(2, 260)]))
                                tiles.append(t)

                            def combine():
                                V(lambda e: e.tensor_copy(out=accS[:, 772:1032], in_=psb[2][:, 0:260]), [('ps', 2)], [('accS', 2)])
                                V(lambda e: e.tensor_copy(out=accS[:, 1032:1292], in_=psb[3][:, 0:260]), [('ps', 3)], [('accS', 3)])
                                ocv = accS[:, 0:772].rearrange("p (h e) -> p h e", h=4)
                                osv = accS[:, 772:1032].rearrange("p (h e) -> p h e", h=4)
                                owv = accS[:, 1032:1292].rearrange("p (h e) -> p h e", h=4)
                                V(lambda e: e.tensor_scalar(out=rsA[:, 4:8], in0=osv[:, :, 64], scalar1=1e-30, scalar2=None, op0=ALU.max), [('accS', 2)], [('rsA', 2)])
                                V(lambda e: e.tensor_scalar(out=rsA[:, 8:12], in0=owv[:, :, 64], scalar1=1e-30, scalar2=None, op0=ALU.max), [('accS', 3)], [('rsA', 3)])
                                V(lambda e: e.reciprocal(out=riA[:, 4:12], in_=rsA[:, 4:12]), [('rsA', 2), ('rsA', 3)], ['riAs'])
                                V(lambda e: e.tensor_tensor(out=fA[:, :].rearrange("p (c h) -> p c h", c=3), in0=riA[:, :].rearrange("p (c h) -> p c h", c=3),
                                                            in1=GATES[:, k, g * 12:(g + 1) * 12].rearrange("p (h c) -> p c h", c=3), op=ALU.mult),
                                  ['riAc', 'riAs'], ['fA'])
                                V(lambda e: e.tensor_tensor(out=oacc[:], in0=ocv[:, :, 0:64], in1=bc(fA[:, 0:4].unsqueeze(2), [128, 4, 64]), op=ALU.mult),
                                  [('accS', 0), ('accS', 1), 'fA'], ['oacc'])
                                G(lambda e: e.tensor_tensor(out=t2[:], in0=osv[:, :, 0:64], in1=bc(fA[:, 4:8].unsqueeze(2), [128, 4, 64]), op=ALU.mult),
                                  [('accS', 2), 'fA'], ['t2'])
                                G(lambda e: e.tensor_tensor(out=t3[:], in0=owv[:, :, 0:64], in1=bc(fA[:, 8:12].unsqueeze(2), [128, 4, 64]), op=ALU.mult),
                                  [('accS', 3), 'fA'], ['t3'])
                                G(lambda e: e.tensor_tensor(out=oacc[:], in0=oacc[:], in1=t2[:], op=ALU.add), ['oacc', 't2'], ['oacc'])
                                G(lambda e: e.tensor_tensor(out=oacc[:], in0=oacc[:], in1=t3[:], op=ALU.add), ['oacc', 't3'], ['oacc'])
                                G(lambda e: e.tensor_tensor(out=mixn[:, g * 256:(g + 1) * 256], in0=oacc[:].rearrange("p h d -> p (h d)"),
                                                            in1=SZ[:, k, g * 256:(g + 1) * 256], op=ALU.mult), ['oacc'], [('mixn', g)])

                            def bo_a():
                                for c in range(8):
                                    if c < 2:
                                        src, rk_ = YGM[:, k, c * 128:(c + 1) * 128], []
                                    elif c < 6:
                                        src, rk_ = mixn[:, (c - 2) * 128:(c - 1) * 128], [('mixn', (c - 2) // 2)]
                                    else:
                                        src, rk_ = YGM[:, k, 256 + (c - 6) * 128:256 + (c - 5) * 128], []
                                    T(lambda e, c=c, src=src: e.transpose(out=psT[:, c * 128:(c + 1) * 128], in_=src, identity=identB[:]), rk_, ['psT'])
                                V(lambda e: e.tensor_copy(out=mixT[:].rearrange("p c t -> p (c t)"), in_=psT[:, :]), ['psT'], ['mixT'])

                            def bo_b(j):
                                def f():
                                    half = j // 4
                                    for c in (2 * (j % 4), 2 * (j % 4) + 1):
                                        T(lambda e, half=half, c=c: e.matmul(psb[half][:, :], lhsT=mixT[:, c, :], rhs=Wout[:, c, half * 512:(half + 1) * 512],
                                                                             start=(c == 0), stop=(c == 7)), ['mixT'], [('ps', half)])
                                return f

                            def bo_c():
                                yb = [0, 1]
                                for half in range(2):
                                    hs = slice(half * 512, (half + 1) * 512)
                                    V(lambda e, half=half, hs=hs: e.scalar_tensor_tensor(out=zb[:, hs], in0=xblk[:, hs], scalar=ALPHA, in1=psb[yb[half]][:, :],
                                                                                         op0=ALU.mult, op1=ALU.add), [('ps', yb[half]), 'xblk'], [('zb', half)])
                                    V(lambda e, half=half, hs=hs: e.bn_stats(out=stt[:, half * 6:(half + 1) * 6], in_=zb[:, hs]), [('zb', half)], [('stt', half)])
                                V(lambda e: e.bn_aggr(out=mv[:, 0:2], in_=stt[:, :]), [('stt', 0), ('stt', 1)], ['mv'])
                                V(lambda e: e.tensor_scalar(out=mv[:, 2:3], in0=mv[:, 1:2], scalar1=LN_EPS, scalar2=None, op0=ALU.add), ['mv'], ['mv2'])
                                G(lambda e: e.tensor_tensor(out=mv[:, 3:4], in0=mv[:, 2:3], in1=mhalfB[:, 0:1], op=ALU.pow), ['mv2', 'mhalfB'], ['mv3'])
                                V(lambda e: e.tensor_scalar(out=zb[:], in0=zb[:], scalar1=mv[:, 0:1], scalar2=mv[:, 3:4], op0=ALU.subtract, op1=ALU.mult),
                                  [('zb', 0), ('zb', 1), 'mv', 'mv3'], [('zb', 0), ('zb', 1)])
                                V(lambda e: e.tensor_tensor(out=zb[:], in0=zb[:], in1=lngt[:], op=ALU.mult), [('zb', 0), ('zb', 1), 'lngt'], [('zb', 0), ('zb', 1)])
                                V(lambda e: e.tensor_tensor(out=zb[:], in0=zb[:], in1=lnbt[:], op=ALU.add), [('zb', 0), ('zb', 1), 'lnbt'], [('zb', 0), ('zb', 1)])
                                DMA(lambda e: e.dma_start(out=xdst[k * 128:(k + 1) * 128, :], in_=zb[:]), [('zb', 0), ('zb', 1)], [('xdst', k)])
                                if k + 1 < NK:
                                    DMA(lambda e: e.dma_start(out=xblk[:], in_=xsrc[(k + 1) * 128:(k + 2) * 128, :]), [], ['xblk'])
                            tiles[-1]['post_pv'].append(combine)
                            if g == 1:
                                L_ = len(tiles) - 1
                                D_ = min(18, 12 + 4 * (k + 2) - 10)
                                deferred.append((L_ + D_, bo_a))
                                for j in range(8):
                                    deferred.append((L_ + D_ + 1 + j, bo_b(j)))
                                deferred.append((L_ + D_ + 9, bo_c))

                        DEFER = 14
                        deferred = []
                        DMA(lambda e: e.dma_start(out=xblk[:], in_=xsrc[0:128, :]), [], ['xblk'])
                        qcopies, firstwin = [], []
                        for k in range(NK):
                            for g in range(2):
                                mk_group(k, g)
                        tiles[0]['pre_qk'].insert(0, qcopies[0])
                        for n_ in range(1, len(qcopies)):
                            firstwin[n_ - 1]['pre_qk'].insert(0, qcopies[n_])
                        tail_hooks = []
                        for ti, fn in deferred:
                            if ti < len(tiles):
                                tiles[ti]['post_pv'].append(fn)
                            else:
                                tail_hooks.append(fn)
                        LOOK = 2
                        nt_ = len(tiles)
                        binfo = {}
                        for idx in range(nt_ + LOOK):
                            if idx < nt_:
                                t = tiles[idx]
                                for f in t['pre_qk']:
                                    f()
                                b = nbank((4, 5, 6))
                                binfo[idx] = b
                                t['qk'](b)
                            i = idx - LOOK
                            if i >= 0:
                                t = tiles[i]
                                pi = i % 4
                                t['exp'](binfo[i], pi)
                                for f in t['pre_pv']:
                                    f()
                                t['pv'](pi)
                                for f in t['post_pv']:
                                    f()
                        for f in tail_hooks:
                            f()
                    except _Stop:
                        pass
                    S.barrier()
                    S.emit()
        S.barrier()
        S.emit()
    return nc


def _bf(a):
    return np.asarray(a, dtype=np.float32).astype(ml_dtypes.bfloat16)


def _consts_common():
    eind = np.zeros((64, 64, 128), np.float32)
    for s in range(64):
        for half in range(2):
            eind[(2 * s + half) % 64, s, half * 64:(half + 1) * 64] = 1.0
    zc = np.zeros((128, 256), np.float32)
    for e in range(16):
        zc[e, e + 120] = 1.0
    ov = np.zeros((128, 4, 128), np.float32)
    for rp in range(4):
        for kp in range(16):
            for m in range(8):
                n = 8 * (4 * kp + rp) + m
                for jb in ([n // 4] + ([n // 4 + 1] if n % 4 == 3 else [])):
                    if jb > 127:
                        continue
                    j2, half = jb // 2, jb % 2
                    beta = 2 * ((j2 % 4) * 16 + j2 // 4) + half
                    ov[8 * kp + m, rp, beta] = 1.0
    return _bf(eind.reshape(64, SEQ)), _bf(zc), _bf(ov)


def _consts_core(r):
    p = np.arange(128)
    bcmp = np.zeros((128, 4, 128), np.float32)
    for dk in range(2):
        for m in range(8):
            for rp in range(4):
                dj = 4 * (dk - 1) + rp - r
                ok = (128 * dj + 16 * m + 31) <= p
                bcmp[dk * 8 + m, rp, :] = np.where(ok, 0.0, -BIG)
    c = np.arange(128)[:, None]
    q = np.arange(128)[None, :]
    dmb = np.zeros((128, 4, 128), np.float32)
    for rp in range(4):
        if rp < r:
            dmb[:, rp, :] = 0.0
        elif rp > r:
            dmb[:, rp, :] = -BIG
        else:
            dmb[:, rp, :] = np.where(c <= q, 0.0, -BIG)
    wmb = np.zeros((128, 4, 2, 128), np.float32)
    for rp in range(4):
        for dk in range(2):
            dj = 4 * (dk - 1) + rp - r
            if dj == 0:
                wmb[:, rp, dk, :] = np.where(c <= q, 0.0, -BIG)
            elif dj in (-1, -2, -3):
                wmb[:, rp, dk, :] = 0.0
            elif dj == -4:
                wmb[:, rp, dk, :] = np.where(c > q, 0.0, -BIG)
            else:
                wmb[:, rp, dk, :] = -BIG
    beta = np.arange(128)
    s_ = beta // 2
    jb = 2 * (4 * (s_ % 16) + s_ // 16) + beta % 2
    vm = np.zeros((128, NK, 1, 128), np.float32)
    for k in range(NK):
        i = 4 * k + r
        tblk = (2 * i + (p >= 64))[:, None]
        valid = jb[None, :] <= tblk
        forced = (jb[None, :] == 0) | (valid & (jb[None, :] > tblk - 2))
        vm[:, k, 0, :] = np.where(forced, 100.0, np.where(valid, 0.0, -100.0))
    return _bf(bcmp), _bf(dmb), _bf(wmb.reshape(128, 8, 128)), _bf(vm)


def _rope_tables(r):
    blocks = 4 * np.arange(NK) + r
    pos = (blocks[:, None] * 128 + np.arange(128)[None, :]).reshape(-1).astype(np.float32)
    inv = (np.float32(10000.0) ** (-np.arange(32, dtype=np.float32) * np.float32(2.0) / np.float32(64))).astype(np.float32)
    ang = (pos[None, :] * inv[:, None]).astype(np.float32)
    cos = np.cos(ang).astype(np.float32)
    sin = np.sin(ang).astype(np.float32)
    C = np.concatenate([cos, cos, cos, cos], axis=0)
    Sg = np.concatenate([-sin, sin, -sin, sin], axis=0)
    return np.ascontiguousarray(C), np.ascontiguousarray(Sg)


def make_in_maps(inputs):
    f = lambda k: np.asarray(inputs[k], dtype=np.float32)
    x, mem = f("x"), f("mem")
    w_in = f("w_in")
    allcols = np.concatenate([g[2] for g in COLG])
    shared = {
        "w_in_p": np.ascontiguousarray(w_in[:, :, allcols]),
        "gm_ln_g": f("gm_ln_g").reshape(DEPTH, 1, 256),
        "gm_ln_b": f("gm_ln_b").reshape(DEPTH, 1, 256),
        "gm_ws": f("gm_ws"),
        "gm_bsT": np.ascontiguousarray(f("gm_bs").transpose(0, 2, 1)),
        "cmp_pos_kT": np.ascontiguousarray(f("cmp_pos_k").transpose(0, 2, 1)),
        "cmp_pos_vT": np.ascontiguousarray(f("cmp_pos_v").transpose(0, 2, 1)),
        "cmp_k_w1": f("cmp_k_w1"), "cmp_v_w1": f("cmp_v_w1"),
        "cmp_k_w2": f("cmp_k_w2"), "cmp_v_w2": f("cmp_v_w2"),
        "w_mem_kv": f("w_mem_kv"), "w_out": f("w_out"),
        "ln_g": f("ln_g").reshape(DEPTH, 1, D), "ln_b": f("ln_b").reshape(DEPTH, 1, D),
    }
    eind, zc, ov = _consts_common()
    pm = np.zeros((128, 128), np.float32)
    for pq in range(128):
        pm[(pq // 64) * 64 + (pq % 64 + 32) % 64, pq] = 1.0
    shared.update({"eind": eind, "zc": zc, "ovc": ov, "pswap": _bf(pm)})
    maps = []
    for c in range(8):
        b, r = c // 4, c % 4
        blocks = 4 * np.arange(NK) + r
        m = dict(shared)
        m["x_own"] = np.ascontiguousarray(x[b].reshape(64, 128, D)[blocks].reshape(NT, D))
        m["mem_b"] = np.ascontiguousarray(mem[b])
        C, Sg = _rope_tables(r)
        m["ropeC"], m["ropeS"] = C, Sg
        bcmp, dmb, wmb, vm = _consts_core(r)
        m.update({"bcmp": bcmp, "dmb": dmb, "wmb": wmb, "vmc": vm})
        maps.append(m)
    return maps


def assemble(results):
    out = np.zeros((NB, SEQ, D), np.float32)
    for c in range(8):
        b, r = c // 4, c % 4
        blocks = 4 * np.arange(NK) + r
        o = np.asarray(results[c]["out"], dtype=np.float32).reshape(NK, 128, D)
        out[b].reshape(64, 128, D)[blocks] = o
    return out


_NC_CACHE = {}


def kernel(**inputs):
    if "nc" not in _NC_CACHE:
        _NC_CACHE["nc"] = build()
    nc = _NC_CACHE["nc"]
    maps = make_in_maps(inputs)
    res = run_bass_kernel_spmd(nc, maps, core_ids=list(range(8)))
    return assemble(res.results)
```

```python
import os
import numpy as np
import ml_dtypes
from contextlib import ExitStack
import concourse.bass as bass
import concourse.mybir as mybir
from concourse.bass_utils import run_bass_kernel_spmd

F32 = mybir.dt.float32
BF16 = mybir.dt.bfloat16
AF = mybir.ActivationFunctionType
ALU = mybir.AluOpType
AX = mybir.AxisListType

D = 1024
SEQ = 8192
NB = 2
DEPTH = 2
NK = 16
NT = NK * 128
ALPHA = (2.0 * DEPTH) ** 0.25
LN_EPS = 1e-5
SCALE = 0.125
BIG = 30000.0
CCW = 12352
O_KC, O_VC, O_KS, O_KW, O_VS, O_VW = 0, 2048, 4096, 6144, 8192, 10272

C_U, C_V, C_Z, C_Q, C_KC, C_VC, C_KS, C_VS, C_KW, C_VW, C_G, C_NZ, C_MQ, C_MZ = (
    0, 256, 512, 768, 1280, 1408, 1536, 1664, 1792, 1920, 2048, 2072, 2584, 2840)


def _swap64(cols):
    cols = np.asarray(cols).reshape(-1, 64)
    return np.concatenate([cols[:, 32:], cols[:, :32]], axis=1).reshape(-1)


def _col_groups():
    ar = np.arange
    groups = []
    for nm, c0 in (("kc", C_KC), ("ks", C_KS), ("kw", C_KW)):
        p = ar(c0, c0 + 128)
        groups.append((nm, "F", p))
    groups.append(("vg", "T", np.concatenate([ar(C_VS, C_VS + 128), ar(C_VW, C_VW + 128)] + [ar(C_G, C_G + 24)] * 5 + [ar(C_G, C_G + 8)])))
    groups.append(("vcmq", "F", np.concatenate([ar(C_VC, C_VC + 128), ar(C_MQ, C_MQ + 256)])))
    groups.append(("uv", "T", np.concatenate([ar(C_U, C_U + 256), ar(C_V, C_V + 256)])))
    groups.append(("zz", "T", np.concatenate([ar(C_Z, C_Z + 256), ar(C_MZ, C_MZ + 256)])))
    groups.append(("nz", "T", ar(C_NZ, C_NZ + 512)))
    for hg in range(4):
        p = np.concatenate([ar(C_Q + hg * 64, C_Q + hg * 64 + 64), ar(C_Q + (4 + hg) * 64, C_Q + (4 + hg) * 64 + 64)])
        groups.append((f"q{hg}", "F", p))
    return groups


COLG = _col_groups()
WPC = int(sum(len(g[2]) for g in COLG))


class Sched:
    ENG = ['tensor', 'vector', 'scalar', 'gpsimd', 'sync']
    EPOCH = 30000
    NEPOCH = 4
    NDS = 8

    def __init__(self, nc, es):
        self.nc = nc
        self.es = es
        self.ops = {e: [] for e in self.ENG}
        self.cnt = {e: 0 for e in self.ENG}
        self.sems = {}
        for e in self.ENG:
            for ep in range(self.NEPOCH):
                self.sems[(e, 'e', ep)] = es.enter_context(nc.semaphore(f"s_{e}_{ep}"))
        self.dq = {}
        for q in ['sync', 'gpsimd']:
            self.dq[q] = {'i': 0, 'use': [0] * self.NDS}
            for k in range(self.NDS):
                self.sems[(q, 'd', k)] = es.enter_context(nc.semaphore(f"d_{q}_{k}"))
        self.ncc = 0
        self.waited = {e: {} for e in self.ENG}
        self.lastw = {}
        self.readers = {}

    def op(self, eng, fn, reads=(), writes=(), dma=False):
        deps = []
        for r in reads:
            if r in self.lastw:
                deps.append(self.lastw[r])
        for w in writes:
            if w in self.lastw:
                deps.append(self.lastw[w])
            deps.extend(self.readers.get(w, []))
        if dma == 'cc':
            self.ncc += 1
            sk = ('cc', 'c', self.ncc)
            self.sems[sk] = self.es.enter_context(self.nc.semaphore(f"cc_{self.ncc}"))
            tok = (sk, 1)
        elif dma:
            q = self.dq[eng]
            k = q['i'] % self.NDS
            q['i'] += 1
            q['use'][k] += 1
            sk = (eng, 'd', k)
            val = 16 * q['use'][k]
            if q['use'][k] > 1:
                deps.append((sk, val - 16))
            tok = (sk, val)
        else:
            self.cnt[eng] += 1
            ep = (self.cnt[eng] - 1) // self.EPOCH
            sk = (eng, 'e', ep)
            tok = (sk, self.cnt[eng] - ep * self.EPOCH)
        need = {}
        for (dk, v) in deps:
            if eng == 'tensor' and dk[0] == 'tensor' and dk[1] == 'e':
                continue
            if self.waited[eng].get(dk, 0) >= v:
                continue
            need[dk] = max(need.get(dk, 0), v)
        for dk, v in need.items():
            self.waited[eng][dk] = v
        self.ops[eng].append((list(need.items()), fn, tok))
        for w in writes:
            self.lastw[w] = tok
            self.readers[w] = []
        for r in reads:
            self.readers.setdefault(r, []).append(tok)
        return tok

    def barrier(self):
        toks = set()
        for t in self.lastw.values():
            toks.add(t)
        for rl in self.readers.values():
            toks.update(rl)
        best = {}
        for (dk, v) in toks:
            best[dk] = max(best.get(dk, 0), v)
        for e in self.ENG:
            need = []
            for dk, v in best.items():
                if self.waited[e].get(dk, 0) >= v:
                    continue
                self.waited[e][dk] = v
                need.append((dk, v))
            if need:
                self.ops[e].append((need, None, None))
        self.lastw = {}
        self.readers = {}

    def emit(self):
        with self.nc.Block() as block:
            for e in self.ENG:
                ops = self.ops[e]
                if not ops:
                    continue

                def body(engobj, ops=ops):
                    for waits, fn, tok in ops:
                        for dk, v in waits:
                            engobj.wait_ge(self.sems[dk], v)
                        if fn is None:
                            continue
                        ins = fn(engobj)
                        if tok[0][1] == 'c':
                            ins.then_inc(self.sems[tok[0]])
                        else:
                            ins.then_inc(self.sems[tok[0]], 16 if tok[0][1] == 'd' else 1)
                getattr(block, e)(body)
        self.ops = {e: [] for e in self.ENG}


class _Stop(Exception):
    pass


def _kstop(n):
    import os
    if int(os.environ.get('KSTOP', '99')) <= n:
        raise _Stop()


KBK = int(os.environ.get('KBK', '0'))
KBG = int(os.environ.get('KBG', '0'))


def bc(ap, shape):
    return ap.to_broadcast(list(shape))


def build(nlayers=DEPTH, stop_after=None, debug=False):
    nc = bass.Bass("TRN2", target_bir_lowering=False)
    dt_in = lambda n, s, d=F32: nc.dram_tensor(n, list(s), d, kind="ExternalInput").ap()
    x_in = dt_in("x_own", [NT, D])
    mem_in = dt_in("mem_b", [256, D])
    wp_in = dt_in("w_in_p", [DEPTH, D, WPC])
    gm_ln_g = dt_in("gm_ln_g", [DEPTH, 1, 256])
    gm_ln_b = dt_in("gm_ln_b", [DEPTH, 1, 256])
    gm_ws = dt_in("gm_ws", [DEPTH, 4, 128, 128])
    gm_bsT = dt_in("gm_bsT", [DEPTH, 128, 4])
    posk_in = dt_in("cmp_pos_kT", [DEPTH, 64, 32])
    posv_in = dt_in("cmp_pos_vT", [DEPTH, 64, 32])
    w1k_in = dt_in("cmp_k_w1", [DEPTH, 2048, 128])
    w1v_in = dt_in("cmp_v_w1", [DEPTH, 2048, 128])
    w2k_in = dt_in("cmp_k_w2", [DEPTH, 128, 64])
    w2v_in = dt_in("cmp_v_w2", [DEPTH, 128, 64])
    wmem_in = dt_in("w_mem_kv", [DEPTH, D, 512])
    wout_in = dt_in("w_out", [DEPTH, D, D])
    lng_in = dt_in("ln_g", [DEPTH, 1, D])
    lnb_in = dt_in("ln_b", [DEPTH, 1, D])
    ropeC_in = dt_in("ropeC", [128, NT])
    ropeS_in = dt_in("ropeS", [128, NT])
    eind_in = dt_in("eind", [64, SEQ], BF16)
    zc_in = dt_in("zc", [128, 256], BF16)
    ov_in = dt_in("ovc", [128, 4, 128], BF16)
    bcmp_in = dt_in("bcmp", [128, 4, 128], BF16)
    dmb_in = dt_in("dmb", [128, 4, 128], BF16)
    wmb_in = dt_in("wmb", [128, 8, 128], BF16)
    pm_in = dt_in("pswap", [128, 128], BF16)
    vm_in = dt_in("vmc", [128, NK, 1, 128], BF16)
    out_d = nc.dram_tensor("out", [NT, D], F32, kind="ExternalOutput").ap()
    xmid = nc.dram_tensor("xmid", [NT, D], F32)
    CCN = [4096, 4096, 2080, 2080]
    cc_src = [nc.dram_tensor(f"cc_src{i}", [128, CCN[i]], BF16) for i in range(4)]
    cc_dst = [nc.dram_tensor(f"cc_dst{i}", [512, CCN[i]], BF16) for i in range(4)]
    groups = [[0, 1, 2, 3], [4, 5, 6, 7]]
    dbg = {}
    if debug:
        dbg["qt"] = nc.dram_tensor("dbg_qt", [128, NK * 512], BF16, kind="ExternalOutput").ap()
        for bi, n_ in enumerate([4096, 4096, 2080, 2080]):
            dbg[f"cc{bi}"] = nc.dram_tensor(f"dbg_cc{bi}", [512, n_], BF16, kind="ExternalOutput").ap()
        dbg["ygm"] = nc.dram_tensor("dbg_ygm", [128, NK * 512], BF16, kind="ExternalOutput").ap()
        dbg["sz"] = nc.dram_tensor("dbg_sz", [128, NK * 512], BF16, kind="ExternalOutput").ap()
        dbg["gates"] = nc.dram_tensor("dbg_gates", [128, NK * 24], F32, kind="ExternalOutput").ap()
        dbg["kcmp"] = nc.dram_tensor("dbg_kcmp", [128, 512], BF16, kind="ExternalOutput").ap()
        dbg["rhsc"] = nc.dram_tensor("dbg_rhsc", [128, 8 * 193], BF16, kind="ExternalOutput").ap()
        dbg["mix"] = nc.dram_tensor("dbg_mix", [128, NK * 512], BF16, kind="ExternalOutput").ap()
        dbg["acc"] = nc.dram_tensor("dbg_acc", [128, NK * 2 * 1292], F32, kind="ExternalOutput").ap()
        dbg["nb"] = nc.dram_tensor("dbg_nb", [128, NK * 2 * 128], BF16, kind="ExternalOutput").ap()

    with ExitStack() as es:
        S = Sched(nc, es)
        sbP = lambda n, s, d: es.enter_context(nc.sbuf_tensor(n, list(s), d))
        psb = [es.enter_context(nc.psum_tensor(f"psb{i}", [128, 512], F32)) for i in range(7)]
        psT = es.enter_context(nc.psum_tensor("psT", [128, 1024], BF16))
        rr = {'i': 0}

        def nbank(pool=(4, 5, 6)):
            b = pool[rr['i'] % len(pool)]
            rr['i'] += 1
            return b

        QT = sbP("QT", [128, NK, 4, 128], BF16)
        YGM = sbP("YGM", [128, NK, 512], BF16)
        SZ = sbP("SZ", [128, NK, 512], BF16)
        GATES = sbP("GATES", [128, NK, 24], F32)
        identF = sbP("identF", [128, 128], F32)
        identB = sbP("identB", [128, 128], BF16)
        ZC = sbP("ZC", [128, 256], BF16)
        BCMP = sbP("BCMP", [128, 4, 128], BF16)
        DMB = sbP("DMB", [128, 4, 128], BF16)
        WMB = sbP("WMB", [128, 8, 128], BF16)

        V = lambda fn, r, w: S.op('vector', fn, reads=r, writes=w)
        A = lambda fn, r, w: S.op('scalar', fn, reads=r, writes=w)
        G = lambda fn, r, w: S.op('gpsimd', fn, reads=r, writes=w)
        T = lambda fn, r, w: S.op('tensor', fn, reads=r, writes=w)
        DMA = lambda fn, r, w, q='sync': S.op(q, fn, reads=r, writes=w, dma=True)

        G(lambda e: e.memset(identF[:], 0.0), [], ['identF'])
        G(lambda e: e.affine_select(out=identF[:], in_=identF[:], pattern=[[-1, 128]], compare_op=ALU.not_equal,
                                    fill=1.0, base=0, channel_multiplier=1), ['identF'], ['identF'])
        G(lambda e: e.tensor_copy(out=identB[:], in_=identF[:]), ['identF'], ['identB'])
        PMB = sbP("PMB", [128, 128], BF16)
        DMA(lambda e: e.dma_start(out=PMB[:], in_=pm_in[:, :]), [], ['PMB'])
        DMA(lambda e: e.dma_start(out=ZC[:], in_=zc_in[:, :]), [], ['ZC'])
        DMA(lambda e: e.dma_start(out=BCMP[:], in_=bcmp_in[:, :, :]), [], ['BCMP'])
        DMA(lambda e: e.dma_start(out=DMB[:], in_=dmb_in[:, :, :]), [], ['DMB'])
        DMA(lambda e: e.dma_start(out=WMB[:], in_=wmb_in[:, :, :]), [], ['WMB'])

        for l in range(nlayers):
            xsrc = x_in if l == 0 else xmid.ap()
            xdst = out_d if l == nlayers - 1 else xmid.ap()
            with ExitStack() as pa:
                sb = lambda n, s, d: pa.enter_context(nc.sbuf_tensor(f"{n}_A{l}", list(s), d))
                xT = sb("xT", [128, 8, NT], BF16)
                memT = sb("memT", [128, 8, 256], BF16)
                wst = sb("wst", [128, 8, 512], F32)
                wbf = [sb(f"wbf{i}", [128, 8, 512], BF16) for i in range(2)]
                ropeC = sb("ropeC", [128, NT], F32)
                ropeS = sb("ropeS", [128, NT], F32)
                xin = [sb(f"xin{i}", [128, D], F32) for i in range(3)]
                memK = sb("memK", [128, 2, 256], BF16)
                memV = sb("memV", [128, 2, 4, 65], BF16)
                mqT = sb("mqT", [128, 2, NT], BF16)
                wcT = sb("wcT", [128, 4, 128], BF16)
                wsf = sb("wsf", [128, 4, 128], F32)
                bsT = sb("bsT", [128, 4], F32)
                glng = sb("glng", [128, 256], F32)
                glnb = sb("glnb", [128, 256], F32)
                kst = [sb(f"kst{i}", [128, NT], BF16) for i in range(2)]
                vst = [sb(f"vst{i}", [128, NK, 2, 65], BF16) for i in range(2)]
                guL = [sb(f"gu{i}", [128, 256], BF16) for i in range(3)]
                gvL = [sb(f"gv{i}", [128, 4, 64], F32) for i in range(3)]
                gcL = [sb(f"gc{i}", [128, 4, 64], F32) for i in range(2)]
                gsqL = [sb(f"gsq{i}", [128, 4, 64], F32) for i in range(2)]
                mhalf = sb("mhalf", [128, 4], F32)
                G(lambda e: e.memset(mhalf[:], -0.5), [], ['mhalf'])
                vlnL = [sb(f"vln{i}", [128, 256], BF16) for i in range(2)]
                st4L = [sb(f"st4{i}", [128, 16], F32) for i in range(2)]
                pbf = [sb(f"pbf{i}", [128, 512], BF16) for i in range(2)]
                tA = [sb(f"tA{i}", [128, 512], F32) for i in range(2)]
                tB = [sb(f"tB{i}", [128, 512], F32) for i in range(2)]
                pmL = [sb(f"pm{i}", [128, 2, 2, 2, 128], BF16) for i in range(2)]
                om = sb("om", [128, 4, 64], F32)

                try:
                    DMA(lambda e: e.dma_start(out=ropeC[:], in_=ropeC_in[:, :]), [], ['ropeC'])
                    DMA(lambda e: e.dma_start(out=ropeS[:], in_=ropeS_in[:, :]), [], ['ropeS'])
                    for i in range(2):
                        G(lambda e, i=i: e.memset(vst[i][:, :, :, 64:65], 1.0), [], [('vst1', i)])
                    G(lambda e: e.memset(memV[:, :, :, 64:65], 1.0), [], ['memV1'])

                    cp = 0
                    for kt in range(NK + 2):
                        buf = xin[kt % 3]
                        bk = ('xin', kt % 3)
                        if kt < NK:
                            DMA(lambda e, kt=kt, buf=buf: e.dma_start(out=buf[:], in_=xsrc[kt * 128:(kt + 1) * 128, :]), [], [bk])
                        else:
                            m = kt - NK
                            DMA(lambda e, m=m, buf=buf: e.dma_start(out=buf[:], in_=mem_in[m * 128:(m + 1) * 128, :]), [], [bk])
                        for half in range(2):
                            b = nbank((0, 1, 2, 3))
                            for cc in range(4):
                                c = half * 4 + cc
                                T(lambda e, b=b, cc=cc, c=c, buf=buf: e.transpose(out=psb[b][:, cc * 128:(cc + 1) * 128],
                                                                                   in_=buf[:, c * 128:(c + 1) * 128], identity=identF[:]),
                                  [bk, 'identF'], [('ps', b)])
                            if kt < NK:
                                dst = xT[:, half * 4:(half + 1) * 4, kt * 128:(kt + 1) * 128]
                                dk = ('xT', kt)
                            else:
                                dst = memT[:, half * 4:(half + 1) * 4, (kt - NK) * 128:(kt - NK + 1) * 128]
                                dk = ('memT', kt - NK, half)
                            src = psb[b][:, :].rearrange("p (a b) -> p a b", a=4)
                            if cp % 2 == 0:
                                V(lambda e, dst=dst, src=src: e.tensor_copy(out=dst, in_=src), [('ps', b)], [(dk, half)])
                            else:
                                A(lambda e, dst=dst, src=src: e.copy(out=dst, in_=src), [('ps', b)], [(dk, half)])
                            cp += 1
                    _kstop(1)
                    xT_keys = [(('xT', kt), h) for kt in range(NK) for h in range(2)]
                    memT_keys = [(('memT', m, h), h) for m in range(2) for h in range(2)]

                    wi = {'i': 0}

                    def load_w(src_ap, ncols):
                        i = wi['i'] % 2
                        wi['i'] += 1
                        DMA(lambda e: e.dma_start(out=wst[:, :, 0:ncols], in_=src_ap.rearrange("(c p) n -> p c n", p=128)),
                            [], ['wst'])
                        V(lambda e: e.tensor_copy(out=wbf[i][:, 0:4, 0:ncols], in_=wst[:, 0:4, 0:ncols]), ['wst'], [('wbf', i, 0)])
                        A(lambda e: e.copy(out=wbf[i][:, 4:8, 0:ncols], in_=wst[:, 4:8, 0:ncols]), ['wst'], [('wbf', i, 1)])
                        return wbf[i], [('wbf', i, 0), ('wbf', i, 1)]

                    wm, wmk = load_w(wmem_in[l], 512)
                    for pr in range(2):
                        b = nbank((0, 1, 2, 3))
                        for c in range(8):
                            T(lambda e, b=b, c=c, pr=pr: e.matmul(psb[b][:, 0:256], lhsT=wm[:, c, pr * 128:(pr + 1) * 128],
                                                                 rhs=memT[:, c, :], start=(c == 0), stop=(c == 7)),
                              wmk + memT_keys, [('ps', b)])
                        V(lambda e, b=b, pr=pr: e.tensor_copy(out=memK[:, pr, :], in_=psb[b][:, 0:256]), [('ps', b)], [('memK', pr)])
                    for mt in range(2):
                        b = nbank((0, 1, 2, 3))
                        for c in range(8):
                            T(lambda e, b=b, c=c, mt=mt: e.matmul(psb[b][:, 0:256], lhsT=memT[:, c, mt * 128:(mt + 1) * 128],
                                                                 rhs=wm[:, c, 256:512], start=(c == 0), stop=(c == 7)),
                              wmk + memT_keys, [('ps', b)])
                        V(lambda e, b=b, mt=mt: e.tensor_copy(out=memV[:, mt, :, 0:64],
                                                             in_=psb[b][:, 0:256].rearrange("p (h d) -> p h d", h=4)),
                          [('ps', b), 'memV1'], [('memV', mt)])

                    _kstop(2)
                    DMA(lambda e: e.dma_start(out=wsf[:], in_=gm_ws[l].rearrange("g i j -> i g j")), [], ['wsf'])
                    DMA(lambda e: e.dma_start(out=bsT[:], in_=gm_bsT[l]), [], ['bsT'])
                    DMA(lambda e: e.dma_start(out=glng[:], in_=gm_ln_g[l].partition_broadcast(128)), [], ['glng'])
                    DMA(lambda e: e.dma_start(out=glnb[:], in_=gm_ln_b[l].partition_broadcast(128)), [], ['glnb'])
                    for g in range(4):
                        G(lambda e, g=g: e.affine_select(out=wsf[:, g, :], in_=wsf[:, g, :], pattern=[[-1, 128]],
                                                         compare_op=ALU.is_ge, fill=0.0, base=0, channel_multiplier=1),
                          ['wsf'], ['wsf'])
                    b = nbank((0, 1, 2, 3))
                    for g in range(4):
                        T(lambda e, g=g, b=b: e.transpose(out=psb[b][:, g * 128:(g + 1) * 128], in_=wsf[:, g, :], identity=identF[:]),
                          ['wsf', 'identF'], [('ps', b)])
                    V(lambda e, b=b: e.tensor_copy(out=wcT[:], in_=psb[b][:, :].rearrange("p (g i) -> p g i", g=4)), [('ps', b)], ['wcT'])

                    _kstop(3)
                    col0 = 0
                    for gi_, (gname, kind, cols) in enumerate(COLG):
                        _kstop(4 + gi_)
                        ncols = len(cols)
                        w, wk = load_w(wp_in[l][:, col0:col0 + ncols], ncols)
                        col0 += ncols
                        if gname == "uv":
                            uvb = {}

                            def uv_s0(kt):
                                p3 = kt % 3
                                gu_, gv_ = guL[p3], gvL[p3]
                                b = nbank((0, 1, 2, 3))
                                for c in range(8):
                                    T(lambda e, b=b, c=c, w=w, ncols=ncols: e.matmul(psb[b][:, 0:ncols], lhsT=xT[:, c, kt * 128:(kt + 1) * 128], rhs=w[:, c, 0:ncols],
                                                                   start=(c == 0), stop=(c == 7)), wk + [(('xT', kt), 0), (('xT', kt), 1)], [('ps', b)])
                                P = psb[b]
                                pk = ('ps', b)
                                A(lambda e: e.activation(out=gu_[:], in_=P[:, 0:256], func=AF.Gelu), [pk], [('gu', p3)])
                                A(lambda e: e.activation(out=gv_[:].rearrange("p g c -> p (g c)"), in_=P[:, 256:512], func=AF.Gelu), [pk], [('gv', p3)])

                            def uv_s1(kt):
                                pp = kt % 2
                                p3 = kt % 3
                                gv_, gc_, gsq_, st4_ = gvL[p3], gcL[pp], gsqL[pp], st4L[pp]
                                V(lambda e: e.tensor_reduce(out=st4_[:, 0:4], in_=gv_[:], axis=AX.X, op=ALU.add), [('gv', p3)], [('st_sum', pp)])
                                V(lambda e: e.tensor_scalar(out=st4_[:, 4:8], in0=st4_[:, 0:4], scalar1=-1.0 / 64, scalar2=None, op0=ALU.mult),
                                  [('st_sum', pp)], [('st_nm', pp)])
                                V(lambda e: e.tensor_tensor(out=gc_[:], in0=gv_[:], in1=bc(st4_[:, 4:8].unsqueeze(2), [128, 4, 64]), op=ALU.add),
                                  [('gv', p3), ('st_nm', pp)], [('gc', pp)])
                                G(lambda e: e.tensor_tensor(out=gsq_[:], in0=gc_[:], in1=gc_[:], op=ALU.mult), [('gc', pp)], [('gsq', pp)])
                                V(lambda e: e.tensor_reduce(out=st4_[:, 8:12], in_=gsq_[:], axis=AX.X, op=ALU.add), [('gsq', pp)], [('st_ss', pp)])
                                V(lambda e: e.tensor_scalar(out=st4_[:, 8:12], in0=st4_[:, 8:12], scalar1=1.0 / 64, scalar2=LN_EPS,
                                                            op0=ALU.mult, op1=ALU.add), [('st_ss', pp)], [('st_ss', pp)])
                                G(lambda e: e.tensor_tensor(out=st4_[:, 12:16], in0=st4_[:, 8:12], in1=mhalf[:, 0:4], op=ALU.pow), [('st_ss', pp), 'mhalf'], [('st_rs', pp)])

                            def uv_s2(kt):
                                pp = kt % 2
                                gc_, vln_, st4_ = gcL[pp], vlnL[pp], st4L[pp]
                                V(lambda e: e.tensor_tensor(out=gc_[:], in0=gc_[:], in1=bc(st4_[:, 12:16].unsqueeze(2), [128, 4, 64]), op=ALU.mult),
                                  [('gc', pp), ('st_rs', pp)], [('gc', pp)])
                                G(lambda e: e.tensor_tensor(out=gc_[:].rearrange("p g c -> p (g c)"), in0=gc_[:].rearrange("p g c -> p (g c)"),
                                                            in1=glng[:], op=ALU.mult), [('gc', pp), 'glng'], [('gc', pp)])
                                G(lambda e: e.tensor_tensor(out=vln_[:], in0=gc_[:].rearrange("p g c -> p (g c)"), in1=glnb[:], op=ALU.add),
                                  [('gc', pp), 'glnb'], [('vln', pp)])
                                b2 = nbank((4, 5, 6))
                                uvb[kt] = b2
                                for g in range(4):
                                    T(lambda e, g=g: e.matmul(psb[b2][:, g * 64:(g + 1) * 64], lhsT=wcT[:, g, :], rhs=vln_[:, g * 64:(g + 1) * 64],
                                                              start=True, stop=True), [('vln', pp), 'wcT'], [('ps', b2)])

                            def uv_s3(kt):
                                pp = kt % 2
                                p3 = kt % 3
                                gu_, gsq_ = guL[p3], gsqL[pp]
                                b2 = uvb[kt]
                                V(lambda e: e.tensor_tensor(out=gsq_[:], in0=psb[b2][:, 0:256].rearrange("p (g c) -> p g c", g=4),
                                                            in1=bc(bsT[:, :].unsqueeze(2), [128, 4, 64]), op=ALU.add), [('ps', b2), 'bsT'], [('gsq', pp)])
                                V(lambda e: e.tensor_tensor(out=YGM[:, kt, 0:256], in0=gsq_[:].rearrange("p g c -> p (g c)"), in1=gu_[:], op=ALU.mult),
                                  [('gsq', pp), ('gu', p3)], [('YGMa', kt)])
                            for step in range(NK + 2):
                                if step < NK:
                                    uv_s0(step)
                                if 0 <= step - 2 < NK:
                                    uv_s3(step - 2)
                                if 0 <= step - 1 < NK:
                                    uv_s2(step - 1)
                                if step < NK:
                                    uv_s1(step)
                        elif kind == "T":
                            for kt in range(NK):
                                b = nbank((0, 1, 2, 3))
                                for c in range(8):
                                    T(lambda e, b=b, c=c, kt=kt, w=w, ncols=ncols: e.matmul(
                                        psb[b][:, 0:ncols], lhsT=xT[:, c, kt * 128:(kt + 1) * 128], rhs=w[:, c, 0:ncols],
                                        start=(c == 0), stop=(c == 7)),
                                      wk + [(('xT', kt), 0), (('xT', kt), 1)], [('ps', b)])
                                P = psb[b]
                                pk = ('ps', b)
                                if gname == "uv":
                                    pass
                                elif gname == "zz":
                                    tb = tA[kt % 2]
                                    A(lambda e, P=P, tb=tb: e.activation(out=tb[:, 0:256], in_=P[:, 0:256], func=AF.Silu), [pk], [('tA', kt % 2)])
                                    A(lambda e, P=P, kt=kt: e.activation(out=YGM[:, kt, 256:512], in_=P[:, 256:512], func=AF.Silu), [pk], [('YGMb', kt)])
                                    G(lambda e, kt=kt, tb=tb: e.tensor_tensor(out=YGM[:, kt, 0:256], in0=YGM[:, kt, 0:256], in1=tb[:, 0:256], op=ALU.mult),
                                      [('tA', kt % 2), ('YGMa', kt)], [('YGMa', kt)])
                                elif gname == "nz":
                                    A(lambda e, P=P, kt=kt: e.activation(out=SZ[:, kt, :], in_=P[:, 0:512], func=AF.Silu), [pk], [('SZ', kt)])
                                elif gname == "vg":
                                    for i in range(2):
                                        V(lambda e, P=P, kt=kt, i=i: e.tensor_copy(out=vst[i][:, kt, :, 0:64],
                                                                                 in_=P[:, i * 128:(i + 1) * 128].rearrange("p (g d) -> p g d", g=2)),
                                          [pk, ('vst1', i)], [('vst', i, kt)])
                                    A(lambda e, P=P, kt=kt: e.activation(out=GATES[:, kt, :], in_=P[:, 256:280], func=AF.Sigmoid),
                                      [pk, ('vst', 0, kt), ('vst', 1, kt)], [('GATES', kt)])
                        else:
                            nch = ncols // 128
                            roped = gname[0] in ("q", "k")
                            for tg in range(4):
                                xk = [(('xT', kt), h) for kt in range(tg * 4, tg * 4 + 4) for h in range(2)]
                                bl = []
                                for ch in range(nch):
                                    b = nbank((0, 1, 2, 3))
                                    bl.append(b)
                                    for c in range(8):
                                        T(lambda e, b=b, c=c, ch=ch, tg=tg, w=w: e.matmul(
                                            psb[b][:, :], lhsT=w[:, c, ch * 128:(ch + 1) * 128], rhs=xT[:, c, tg * 512:(tg + 1) * 512],
                                            start=(c == 0), stop=(c == 7)), wk + xk, [('ps', b)])
                                tsl = slice(tg * 512, (tg + 1) * 512)
                                if roped:
                                    b1 = bl[0]
                                    b2 = nbank((0, 1, 2, 3))
                                    ta, tb = tA[tg % 2], tB[tg % 2]
                                    pb_ = pbf[tg % 2]
                                    A(lambda e, b1=b1, pb_=pb_: e.copy(out=pb_[:], in_=psb[b1][:, :]), [('ps', b1)], [('pbf', tg % 2)])
                                    T(lambda e, b2=b2, pb_=pb_: e.matmul(psb[b2][:, :], lhsT=PMB[:], rhs=pb_[:], start=True, stop=True),
                                      [('pbf', tg % 2), 'PMB'], [('ps', b2)])
                                    V(lambda e, b1=b1, ta=ta, tsl=tsl: e.tensor_tensor(out=ta[:], in0=psb[b1][:, :], in1=ropeC[:, tsl], op=ALU.mult),
                                      [('ps', b1), ('pbf', tg % 2), 'ropeC'], [('tA', tg % 2)])
                                    V(lambda e, b2=b2, tb=tb, tsl=tsl: e.tensor_tensor(out=tb[:], in0=psb[b2][:, :], in1=ropeS[:, tsl], op=ALU.mult),
                                      [('ps', b2), 'ropeS'], [('tB', tg % 2)])
                                    if gname[0] == "q":
                                        hg = int(gname[1])
                                        dst = QT[:, tg * 4:(tg + 1) * 4, hg, :]
                                        dkey = ('QT', tg, hg)
                                        G(lambda e, ta=ta, tb=tb, dst=dst: e.tensor_tensor(out=dst, in0=ta[:].rearrange("p (a b) -> p a b", a=4),
                                                                                          in1=tb[:].rearrange("p (a b) -> p a b", a=4), op=ALU.add),
                                          [('tA', tg % 2), ('tB', tg % 2)], [dkey])
                                    else:
                                        kb_ = {"kc": 0, "ks": 1, "kw": 0}[gname]
                                        G(lambda e, ta=ta, tb=tb, kb_=kb_, tsl=tsl: e.tensor_tensor(out=kst[kb_][:, tsl], in0=ta[:], in1=tb[:], op=ALU.add),
                                          [('tA', tg % 2), ('tB', tg % 2)], [('kst', kb_, tg)])
                                else:
                                    A(lambda e, b=bl[0], tsl=tsl: e.copy(out=kst[1][:, tsl], in_=psb[b][:, :]), [('ps', bl[0])], [('kst', 1, tg)])
                                    V(lambda e, b=bl[1], tsl=tsl: e.tensor_copy(out=mqT[:, 0, tsl], in_=psb[b][:, :]), [('ps', bl[1])], [('mqT', 0, tg)])
                                    V(lambda e, b=bl[2], tsl=tsl: e.tensor_copy(out=mqT[:, 1, tsl], in_=psb[b][:, :]), [('ps', bl[2])], [('mqT', 1, tg)])
                        if gname in ("kc", "ks", "kw", "vcmq"):
                            si, kb_, bi, o = {"kc": (0, 0, 0, 0), "vcmq": (1, 1, 0, 2048), "ks": (2, 1, 1, 0), "kw": (3, 0, 1, 2048)}[gname]
                            DMA(lambda e, kb_=kb_, bi=bi, o=o: e.dma_start(out=cc_src[bi].ap()[:, o:o + NT], in_=kst[kb_][:]),
                                [('kst', kb_, tg) for tg in range(4)], [('cc_src', si)])
                        if gname == "vcmq":
                            for i in range(2):
                                DMA(lambda e, i=i: e.dma_start(out=cc_src[2 + i].ap()[:, :], in_=vst[i][:].rearrange("p k g e -> p (k g e)")),
                                    [('vst', i, kt) for kt in range(NK)], [('cc_src', 4 + i)])
                            if not os.environ.get("KNOCC"):
                                ccr = [[('cc_src', 0), ('cc_src', 1)], [('cc_src', 2), ('cc_src', 3)], [('cc_src', 4)], [('cc_src', 5)]]
                                for bi in range(4):
                                    S.op('gpsimd', lambda e, bi=bi: e.collective_compute("AllGather", ALU.bypass, replica_groups=groups,
                                                                                       ins=[cc_src[bi].ap().opt()], outs=[cc_dst[bi].ap().opt()]),
                                         reads=ccr[bi], writes=[('cc_dst', bi)], dma='cc')
                        if gname == "zz":
                            def mem_qk(kt):
                                tg = kt // 4
                                pm_ = pmL[kt % 2]
                                for half, b in ((0, 4), (1, 5)):
                                    rs = slice(half * 64, half * 64 + 64)
                                    for mt in range(2):
                                        for hh in range(2):
                                            T(lambda e, b=b, mt=mt, hh=hh, rs=rs: e.matmul(
                                                psb[b][:, (mt * 2 + hh) * 128:(mt * 2 + hh + 1) * 128],
                                                lhsT=memK[rs, hh, mt * 128:(mt + 1) * 128],
                                                rhs=mqT[rs, hh, kt * 128:(kt + 1) * 128], start=True, stop=True),
                                              [('memK', 0), ('memK', 1), ('mqT', 0, tg), ('mqT', 1, tg)], [('ps', b)])
                                    A(lambda e, b=b, half=half: e.activation(out=pm_[:, half, :, :, :].rearrange("p m h q -> p (m h q)"),
                                                                            in_=psb[b][:, :], func=AF.Exp, scale=SCALE),
                                      [('ps', b)], [('pm', kt % 2, half)])

                            def mem_pv(kt):
                                pm_ = pmL[kt % 2]
                                b = 6
                                for h in range(4):
                                    for mt in range(2):
                                        T(lambda e, h=h, mt=mt: e.matmul(psb[b][:, h * 65:(h + 1) * 65], lhsT=pm_[:, h % 2, mt, h // 2, :],
                                                                        rhs=memV[:, mt, h, :], start=(mt == 0), stop=(mt == 1)),
                                          [('pm', kt % 2, 0), ('pm', kt % 2, 1), ('memV', 0), ('memV', 1)], [('ps', b)])
                                ov = psb[b][:, 0:260].rearrange("p (h e) -> p h e", h=4)
                                V(lambda e: e.reciprocal(out=st4L[0][:, 0:4], in_=ov[:, :, 64]), [('ps', b)], [('st_sum', 0)])
                                V(lambda e: e.tensor_tensor(out=om[:], in0=ov[:, :, 0:64], in1=bc(st4L[0][:, 0:4].unsqueeze(2), [128, 4, 64]), op=ALU.mult),
                                  [('ps', b), ('st_sum', 0)], ['om'])
                                G(lambda e: e.tensor_tensor(out=YGM[:, kt, 256:512], in0=om[:].rearrange("p h d -> p (h d)"),
                                                            in1=YGM[:, kt, 256:512], op=ALU.mult), ['om', ('YGMb', kt)], [('YGMb', kt)])
                            for kt in range(NK + 1):
                                if kt < NK:
                                    mem_qk(kt)
                                if kt >= 1:
                                    mem_pv(kt - 1)
                    if debug and l == 0:
                        DMA(lambda e: e.dma_start(out=dbg["qt"][:, :], in_=QT[:].rearrange("p k h q -> p (k h q)")),
                            [('QT', tg, hg) for tg in range(4) for hg in range(4)], ['dbg_qt'])
                        DMA(lambda e: e.dma_start(out=dbg["ygm"][:, :], in_=YGM[:].rearrange("p k c -> p (k c)")),
                            [('YGMa', kt) for kt in range(NK)] + [('YGMb', kt) for kt in range(NK)], ['dbg_ygm'])
                        DMA(lambda e: e.dma_start(out=dbg["sz"][:, :], in_=SZ[:].rearrange("p k c -> p (k c)")),
                            [('SZ', kt) for kt in range(NK)], ['dbg_sz'])
                        DMA(lambda e: e.dma_start(out=dbg["gates"][:, :], in_=GATES[:].rearrange("p k c -> p (k c)")),
                            [('GATES', kt) for kt in range(NK)], ['dbg_gates'])
                        for bi in range(4):
                            DMA(lambda e, bi=bi: e.dma_start(out=dbg[f"cc{bi}"][:, :], in_=cc_dst[bi].ap()[:, :]), [('cc_dst', bi)], [f'dbg_cc{bi}'])
                except _Stop:
                    pass
                S.barrier()
                S.emit()
            if stop_after == "A":
                break
            with ExitStack() as lb:
                sbL = lambda n, s, d: lb.enter_context(nc.sbuf_tensor(f"{n}_L{l}", list(s), d))
                KcmpT = sbL("KcmpT", [128, 512], BF16)
                RHSc = sbL("RHSc", [128, 4, 2, 193], BF16)
                Wout = sbL("Wout", [128, 8, D], BF16)
                wos = sbL("wos", [128, 8, 128], F32)
                lngt = sbL("lngt", [128, D], F32)
                lnbt = sbL("lnbt", [128, D], F32)
                with ExitStack() as p0:
                    sb = lambda n, s, d: p0.enter_context(nc.sbuf_tensor(f"{n}_B0{l}", list(s), d))
                    tT = [sb("kcT", [128, 4, NT], BF16), sb("vcT", [128, 4, NT], BF16)]
                    w1s = sb("w1s", [128, 32, 128], F32)
                    w1b = [sb("w1k", [128, 32, 128], BF16), sb("w1v", [128, 32, 128], BF16)]
                    posf = sb("posf", [128, 2, 32], F32)
                    posb = sb("posb", [128, 2, 32], BF16)
                    w2f = sb("w2f", [128, 2, 64], F32)
                    w2kp = sb("w2kp", [128, 2, 128], BF16)
                    w2vb = sb("w2vb", [128, 64], BF16)
                    hid = sb("hid", [128, 4, 512], BF16)
                    pos_ins = [posk_in, posv_in]
                    for t in range(2):
                        for hf in range(2):
                            DMA(lambda e, t=t, hf=hf: e.dma_start(out=posf[hf * 64:(hf + 1) * 64, t, :], in_=pos_ins[t][l]), [], [('posf', t, hf)])
                    V(lambda e: e.tensor_copy(out=posb[:], in_=posf[:]), [('posf', t, hf) for t in range(2) for hf in range(2)], ['posb'])
                    DMA(lambda e: e.dma_start(out=w2f[:, 0, :], in_=w2k_in[l]), [], [('w2f', 0)])
                    DMA(lambda e: e.dma_start(out=w2f[:, 1, :], in_=w2v_in[l]), [], [('w2f', 1)])
                    G(lambda e: e.memset(w2kp[:], 0.0), [], ['w2kp'])
                    G(lambda e: e.tensor_copy(out=w2kp[:, 0, 0:64], in_=w2f[:, 0, :]), ['w2kp', ('w2f', 0)], ['w2kp'])
                    G(lambda e: e.tensor_copy(out=w2kp[:, 1, 64:128], in_=w2f[:, 0, :]), ['w2kp', ('w2f', 0)], ['w2kp'])
                    G(lambda e: e.tensor_copy(out=w2vb[:], in_=w2f[:, 1, :]), [('w2f', 1)], ['w2vb'])
                    w1_ins = [w1k_in, w1v_in]
                    for t in range(2):
                        if t == 1:
                            for t2_ in range(2):
                                DMA(lambda e, t2_=t2_: e.dma_start(out=tT[t2_][:], in_=cc_dst[0].ap()[:, t2_ * 2048:(t2_ + 1) * 2048].rearrange("(r p) c -> p r c", r=4)),
                                    [], [('tT', t2_)])
                        for hf in range(2):
                            DMA(lambda e, t=t, hf=hf: e.dma_start(out=w1s[hf * 64:(hf + 1) * 64, :, :],
                                                                 in_=w1_ins[t][l].rearrange("(l d) m -> d l m", d=64)), [], [('w1s', hf)])
                        V(lambda e, t=t: e.tensor_copy(out=w1b[t][:, 0:16, :], in_=w1s[:, 0:16, :]), [('w1s', 0), ('w1s', 1)], [('w1b', t, 0)])
                        A(lambda e, t=t: e.copy(out=w1b[t][:, 16:32, :], in_=w1s[:, 16:32, :]), [('w1s', 0), ('w1s', 1)], [('w1b', t, 1)])
                    DMA(lambda e: e.dma_start(out=lngt[:], in_=lng_in[l].partition_broadcast(128)), [], ['lngt'])
                    DMA(lambda e: e.dma_start(out=lnbt[:], in_=lnb_in[l].partition_broadcast(128)), [], ['lnbt'])
                    for cs in range(8):
                        DMA(lambda e, cs=cs: e.dma_start(out=wos[:], in_=wout_in[l][:, cs * 128:(cs + 1) * 128].rearrange("(c p) n -> p c n", p=128)),
                            [], ['wos'])
                        if cs % 2 == 0:
                            V(lambda e, cs=cs: e.tensor_copy(out=Wout[:, :, cs * 128:(cs + 1) * 128], in_=wos[:]), ['wos'], [('Wout', cs)])
                        else:
                            A(lambda e, cs=cs: e.copy(out=Wout[:, :, cs * 128:(cs + 1) * 128], in_=wos[:]), ['wos'], [('Wout', cs)])
                    biasS = sb("biasS", [128, 4], F32)
                    for t in range(2):
                        for g in range(2):
                            gr = slice(g * 64, g * 64 + 64)
                            bb = 4 + g
                            for li in range(32):
                                T(lambda e, bb=bb, t=t, gr=gr, li=li: e.matmul(psb[bb][:, t * 8:(t + 1) * 8], lhsT=w1b[t][gr, li, :],
                                                                              rhs=bc(posb[gr, t, li:li + 1], [64, 8]), start=(li == 0), stop=(li == 31)),
                                  [('w1b', t, 0), ('w1b', t, 1), 'posb'], [('ps', bb)])
                            V(lambda e, bb=bb, t=t, g=g: e.tensor_copy(out=biasS[:, t * 2 + g:t * 2 + g + 1], in_=psb[bb][:, t * 8:t * 8 + 1]),
                              [('ps', bb)], [('biasS', t * 2 + g)])
                    for t in range(2):
                        for g in range(2):
                            b = t * 2 + g
                            gr = slice(g * 64, g * 64 + 64)
                            rk = [('w1b', t, 0), ('w1b', t, 1), ('tT', t)]
                            pk = [('ps', b)]
                            pf = psb[b]
                            svm = tT[t][gr, :, :].rearrange("p r (k m l) -> p m r k l", k=16, m=8)
                            for li in range(32):
                                if li < 16:
                                    T(lambda e, pf=pf, svm=svm, li=li, t=t, gr=gr: e.matmul(pf[:, :], lhsT=w1b[t][gr, li, :], rhs=svm[:, :, :, :, li],
                                                                                          start=(li == 0), stop=False), rk, pk)
                                else:
                                    T(lambda e, pf=pf, svm=svm, li=li, t=t, gr=gr: e.matmul(pf[:, 0:448], lhsT=w1b[t][gr, li, :],
                                                                                          rhs=svm[:, 1:8, :, :, li - 16], start=False, stop=False), rk, pk)
                                    T(lambda e, pf=pf, svm=svm, li=li, t=t, gr=gr: e.matmul(pf[:, 448:496], lhsT=w1b[t][gr, li, :],
                                                                                          rhs=svm[:, 0, 1:4, :, li - 16], start=False, stop=False), rk, pk)
                                    T(lambda e, pf=pf, svm=svm, li=li, t=t, gr=gr: e.matmul(pf[:, 496:511], lhsT=w1b[t][gr, li, :],
                                                                                          rhs=svm[:, 0, 0, 1:16, li - 16], start=False, stop=(li == 31)), rk, pk)
                            A(lambda e, b=b: e.activation(out=hid[:, b, :].rearrange("p (rk m) -> p m rk", m=8),
                                                          in_=psb[b][:, :].rearrange("p (m rk) -> p m rk", m=8), func=AF.Gelu, bias=biasS[:, b:b + 1]),
                              pk + [('biasS', b)], [('hid', b)])
                    T(lambda e: e.matmul(psb[4][:, :], lhsT=w2kp[:, 0, :], rhs=hid[:, 0, :], start=True, stop=False), ['w2kp', ('hid', 0)], [('ps', 4)])
                    T(lambda e: e.matmul(psb[4][:, :], lhsT=w2kp[:, 1, :], rhs=hid[:, 1, :], start=False, stop=True), ['w2kp', ('hid', 1)], [('ps', 4)])
                    V(lambda e: e.tensor_copy(out=KcmpT[:], in_=psb[4][:, :]), [('ps', 4)], ['KcmpT'])
                    for g in range(2):
                        for rp in range(4):
                            T(lambda e, g=g, rp=rp: e.matmul(psb[5][:, (rp * 2 + g) * 64:(rp * 2 + g + 1) * 64], lhsT=hid[:, 2 + g, rp * 128:(rp + 1) * 128],
                                                             rhs=w2vb[:], start=True, stop=True), ['w2vb', ('hid', 2 + g)], [('ps', 5)])
                    G(lambda e: e.memset(RHSc[:, :, :, 64:65], 1.0), [], ['RHSc1'])
                    V(lambda e: e.tensor_copy(out=RHSc[:, :, :, 0:64], in_=psb[5][:, :].rearrange("p (r g d) -> p r g d", r=4, g=2)),
                      [('ps', 5), 'RHSc1'], ['RHScV'])
                    for g in range(2):
                        DMA(lambda e, g=g: e.dma_start(out=RHSc[:, :, g, 65:193], in_=ov_in[:, :, :]), [], [('RHScO', g)])
                    if debug and l == 0:
                        DMA(lambda e: e.dma_start(out=dbg["kcmp"][:, :], in_=KcmpT[:]), ['KcmpT'], ['dbg_kcmp'])
                        DMA(lambda e: e.dma_start(out=dbg["rhsc"][:, :], in_=RHSc[:].rearrange("p r g e -> p (r g e)")),
                            ['RHScV', 'RHSc1', ('RHScO', 0), ('RHScO', 1)], ['dbg_rhsc'])
                    S.barrier()
                    S.emit()
                if stop_after == "B0":
                    break
                with ExitStack() as p1:
                    sb = lambda n, s, d: p1.enter_context(nc.sbuf_tensor(f"{n}_B1{l}", list(s), d))
                    Kaug = [sb(f"Kaug{g}", [128, 64, 128], BF16) for g in range(2)]
                    Kwin = sb("Kwin", [128, 64, 128], BF16)
                    Vs = sb("Vs", [128, 64, 2, 65], BF16)
                    Vw = sb("Vw", [128, 64, 2, 65], BF16)
                    Qaug = [sb(f"Qaug{g}", [128, 3, 512], BF16) for g in range(2)]
                    Pt = [sb(f"Pt{i}", [128, 512], BF16) for i in range(4)]
                    vmk = [sb(f"vmk{i}", [128, 1, 128], BF16) for i in range(2)]
                    imp = sb("imp", [128, 128], F32)
                    impm = sb("impm", [128, 128], F32)
                    wkt = sb("wkt", [128, 128], F32)
                    selm = sb("selm", [128, 128], F32)
                    m8 = sb("m8", [128, 16], F32)
                    thr = sb("thr", [128, 1], F32)
                    NBt = sb("NBt", [128, 192], BF16)
                    rsA = sb("rsA", [128, 12], F32)
                    riA = sb("riA", [128, 12], F32)
                    fA = sb("fA", [128, 12], F32)
                    oacc = sb("oacc", [128, 4, 64], F32)
                    t2 = sb("t2", [128, 4, 64], F32)
                    t3 = sb("t3", [128, 4, 64], F32)
                    mixn = sb("mixn", [128, 512], BF16)
                    mixT = sb("mixT", [128, 8, 128], BF16)
                    xblk = sb("xblk", [128, D], F32)
                    zb = sb("zb", [128, D], F32)
                    stt = sb("stt", [128, 12], F32)
                    mv = sb("mv", [128, 4], F32)
                    zeroB = sb("zeroB", [128, 386], BF16)
                    G(lambda e: e.memset(zeroB[:], 0.0), [], ['zeroB'])
                    mhalfB = sb("mhalfB", [128, 1], F32)
                    G(lambda e: e.memset(mhalfB[:], -0.5), [], ['mhalfB'])
                    G(lambda e: e.memset(NBt[:], 0.0), [], ['NBa'])
                    accS = sb("accS", [128, 1292], F32)

                    try:
                        ksrc = cc_dst[1].ap()[:, 0:2048].rearrange("(r p) c -> p r c", r=4)
                        DMA(lambda e: e.dma_start(out=Kaug[0][0:64, :, :].rearrange("p (r k) c -> p r (k c)", r=4), in_=ksrc[0:64]), [], [('Kaug', 0, 'k')])
                        DMA(lambda e: e.dma_start(out=Kaug[1][64:128, :, :].rearrange("p (r k) c -> p r (k c)", r=4), in_=ksrc[64:128]), [], [('Kaug', 1, 'k')])
                        DMA(lambda e: e.dma_start(out=Kaug[0][64:128, :, :].rearrange("p s c -> p (s c)"), in_=eind_in[:, :]), [], [('Kaug', 0, 'e')])
                        DMA(lambda e: e.dma_start(out=Kaug[1][0:64, :, :].rearrange("p s c -> p (s c)"), in_=eind_in[:, :]), [], [('Kaug', 1, 'e')], q='gpsimd')
                        DMA(lambda e: e.dma_start(out=Kwin[:].rearrange("p (r k) c -> p r (k c)", r=4),
                                                  in_=cc_dst[1].ap()[:, 2048:4096].rearrange("(r p) c -> p r c", r=4)), [], ['Kwin'], q='gpsimd')
                        DMA(lambda e: e.dma_start(out=Vs[:].rearrange("p (r k) g e -> p r (k g e)", r=4),
                                                  in_=cc_dst[2].ap()[:, :].rearrange("(r p) c -> p r c", r=4)), [], ['Vs'])
                        DMA(lambda e: e.dma_start(out=Vw[:].rearrange("p (r k) g e -> p r (k g e)", r=4),
                                                  in_=cc_dst[3].ap()[:, :].rearrange("(r p) c -> p r c", r=4)), [], ['Vw'], q='gpsimd')
                        Wk = [('Wout', cs) for cs in range(8)]
                        G(lambda e: e.memset(Qaug[0][64:128, 2, :], 0.0), [], [('Qz', 0)])
                        G(lambda e: e.memset(Qaug[1][0:64, 2, :], 0.0), [], [('Qz', 1)])
                        KaugK = [[('Kaug', g, 'k'), ('Kaug', g, 'e')] for g in range(2)]

                        tiles = []

                        def mk_group(k, g):
                            M = 8 * (k + 1)
                            sk = 128 - 8 * k
                            gr = slice(g * 64, g * 64 + 64)
                            mr = slice(64, 128) if g == 0 else slice(0, 64)
                            qk = ('Qq', g)
                            vk = ('vmk', k % 2)
                            vm_ = vmk[k % 2]
                            ob = [psb[i][:, 0:386].rearrange("p (h e) -> p h e", h=2) for i in range(2)]

                            def group_begin():
                                if g == 0:
                                    DMA(lambda e: e.dma_start(out=vmk[k % 2][:], in_=vm_in[:, k, :, :]), [], [('vmk', k % 2)])

                            def q_copy():
                                V(lambda e: e.tensor_copy(out=Qaug[g][gr, :, :],
                                                          in_=bc(QT[gr, k, :, :].rearrange("p h q -> p (h q)").unsqueeze(1), [64, 3, 512])),
                                  [], [qk])
                            qcopies.append(q_copy)

                            def zero_acc(banks):
                                def f():
                                    for b_, n_ in banks:
                                        T(lambda e, b_=b_, n_=n_: e.matmul(psb[b_][:, 0:n_], lhsT=zeroB[:, 0:128], rhs=zeroB[:, 0:n_], start=True, stop=False),
                                          ['zeroB'], [('ps', b_)])
                                return f

                            for rp in range(4):
                                def qk_c(b, rp=rp):
                                    T(lambda e: e.matmul(psb[b][0:M, :], lhsT=KcmpT[:, rp * 128:rp * 128 + M], rhs=Qaug[g][:, 2, :],
                                                         start=True, stop=False), [qk, ('Qz', g)], [('ps', b)])
                                    T(lambda e: e.matmul(psb[b][0:M, :].rearrange("p (h q) -> p h q", h=4), lhsT=ZC[:, sk:sk + M],
                                                         rhs=bc(BCMP[:, rp, :].unsqueeze(1), [128, 4, 128]), start=False, stop=True), [], [('ps', b)])

                                def exp_c(b, pi):
                                    A(lambda e: e.activation(out=Pt[pi][0:M, :], in_=psb[b][0:M, :], func=AF.Exp, scale=SCALE), [('ps', b)], [('Pt', pi)])

                                def pv_c(pi, rp=rp):
                                    for h in range(4):
                                        bo, co = h // 2, (h % 2) * 193
                                        T(lambda e, h=h, bo=bo, co=co: e.matmul(psb[bo][:, co:co + 193], lhsT=Pt[pi][0:M, h * 128:(h + 1) * 128],
                                                                               rhs=RHSc[0:M, rp, g, :], start=False, stop=(rp == 3)),
                                          [('Pt', pi)], [('ps', bo)])
                                t = dict(qk=qk_c, exp=exp_c, pv=pv_c, pre_qk=[], pre_pv=[], post_pv=[])
                                if rp == 0:
                                    t['pre_qk'].append(group_begin)
                                    t['pre_pv'].append(zero_acc([(0, 386), (1, 386)]))
                                tiles.append(t)

                            def imp_chain():
                                for i in range(2):
                                    V(lambda e, i=i: e.tensor_scalar(out=rsA[:, 2 * i:2 * i + 2], in0=ob[i][:, :, 64], scalar1=1e-30, scalar2=None, op0=ALU.max),
                                      [('ps', i)], [('rsA', i)])
                                V(lambda e: e.reciprocal(out=riA[:, 0:4], in_=rsA[:, 0:4]), [('rsA', 0), ('rsA', 1)], ['riAc'])
                                V(lambda e: e.scalar_tensor_tensor(out=imp[:], in0=ob[0][:, 0, 65:193], scalar=riA[:, 0:1], in1=vm_[:, 0, :],
                                                                   op0=ALU.mult, op1=ALU.add), [('ps', 0), 'riAc', vk], ['imp'])
                                for h in range(1, 4):
                                    V(lambda e, h=h: e.scalar_tensor_tensor(out=imp[:], in0=ob[h // 2][:, h % 2, 65:193], scalar=riA[:, h:h + 1], in1=imp[:],
                                                                            op0=ALU.mult, op1=ALU.add), [('ps', h // 2), 'riAc', 'imp'], ['imp'])
                                V(lambda e: e.max(out=m8[:, 0:8], in_=imp[:]), ['imp'], ['m8a'])
                                V(lambda e: e.match_replace(out=wkt[:], in_to_replace=m8[:, 0:8], in_values=imp[:], imm_value=-1e30), ['imp', 'm8a'], ['wkt'])
                                V(lambda e: e.max(out=m8[:, 8:16], in_=wkt[:]), ['wkt'], ['m8b'])
                                V(lambda e: e.tensor_scalar(out=selm[:], in0=imp[:], scalar1=m8[:, 15:16], scalar2=None, op0=ALU.is_ge), ['imp', 'm8b'], ['selm'])
                                V(lambda e: e.tensor_scalar(out=NBt[:, 64:192], in0=selm[:], scalar1=-1.0, scalar2=BIG, op0=ALU.add, op1=ALU.mult), ['selm'], ['NBa'])
                                V(lambda e: e.tensor_copy(out=accS[:, 0:386], in_=psb[0][:, 0:386]), [('ps', 0)], [('accS', 0)])
                                V(lambda e: e.tensor_copy(out=accS[:, 386:772], in_=psb[1][:, 0:386]), [('ps', 1)], [('accS', 1)])
                            tiles[-1]['post_pv'].append(imp_chain)

                            wt = [(rp, dk) for dk in range(2) for rp in range(4) if k - 1 + dk >= 0]
                            for idx, (rp, dk) in enumerate(wt):
                                slot = rp * 16 + k - 1 + dk

                                def qk_w(b, slot=slot, rp=rp, dk=dk):
                                    T(lambda e: e.matmul(psb[b][:, :], lhsT=Kwin[:, slot, :], rhs=Qaug[g][:, 2, :], start=True, stop=False),
                                      [qk, ('Qz', g), 'Kwin'], [('ps', b)])
                                    T(lambda e: e.matmul(psb[b][:, :].rearrange("p (h q) -> p h q", h=4), lhsT=identB[:],
                                                         rhs=bc(WMB[:, rp * 2 + dk, :].unsqueeze(1), [128, 4, 128]), start=False, stop=True), [], [('ps', b)])

                                def exp_f(b, pi):
                                    A(lambda e: e.activation(out=Pt[pi][:], in_=psb[b][:, :], func=AF.Exp, scale=SCALE), [('ps', b)], [('Pt', pi)])

                                def pv_w(pi, slot=slot, last=(idx == len(wt) - 1)):
                                    for h in range(4):
                                        T(lambda e, h=h: e.matmul(psb[3][:, h * 65:(h + 1) * 65], lhsT=Pt[pi][:, h * 128:(h + 1) * 128], rhs=Vw[:, slot, g, :],
                                                                  start=False, stop=last), [('Pt', pi), 'Vw'], [('ps', 3)])
                                t = dict(qk=qk_w, exp=exp_f, pv=pv_w, pre_qk=[], pre_pv=[], post_pv=[])
                                if idx == 0:
                                    t['pre_pv'].append(zero_acc([(3, 260)]))
                                    firstwin.append(t)
                                tiles.append(t)

                            def mask_to_q():
                                if g == 0:
                                    for ver, (w0, w1) in enumerate([(0, 128), (64, 192)]):
                                        T(lambda e, ver=ver, w0=w0, w1=w1: e.transpose(out=psT[:, ver * 128:(ver + 1) * 128], in_=NBt[:, w0:w1], identity=identB[:]),
                                          ['NBa'], ['psT'])
                                else:
                                    for ver, (w0, w1) in enumerate([(64, 128), (128, 192)]):
                                        T(lambda e, ver=ver, w0=w0, w1=w1: e.transpose(out=psT[0:64, ver * 128:(ver + 1) * 128], in_=NBt[:, w0:w1], identity=identB[:]),
                                          ['NBa'], ['psT'])
                                for ver in range(2):
                                    V(lambda e, ver=ver: e.tensor_copy(out=Qaug[g][mr, ver, :].rearrange("p (h q) -> p h q", h=4),
                                                                       in_=bc(psT[mr, ver * 128:(ver + 1) * 128].unsqueeze(1), [64, 4, 128])),
                                      ['psT'], [('Qm', g, ver)])
                            st_ = [(rp, kp) for kp in range(k + 1) for rp in range(4)]
                            for idx, (rp, kp) in enumerate(st_):
                                slot = rp * 16 + kp
                                ver = 0 if rp < 2 else 1
                                diag = (kp == k)

                                def qk_s(b, slot=slot, ver=ver, diag=diag, rp=rp):
                                    T(lambda e: e.matmul(psb[b][:, :], lhsT=Kaug[g][:, slot, :], rhs=Qaug[g][:, ver, :], start=True, stop=(not diag)),
                                      [qk, ('Qm', g, ver)] + KaugK[g], [('ps', b)])
                                    if diag:
                                        T(lambda e: e.matmul(psb[b][:, :].rearrange("p (h q) -> p h q", h=4), lhsT=identB[:],
                                                             rhs=bc(DMB[:, rp, :].unsqueeze(1), [128, 4, 128]), start=False, stop=True), [], [('ps', b)])

                                def pv_s(pi, slot=slot, last=(idx == len(st_) - 1)):
                                    for h in range(4):
                                        T(lambda e, h=h: e.matmul(psb[2][:, h * 65:(h + 1) * 65], lhsT=Pt[pi][:, h * 128:(h + 1) * 128], rhs=Vs[:, slot, g, :],
                                                                  start=False, stop=last), [('Pt', pi), 'Vs'], [('ps', 2)])
                                t = dict(qk=qk_s, exp=exp_f, pv=pv_s, pre_qk=[], pre_pv=[], post_pv=[])
                                if idx == 0:
                                    t['pre_qk'].append(mask_to_q)
                                    t['pre_pv'].append(zero_acc([(2, 260)]))
                                tiles.append(t)

                            def combine():
                                V(lambda e: e.tensor_copy(out=accS[:, 772:1032], in_=psb[2][:, 0:260]), [('ps', 2)], [('accS', 2)])
                                V(lambda e: e.tensor_copy(out=accS[:, 1032:1292], in_=psb[3][:, 0:260]), [('ps', 3)], [('accS', 3)])
                                ocv = accS[:, 0:772].rearrange("p (h e) -> p h e", h=4)
                                osv = accS[:, 772:1032].rearrange("p (h e) -> p h e", h=4)
                                owv = accS[:, 1032:1292].rearrange("p (h e) -> p h e", h=4)
                                V(lambda e: e.tensor_scalar(out=rsA[:, 4:8], in0=osv[:, :, 64], scalar1=1e-30, scalar2=None, op0=ALU.max), [('accS', 2)], [('rsA', 2)])
                                V(lambda e: e.tensor_scalar(out=rsA[:, 8:12], in0=owv[:, :, 64], scalar1=1e-30, scalar2=None, op0=ALU.max), [('accS', 3)], [('rsA', 3)])
                                V(lambda e: e.reciprocal(out=riA[:, 4:12], in_=rsA[:, 4:12]), [('rsA', 2), ('rsA', 3)], ['riAs'])
                                V(lambda e: e.tensor_tensor(out=fA[:, :].rearrange("p (c h) -> p c h", c=3), in0=riA[:, :].rearrange("p (c h) -> p c h", c=3),
                                                            in1=GATES[:, k, g * 12:(g + 1) * 12].rearrange("p (h c) -> p c h", c=3), op=ALU.mult),
                                  ['riAc', 'riAs'], ['fA'])
                                V(lambda e: e.tensor_tensor(out=oacc[:], in0=ocv[:, :, 0:64], in1=bc(fA[:, 0:4].unsqueeze(2), [128, 4, 64]), op=ALU.mult),
                                  [('accS', 0), ('accS', 1), 'fA'], ['oacc'])
                                G(lambda e: e.tensor_tensor(out=t2[:], in0=osv[:, :, 0:64], in1=bc(fA[:, 4:8].unsqueeze(2), [128, 4, 64]), op=ALU.mult),
                                  [('accS', 2), 'fA'], ['t2'])
                                G(lambda e: e.tensor_tensor(out=t3[:], in0=owv[:, :, 0:64], in1=bc(fA[:, 8:12].unsqueeze(2), [128, 4, 64]), op=ALU.mult),
                                  [('accS', 3), 'fA'], ['t3'])
                                G(lambda e: e.tensor_tensor(out=oacc[:], in0=oacc[:], in1=t2[:], op=ALU.add), ['oacc', 't2'], ['oacc'])
                                G(lambda e: e.tensor_tensor(out=oacc[:], in0=oacc[:], in1=t3[:], op=ALU.add), ['oacc', 't3'], ['oacc'])
                                G(lambda e: e.tensor_tensor(out=mixn[:, g * 256:(g + 1) * 256], in0=oacc[:].rearrange("p h d -> p (h d)"),
                                                            in1=SZ[:, k, g * 256:(g + 1) * 256], op=ALU.mult), ['oacc'], [('mixn', g)])

                            def bo_a():
                                for c in range(8):
                                    if c < 2:
                                        src, rk_ = YGM[:, k, c * 128:(c + 1) * 128], []
                                    elif c < 6:
                                        src, rk_ = mixn[:, (c - 2) * 128:(c - 1) * 128], [('mixn', (c - 2) // 2)]
                                    else:
                                        src, rk_ = YGM[:, k, 256 + (c - 6) * 128:256 + (c - 5) * 128], []
                                    T(lambda e, c=c, src=src: e.transpose(out=psT[:, c * 128:(c + 1) * 128], in_=src, identity=identB[:]), rk_, ['psT'])
                                V(lambda e: e.tensor_copy(out=mixT[:].rearrange("p c t -> p (c t)"), in_=psT[:, :]), ['psT'], ['mixT'])

                            def bo_b(j):
                                def f():
                                    half = j // 4
                                    for c in (2 * (j % 4), 2 * (j % 4) + 1):
                                        T(lambda e, half=half, c=c: e.matmul(psb[half][:, :], lhsT=mixT[:, c, :], rhs=Wout[:, c, half * 512:(half + 1) * 512],
                                                                             start=(c == 0), stop=(c == 7)), ['mixT'], [('ps', half)])
                                return f

                            def bo_c():
                                yb = [0, 1]
                                for half in range(2):
                                    hs = slice(half * 512, (half + 1) * 512)
                                    V(lambda e, half=half, hs=hs: e.scalar_tensor_tensor(out=zb[:, hs], in0=xblk[:, hs], scalar=ALPHA, in1=psb[yb[half]][:, :],
                                                                                         op0=ALU.mult, op1=ALU.add), [('ps', yb[half]), 'xblk'], [('zb', half)])
                                    V(lambda e, half=half, hs=hs: e.bn_stats(out=stt[:, half * 6:(half + 1) * 6], in_=zb[:, hs]), [('zb', half)], [('stt', half)])
                                V(lambda e: e.bn_aggr(out=mv[:, 0:2], in_=stt[:, :]), [('stt', 0), ('stt', 1)], ['mv'])
                                V(lambda e: e.tensor_scalar(out=mv[:, 2:3], in0=mv[:, 1:2], scalar1=LN_EPS, scalar2=None, op0=ALU.add), ['mv'], ['mv2'])
                                G(lambda e: e.tensor_tensor(out=mv[:, 3:4], in0=mv[:, 2:3], in1=mhalfB[:, 0:1], op=ALU.pow), ['mv2', 'mhalfB'], ['mv3'])
                                V(lambda e: e.tensor_scalar(out=zb[:], in0=zb[:], scalar1=mv[:, 0:1], scalar2=mv[:, 3:4], op0=ALU.subtract, op1=ALU.mult),
                                  [('zb', 0), ('zb', 1), 'mv', 'mv3'], [('zb', 0), ('zb', 1)])
                                V(lambda e: e.tensor_tensor(out=zb[:], in0=zb[:], in1=lngt[:], op=ALU.mult), [('zb', 0), ('zb', 1), 'lngt'], [('zb', 0), ('zb', 1)])
                                V(lambda e: e.tensor_tensor(out=zb[:], in0=zb[:], in1=lnbt[:], op=ALU.add), [('zb', 0), ('zb', 1), 'lnbt'], [('zb', 0), ('zb', 1)])
                                DMA(lambda e: e.dma_start(out=xdst[k * 128:(k + 1) * 128, :], in_=zb[:]), [('zb', 0), ('zb', 1)], [('xdst', k)])
                                if k + 1 < NK:
                                    DMA(lambda e: e.dma_start(out=xblk[:], in_=xsrc[(k + 1) * 128:(k + 2) * 128, :]), [], ['xblk'])
                            tiles[-1]['post_pv'].append(combine)
                            if g == 1:
                                L_ = len(tiles) - 1
                                D_ = min(12, 12 + 4 * (k + 2) - 10)
                                deferred.append((L_ + D_, bo_a))
                                for j in range(8):
                                    deferred.append((L_ + D_ + 1 + j, bo_b(j)))
                                deferred.append((L_ + D_ + 9, bo_c))

                        DEFER = 14
                        deferred = []
                        DMA(lambda e: e.dma_start(out=xblk[:], in_=xsrc[0:128, :]), [], ['xblk'])
                        qcopies, firstwin = [], []
                        for k in range(NK):
                            for g in range(2):
                                mk_group(k, g)
                        tiles[0]['pre_qk'].insert(0, qcopies[0])
                        for n_ in range(1, len(qcopies)):
                            firstwin[n_ - 1]['pre_qk'].insert(0, qcopies[n_])
                        tail_hooks = []
                        for ti, fn in deferred:
                            if ti < len(tiles):
                                tiles[ti]['post_pv'].append(fn)
                            else:
                                tail_hooks.append(fn)
                        LOOK = 2
                        nt_ = len(tiles)
                        binfo = {}
                        for idx in range(nt_ + LOOK):
                            if idx < nt_:
                                t = tiles[idx]
                                for f in t['pre_qk']:
                                    f()
                                b = nbank((4, 5, 6))
                                binfo[idx] = b
                                t['qk'](b)
                            i = idx - LOOK
                            if i >= 0:
                                t = tiles[i]
                                pi = i % 4
                                t['exp'](binfo[i], pi)
                                for f in t['pre_pv']:
                                    f()
                                t['pv'](pi)
                                for f in t['post_pv']:
                                    f()
                        for f in tail_hooks:
                            f()
                    except _Stop:
                        pass
                    S.barrier()
                    S.emit()
        S.barrier()
        S.emit()
    return nc


def _bf(a):
    return np.asarray(a, dtype=np.float32).astype(ml_dtypes.bfloat16)


def _consts_common():
    eind = np.zeros((64, 64, 128), np.float32)
    for s in range(64):
        for half in range(2):
            eind[(2 * s + half) % 64, s, half * 64:(half + 1) * 64] = 1.0
    zc = np.zeros((128, 256), np.float32)
    for e in range(16):
        zc[e, e + 120] = 1.0
    ov = np.zeros((128, 4, 128), np.float32)
    for rp in range(4):
        for kp in range(16):
            for m in range(8):
                n = 8 * (4 * kp + rp) + m
                for jb in ([n // 4] + ([n // 4 + 1] if n % 4 == 3 else [])):
                    if jb > 127:
                        continue
                    j2, half = jb // 2, jb % 2
                    beta = 2 * ((j2 % 4) * 16 + j2 // 4) + half
                    ov[8 * kp + m, rp, beta] = 1.0
    return _bf(eind.reshape(64, SEQ)), _bf(zc), _bf(ov)


def _consts_core(r):
    p = np.arange(128)
    bcmp = np.zeros((128, 4, 128), np.float32)
    for dk in range(2):
        for m in range(8):
            for rp in range(4):
                dj = 4 * (dk - 1) + rp - r
                ok = (128 * dj + 16 * m + 31) <= p
                bcmp[dk * 8 + m, rp, :] = np.where(ok, 0.0, -BIG)
    c = np.arange(128)[:, None]
    q = np.arange(128)[None, :]
    dmb = np.zeros((128, 4, 128), np.float32)
    for rp in range(4):
        if rp < r:
            dmb[:, rp, :] = 0.0
        elif rp > r:
            dmb[:, rp, :] = -BIG
        else:
            dmb[:, rp, :] = np.where(c <= q, 0.0, -BIG)
    wmb = np.zeros((128, 4, 2, 128), np.float32)
    for rp in range(4):
        for dk in range(2):
            dj = 4 * (dk - 1) + rp - r
            if dj == 0:
                wmb[:, rp, dk, :] = np.where(c <= q, 0.0, -BIG)
            elif dj in (-1, -2, -3):
                wmb[:, rp, dk, :] = 0.0
            elif dj == -4:
                wmb[:, rp, dk, :] = np.where(c > q, 0.0, -BIG)
            else:
                wmb[:, rp, dk, :] = -BIG
    beta = np.arange(128)
    s_ = beta // 2
    jb = 2 * (4 * (s_ % 16) + s_ // 16) + beta % 2
    vm = np.zeros((128, NK, 1, 128), np.float32)
    for k in range(NK):
        i = 4 * k + r
        tblk = (2 * i + (p >= 64))[:, None]
        valid = jb[None, :] <= tblk
        forced = (jb[None, :] == 0) | (valid & (jb[None, :] > tblk - 2))
        vm[:, k, 0, :] = np.where(forced, 100.0, np.where(valid, 0.0, -100.0))
    return _bf(bcmp), _bf(dmb), _bf(wmb.reshape(128, 8, 128)), _bf(vm)


def _rope_tables(r):
    blocks = 4 * np.arange(NK) + r
    pos = (blocks[:, None] * 128 + np.arange(128)[None, :]).reshape(-1).astype(np.float32)
    inv = (np.float32(10000.0) ** (-np.arange(32, dtype=np.float32) * np.float32(2.0) / np.float32(64))).astype(np.float32)
    ang = (pos[None, :] * inv[:, None]).astype(np.float32)
    cos = np.cos(ang).astype(np.float32)
    sin = np.sin(ang).astype(np.float32)
    C = np.concatenate([cos, cos, cos, cos], axis=0)
    Sg = np.concatenate([-sin, sin, -sin, sin], axis=0)
    return np.ascontiguousarray(C), np.ascontiguousarray(Sg)


def make_in_maps(inputs):
    f = lambda k: np.asarray(inputs[k], dtype=np.float32)
    x, mem = f("x"), f("mem")
    w_in = f("w_in")
    allcols = np.concatenate([g[2] for g in COLG])
    shared = {
        "w_in_p": np.ascontiguousarray(w_in[:, :, allcols]),
        "gm_ln_g": f("gm_ln_g").reshape(DEPTH, 1, 256),
        "gm_ln_b": f("gm_ln_b").reshape(DEPTH, 1, 256),
        "gm_ws": f("gm_ws"),
        "gm_bsT": np.ascontiguousarray(f("gm_bs").transpose(0, 2, 1)),
        "cmp_pos_kT": np.ascontiguousarray(f("cmp_pos_k").transpose(0, 2, 1)),
        "cmp_pos_vT": np.ascontiguousarray(f("cmp_pos_v").transpose(0, 2, 1)),
        "cmp_k_w1": f("cmp_k_w1"), "cmp_v_w1": f("cmp_v_w1"),
        "cmp_k_w2": f("cmp_k_w2"), "cmp_v_w2": f("cmp_v_w2"),
        "w_mem_kv": f("w_mem_kv"), "w_out": f("w_out"),
        "ln_g": f("ln_g").reshape(DEPTH, 1, D), "ln_b": f("ln_b").reshape(DEPTH, 1, D),
    }
    eind, zc, ov = _consts_common()
    pm = np.zeros((128, 128), np.float32)
    for pq in range(128):
        pm[(pq // 64) * 64 + (pq % 64 + 32) % 64, pq] = 1.0
    shared.update({"eind": eind, "zc": zc, "ovc": ov, "pswap": _bf(pm)})
    maps = []
    for c in range(8):
        b, r = c // 4, c % 4
        blocks = 4 * np.arange(NK) + r
        m = dict(shared)
        m["x_own"] = np.ascontiguousarray(x[b].reshape(64, 128, D)[blocks].reshape(NT, D))
        m["mem_b"] = np.ascontiguousarray(mem[b])
        C, Sg = _rope_tables(r)
        m["ropeC"], m["ropeS"] = C, Sg
        bcmp, dmb, wmb, vm = _consts_core(r)
        m.update({"bcmp": bcmp, "dmb": dmb, "wmb": wmb, "vmc": vm})
        maps.append(m)
    return maps


def assemble(results):
    out = np.zeros((NB, SEQ, D), np.float32)
    for c in range(8):
        b, r = c // 4, c % 4
        blocks = 4 * np.arange(NK) + r
        o = np.asarray(results[c]["out"], dtype=np.float32).reshape(NK, 128, D)
        out[b].reshape(64, 128, D)[blocks] = o
    return out


_NC_CACHE = {}


def kernel(**inputs):
    if "nc" not in _NC_CACHE:
        _NC_CACHE["nc"] = build()
    nc = _NC_CACHE["nc"]
    maps = make_in_maps(inputs)
    res = run_bass_kernel_spmd(nc, maps, core_ids=list(range(8)))
    return assemble(res.results)
```

```python
import os
import numpy as np
import ml_dtypes
from contextlib import ExitStack
import concourse.bass as bass
import concourse.mybir as mybir
from concourse.bass_utils import run_bass_kernel_spmd

F32 = mybir.dt.float32
BF16 = mybir.dt.bfloat16
AF = mybir.ActivationFunctionType
ALU = mybir.AluOpType
AX = mybir.AxisListType

D = 1024
SEQ = 8192
NB = 2
DEPTH = 2
NK = 16
NT = NK * 128
ALPHA = (2.0 * DEPTH) ** 0.25
LN_EPS = 1e-5
SCALE = 0.125
BIG = 30000.0
CCW = 12352
O_KC, O_VC, O_KS, O_KW, O_VS, O_VW = 0, 2048, 4096, 6144, 8192, 10272

C_U, C_V, C_Z, C_Q, C_KC, C_VC, C_KS, C_VS, C_KW, C_VW, C_G, C_NZ, C_MQ, C_MZ = (
    0, 256, 512, 768, 1280, 1408, 1536, 1664, 1792, 1920, 2048, 2072, 2584, 2840)


def _swap64(cols):
    cols = np.asarray(cols).reshape(-1, 64)
    return np.concatenate([cols[:, 32:], cols[:, :32]], axis=1).reshape(-1)


def _col_groups():
    ar = np.arange
    groups = []
    for nm, c0 in (("kc", C_KC), ("ks", C_KS), ("kw", C_KW)):
        p = ar(c0, c0 + 128)
        groups.append((nm, "F", p))
    groups.append(("vg", "T", np.concatenate([ar(C_VS, C_VS + 128), ar(C_VW, C_VW + 128)] + [ar(C_G, C_G + 24)] * 5 + [ar(C_G, C_G + 8)])))
    groups.append(("vcmq", "F", np.concatenate([ar(C_VC, C_VC + 128), ar(C_MQ, C_MQ + 256)])))
    groups.append(("uv", "T", np.concatenate([ar(C_U, C_U + 256), ar(C_V, C_V + 256)])))
    groups.append(("zz", "T", np.concatenate([ar(C_Z, C_Z + 256), ar(C_MZ, C_MZ + 256)])))
    groups.append(("nz", "T", ar(C_NZ, C_NZ + 512)))
    for hg in range(4):
        p = np.concatenate([ar(C_Q + hg * 64, C_Q + hg * 64 + 64), ar(C_Q + (4 + hg) * 64, C_Q + (4 + hg) * 64 + 64)])
        groups.append((f"q{hg}", "F", p))
    return groups


COLG = _col_groups()
WPC = int(sum(len(g[2]) for g in COLG))


class Sched:
    ENG = ['tensor', 'vector', 'scalar', 'gpsimd', 'sync']
    EPOCH = 30000
    NEPOCH = 4
    NDS = 8

    def __init__(self, nc, es):
        self.nc = nc
        self.es = es
        self.ops = {e: [] for e in self.ENG}
        self.cnt = {e: 0 for e in self.ENG}
        self.sems = {}
        for e in self.ENG:
            for ep in range(self.NEPOCH):
                self.sems[(e, 'e', ep)] = es.enter_context(nc.semaphore(f"s_{e}_{ep}"))
        self.dq = {}
        for q in ['sync', 'gpsimd']:
            self.dq[q] = {'i': 0, 'use': [0] * self.NDS}
            for k in range(self.NDS):
                self.sems[(q, 'd', k)] = es.enter_context(nc.semaphore(f"d_{q}_{k}"))
        self.ncc = 0
        self.waited = {e: {} for e in self.ENG}
        self.lastw = {}
        self.readers = {}

    def op(self, eng, fn, reads=(), writes=(), dma=False):
        deps = []
        for r in reads:
            if r in self.lastw:
                deps.append(self.lastw[r])
        for w in writes:
            if w in self.lastw:
                deps.append(self.lastw[w])
            deps.extend(self.readers.get(w, []))
        if dma == 'cc':
            self.ncc += 1
            sk = ('cc', 'c', self.ncc)
            self.sems[sk] = self.es.enter_context(self.nc.semaphore(f"cc_{self.ncc}"))
            tok = (sk, 1)
        elif dma:
            q = self.dq[eng]
            k = q['i'] % self.NDS
            q['i'] += 1
            q['use'][k] += 1
            sk = (eng, 'd', k)
            val = 16 * q['use'][k]
            if q['use'][k] > 1:
                deps.append((sk, val - 16))
            tok = (sk, val)
        else:
            self.cnt[eng] += 1
            ep = (self.cnt[eng] - 1) // self.EPOCH
            sk = (eng, 'e', ep)
            tok = (sk, self.cnt[eng] - ep * self.EPOCH)
        need = {}
        for (dk, v) in deps:
            if eng == 'tensor' and dk[0] == 'tensor' and dk[1] == 'e':
                continue
            if self.waited[eng].get(dk, 0) >= v:
                continue
            need[dk] = max(need.get(dk, 0), v)
        for dk, v in need.items():
            self.waited[eng][dk] = v
        self.ops[eng].append((list(need.items()), fn, tok))
        for w in writes:
            self.lastw[w] = tok
            self.readers[w] = []
        for r in reads:
            self.readers.setdefault(r, []).append(tok)
        return tok

    def barrier(self):
        toks = set()
        for t in self.lastw.values():
            toks.add(t)
        for rl in self.readers.values():
            toks.update(rl)
        best = {}
        for (dk, v) in toks:
            best[dk] = max(best.get(dk, 0), v)
        for e in self.ENG:
            need = []
            for dk, v in best.items():
                if self.waited[e].get(dk, 0) >= v:
                    continue
                self.waited[e][dk] = v
                need.append((dk, v))
            if need:
                self.ops[e].append((need, None, None))
        self.lastw = {}
        self.readers = {}

    def emit(self):
        with self.nc.Block() as block:
            for e in self.ENG:
                ops = self.ops[e]
                if not ops:
                    continue

                def body(engobj, ops=ops):
                    for waits, fn, tok in ops:
                        for dk, v in waits:
                            engobj.wait_ge(self.sems[dk], v)
                        if fn is None:
                            continue
                        ins = fn(engobj)
                        if tok[0][1] == 'c':
                            ins.then_inc(self.sems[tok[0]])
                        else:
                            ins.then_inc(self.sems[tok[0]], 16 if tok[0][1] == 'd' else 1)
                getattr(block, e)(body)
        self.ops = {e: [] for e in self.ENG}


class _Stop(Exception):
    pass


def _kstop(n):
    import os
    if int(os.environ.get('KSTOP', '99')) <= n:
        raise _Stop()


KBK = int(os.environ.get('KBK', '0'))
KBG = int(os.environ.get('KBG', '0'))


def bc(ap, shape):
    return ap.to_broadcast(list(shape))


def build(nlayers=DEPTH, stop_after=None, debug=False):
    nc = bass.Bass("TRN2", target_bir_lowering=False)
    dt_in = lambda n, s, d=F32: nc.dram_tensor(n, list(s), d, kind="ExternalInput").ap()
    x_in = dt_in("x_own", [NT, D])
    mem_in = dt_in("mem_b", [256, D])
    wp_in = dt_in("w_in_p", [DEPTH, D, WPC])
    gm_ln_g = dt_in("gm_ln_g", [DEPTH, 1, 256])
    gm_ln_b = dt_in("gm_ln_b", [DEPTH, 1, 256])
    gm_ws = dt_in("gm_ws", [DEPTH, 4, 128, 128])
    gm_bsT = dt_in("gm_bsT", [DEPTH, 128, 4])
    posk_in = dt_in("cmp_pos_kT", [DEPTH, 64, 32])
    posv_in = dt_in("cmp_pos_vT", [DEPTH, 64, 32])
    w1k_in = dt_in("cmp_k_w1", [DEPTH, 2048, 128])
    w1v_in = dt_in("cmp_v_w1", [DEPTH, 2048, 128])
    w2k_in = dt_in("cmp_k_w2", [DEPTH, 128, 64])
    w2v_in = dt_in("cmp_v_w2", [DEPTH, 128, 64])
    wmem_in = dt_in("w_mem_kv", [DEPTH, D, 512])
    wout_in = dt_in("w_out", [DEPTH, D, D])
    lng_in = dt_in("ln_g", [DEPTH, 1, D])
    lnb_in = dt_in("ln_b", [DEPTH, 1, D])
    ropeC_in = dt_in("ropeC", [128, NT])
    ropeS_in = dt_in("ropeS", [128, NT])
    eind_in = dt_in("eind", [64, SEQ], BF16)
    zc_in = dt_in("zc", [128, 256], BF16)
    ov_in = dt_in("ovc", [128, 4, 128], BF16)
    bcmp_in = dt_in("bcmp", [128, 4, 128], BF16)
    dmb_in = dt_in("dmb", [128, 4, 128], BF16)
    wmb_in = dt_in("wmb", [128, 8, 128], BF16)
    pm_in = dt_in("pswap", [128, 128], BF16)
    vm_in = dt_in("vmc", [128, NK, 1, 128], BF16)
    out_d = nc.dram_tensor("out", [NT, D], F32, kind="ExternalOutput").ap()
    xmid = nc.dram_tensor("xmid", [NT, D], F32)
    CCN = [4096, 4096, 2080, 2080]
    cc_src = [nc.dram_tensor(f"cc_src{i}", [128, CCN[i]], BF16) for i in range(4)]
    cc_dst = [nc.dram_tensor(f"cc_dst{i}", [512, CCN[i]], BF16) for i in range(4)]
    groups = [[0, 1, 2, 3], [4, 5, 6, 7]]
    dbg = {}
    if debug:
        dbg["qt"] = nc.dram_tensor("dbg_qt", [128, NK * 512], BF16, kind="ExternalOutput").ap()
        for bi, n_ in enumerate([4096, 4096, 2080, 2080]):
            dbg[f"cc{bi}"] = nc.dram_tensor(f"dbg_cc{bi}", [512, n_], BF16, kind="ExternalOutput").ap()
        dbg["ygm"] = nc.dram_tensor("dbg_ygm", [128, NK * 512], BF16, kind="ExternalOutput").ap()
        dbg["sz"] = nc.dram_tensor("dbg_sz", [128, NK * 512], BF16, kind="ExternalOutput").ap()
        dbg["gates"] = nc.dram_tensor("dbg_gates", [128, NK * 24], F32, kind="ExternalOutput").ap()
        dbg["kcmp"] = nc.dram_tensor("dbg_kcmp", [128, 512], BF16, kind="ExternalOutput").ap()
        dbg["rhsc"] = nc.dram_tensor("dbg_rhsc", [128, 8 * 193], BF16, kind="ExternalOutput").ap()
        dbg["mix"] = nc.dram_tensor("dbg_mix", [128, NK * 512], BF16, kind="ExternalOutput").ap()
        dbg["acc"] = nc.dram_tensor("dbg_acc", [128, NK * 2 * 1292], F32, kind="ExternalOutput").ap()
        dbg["nb"] = nc.dram_tensor("dbg_nb", [128, NK * 2 * 128], BF16, kind="ExternalOutput").ap()

    with ExitStack() as es:
        S = Sched(nc, es)
        sbP = lambda n, s, d: es.enter_context(nc.sbuf_tensor(n, list(s), d))
        psb = [es.enter_context(nc.psum_tensor(f"psb{i}", [128, 512], F32)) for i in range(7)]
        psT = es.enter_context(nc.psum_tensor("psT", [128, 1024], BF16))
        rr = {'i': 0}

        def nbank(pool=(4, 5, 6)):
            b = pool[rr['i'] % len(pool)]
            rr['i'] += 1
            return b

        QT = sbP("QT", [128, NK, 4, 128], BF16)
        YGM = sbP("YGM", [128, NK, 512], BF16)
        SZ = sbP("SZ", [128, NK, 512], BF16)
        GATES = sbP("GATES", [128, NK, 24], F32)
        identF = sbP("identF", [128, 128], F32)
        identB = sbP("identB", [128, 128], BF16)
        ZC = sbP("ZC", [128, 256], BF16)
        BCMP = sbP("BCMP", [128, 4, 128], BF16)
        DMB = sbP("DMB", [128, 4, 128], BF16)
        WMB = sbP("WMB", [128, 8, 128], BF16)

        V = lambda fn, r, w: S.op('vector', fn, reads=r, writes=w)
        A = lambda fn, r, w: S.op('scalar', fn, reads=r, writes=w)
        G = lambda fn, r, w: S.op('gpsimd', fn, reads=r, writes=w)
        T = lambda fn, r, w: S.op('tensor', fn, reads=r, writes=w)
        DMA = lambda fn, r, w, q='sync': S.op(q, fn, reads=r, writes=w, dma=True)

        G(lambda e: e.memset(identF[:], 0.0), [], ['identF'])
        G(lambda e: e.affine_select(out=identF[:], in_=identF[:], pattern=[[-1, 128]], compare_op=ALU.not_equal,
                                    fill=1.0, base=0, channel_multiplier=1), ['identF'], ['identF'])
        G(lambda e: e.tensor_copy(out=identB[:], in_=identF[:]), ['identF'], ['identB'])
        PMB = sbP("PMB", [128, 128], BF16)
        DMA(lambda e: e.dma_start(out=PMB[:], in_=pm_in[:, :]), [], ['PMB'])
        DMA(lambda e: e.dma_start(out=ZC[:], in_=zc_in[:, :]), [], ['ZC'])
        DMA(lambda e: e.dma_start(out=BCMP[:], in_=bcmp_in[:, :, :]), [], ['BCMP'])
        DMA(lambda e: e.dma_start(out=DMB[:], in_=dmb_in[:, :, :]), [], ['DMB'])
        DMA(lambda e: e.dma_start(out=WMB[:], in_=wmb_in[:, :, :]), [], ['WMB'])

        for l in range(nlayers):
            xsrc = x_in if l == 0 else xmid.ap()
            xdst = out_d if l == nlayers - 1 else xmid.ap()
            with ExitStack() as pa:
                sb = lambda n, s, d: pa.enter_context(nc.sbuf_tensor(f"{n}_A{l}", list(s), d))
                xT = sb("xT", [128, 8, NT], BF16)
                memT = sb("memT", [128, 8, 256], BF16)
                wst = sb("wst", [128, 8, 512], F32)
                wbf = [sb(f"wbf{i}", [128, 8, 512], BF16) for i in range(2)]
                ropeC = sb("ropeC", [128, NT], F32)
                ropeS = sb("ropeS", [128, NT], F32)
                xin = [sb(f"xin{i}", [128, D], F32) for i in range(3)]
                memK = sb("memK", [128, 2, 256], BF16)
                memV = sb("memV", [128, 2, 4, 65], BF16)
                mqT = sb("mqT", [128, 2, NT], BF16)
                wcT = sb("wcT", [128, 4, 128], BF16)
                wsf = sb("wsf", [128, 4, 128], F32)
                bsT = sb("bsT", [128, 4], F32)
                glng = sb("glng", [128, 256], F32)
                glnb = sb("glnb", [128, 256], F32)
                kst = [sb(f"kst{i}", [128, NT], BF16) for i in range(2)]
                vst = [sb(f"vst{i}", [128, NK, 2, 65], BF16) for i in range(2)]
                guL = [sb(f"gu{i}", [128, 256], BF16) for i in range(3)]
                gvL = [sb(f"gv{i}", [128, 4, 64], F32) for i in range(3)]
                gcL = [sb(f"gc{i}", [128, 4, 64], F32) for i in range(2)]
                gsqL = [sb(f"gsq{i}", [128, 4, 64], F32) for i in range(2)]
                mhalf = sb("mhalf", [128, 4], F32)
                G(lambda e: e.memset(mhalf[:], -0.5), [], ['mhalf'])
                vlnL = [sb(f"vln{i}", [128, 256], BF16) for i in range(2)]
                st4L = [sb(f"st4{i}", [128, 16], F32) for i in range(2)]
                pbf = [sb(f"pbf{i}", [128, 512], BF16) for i in range(2)]
                tA = [sb(f"tA{i}", [128, 512], F32) for i in range(2)]
                tB = [sb(f"tB{i}", [128, 512], F32) for i in range(2)]
                pmL = [sb(f"pm{i}", [128, 2, 2, 2, 128], BF16) for i in range(2)]
                om = sb("om", [128, 4, 64], F32)

                try:
                    DMA(lambda e: e.dma_start(out=ropeC[:], in_=ropeC_in[:, :]), [], ['ropeC'])
                    DMA(lambda e: e.dma_start(out=ropeS[:], in_=ropeS_in[:, :]), [], ['ropeS'])
                    for i in range(2):
                        G(lambda e, i=i: e.memset(vst[i][:, :, :, 64:65], 1.0), [], [('vst1', i)])
                    G(lambda e: e.memset(memV[:, :, :, 64:65], 1.0), [], ['memV1'])

                    cp = 0
                    for kt in range(NK + 2):
                        buf = xin[kt % 3]
                        bk = ('xin', kt % 3)
                        if kt < NK:
                            DMA(lambda e, kt=kt, buf=buf: e.dma_start(out=buf[:], in_=xsrc[kt * 128:(kt + 1) * 128, :]), [], [bk])
                        else:
                            m = kt - NK
                            DMA(lambda e, m=m, buf=buf: e.dma_start(out=buf[:], in_=mem_in[m * 128:(m + 1) * 128, :]), [], [bk])
                        for half in range(2):
                            b = nbank((0, 1, 2, 3))
                            for cc in range(4):
                                c = half * 4 + cc
                                T(lambda e, b=b, cc=cc, c=c, buf=buf: e.transpose(out=psb[b][:, cc * 128:(cc + 1) * 128],
                                                                                   in_=buf[:, c * 128:(c + 1) * 128], identity=identF[:]),
                                  [bk, 'identF'], [('ps', b)])
                            if kt < NK:
                                dst = xT[:, half * 4:(half + 1) * 4, kt * 128:(kt + 1) * 128]
                                dk = ('xT', kt)
                            else:
                                dst = memT[:, half * 4:(half + 1) * 4, (kt - NK) * 128:(kt - NK + 1) * 128]
                                dk = ('memT', kt - NK, half)
                            src = psb[b][:, :].rearrange("p (a b) -> p a b", a=4)
                            if cp % 2 == 0:
                                V(lambda e, dst=dst, src=src: e.tensor_copy(out=dst, in_=src), [('ps', b)], [(dk, half)])
                            else:
                                A(lambda e, dst=dst, src=src: e.copy(out=dst, in_=src), [('ps', b)], [(dk, half)])
                            cp += 1
                    _kstop(1)
                    xT_keys = [(('xT', kt), h) for kt in range(NK) for h in range(2)]
                    memT_keys = [(('memT', m, h), h) for m in range(2) for h in range(2)]

                    wi = {'i': 0}

                    def load_w(src_ap, ncols):
                        i = wi['i'] % 2
                        wi['i'] += 1
                        DMA(lambda e: e.dma_start(out=wst[:, :, 0:ncols], in_=src_ap.rearrange("(c p) n -> p c n", p=128)),
                            [], ['wst'])
                        V(lambda e: e.tensor_copy(out=wbf[i][:, 0:4, 0:ncols], in_=wst[:, 0:4, 0:ncols]), ['wst'], [('wbf', i, 0)])
                        A(lambda e: e.copy(out=wbf[i][:, 4:8, 0:ncols], in_=wst[:, 4:8, 0:ncols]), ['wst'], [('wbf', i, 1)])
                        return wbf[i], [('wbf', i, 0), ('wbf', i, 1)]

                    wm, wmk = load_w(wmem_in[l], 512)
                    for pr in range(2):
                        b = nbank((0, 1, 2, 3))
                        for c in range(8):
                            T(lambda e, b=b, c=c, pr=pr: e.matmul(psb[b][:, 0:256], lhsT=wm[:, c, pr * 128:(pr + 1) * 128],
                                                                 rhs=memT[:, c, :], start=(c == 0), stop=(c == 7)),
                              wmk + memT_keys, [('ps', b)])
                        V(lambda e, b=b, pr=pr: e.tensor_copy(out=memK[:, pr, :], in_=psb[b][:, 0:256]), [('ps', b)], [('memK', pr)])
                    for mt in range(2):
                        b = nbank((0, 1, 2, 3))
                        for c in range(8):
                            T(lambda e, b=b, c=c, mt=mt: e.matmul(psb[b][:, 0:256], lhsT=memT[:, c, mt * 128:(mt + 1) * 128],
                                                                 rhs=wm[:, c, 256:512], start=(c == 0), stop=(c == 7)),
                              wmk + memT_keys, [('ps', b)])
                        V(lambda e, b=b, mt=mt: e.tensor_copy(out=memV[:, mt, :, 0:64],
                                                             in_=psb[b][:, 0:256].rearrange("p (h d) -> p h d", h=4)),
                          [('ps', b), 'memV1'], [('memV', mt)])

                    _kstop(2)
                    DMA(lambda e: e.dma_start(out=wsf[:], in_=gm_ws[l].rearrange("g i j -> i g j")), [], ['wsf'])
                    DMA(lambda e: e.dma_start(out=bsT[:], in_=gm_bsT[l]), [], ['bsT'])
                    DMA(lambda e: e.dma_start(out=glng[:], in_=gm_ln_g[l].partition_broadcast(128)), [], ['glng'])
                    DMA(lambda e: e.dma_start(out=glnb[:], in_=gm_ln_b[l].partition_broadcast(128)), [], ['glnb'])
                    for g in range(4):
                        G(lambda e, g=g: e.affine_select(out=wsf[:, g, :], in_=wsf[:, g, :], pattern=[[-1, 128]],
                                                         compare_op=ALU.is_ge, fill=0.0, base=0, channel_multiplier=1),
                          ['wsf'], ['wsf'])
                    b = nbank((0, 1, 2, 3))
                    for g in range(4):
                        T(lambda e, g=g, b=b: e.transpose(out=psb[b][:, g * 128:(g + 1) * 128], in_=wsf[:, g, :], identity=identF[:]),
                          ['wsf', 'identF'], [('ps', b)])
                    V(lambda e, b=b: e.tensor_copy(out=wcT[:], in_=psb[b][:, :].rearrange("p (g i) -> p g i", g=4)), [('ps', b)], ['wcT'])

                    _kstop(3)
                    col0 = 0
                    for gi_, (gname, kind, cols) in enumerate(COLG):
                        _kstop(4 + gi_)
                        ncols = len(cols)
                        w, wk = load_w(wp_in[l][:, col0:col0 + ncols], ncols)
                        col0 += ncols
                        if gname == "uv":
                            uvb = {}

                            def uv_s0(kt):
                                p3 = kt % 3
                                gu_, gv_ = guL[p3], gvL[p3]
                                b = nbank((0, 1, 2, 3))
                                for c in range(8):
                                    T(lambda e, b=b, c=c, w=w, ncols=ncols: e.matmul(psb[b][:, 0:ncols], lhsT=xT[:, c, kt * 128:(kt + 1) * 128], rhs=w[:, c, 0:ncols],
                                                                   start=(c == 0), stop=(c == 7)), wk + [(('xT', kt), 0), (('xT', kt), 1)], [('ps', b)])
                                P = psb[b]
                                pk = ('ps', b)
                                A(lambda e: e.activation(out=gu_[:], in_=P[:, 0:256], func=AF.Gelu), [pk], [('gu', p3)])
                                A(lambda e: e.activation(out=gv_[:].rearrange("p g c -> p (g c)"), in_=P[:, 256:512], func=AF.Gelu), [pk], [('gv', p3)])

                            def uv_s1(kt):
                                pp = kt % 2
                                p3 = kt % 3
                                gv_, gc_, gsq_, st4_ = gvL[p3], gcL[pp], gsqL[pp], st4L[pp]
                                V(lambda e: e.tensor_reduce(out=st4_[:, 0:4], in_=gv_[:], axis=AX.X, op=ALU.add), [('gv', p3)], [('st_sum', pp)])
                                V(lambda e: e.tensor_scalar(out=st4_[:, 4:8], in0=st4_[:, 0:4], scalar1=-1.0 / 64, scalar2=None, op0=ALU.mult),
                                  [('st_sum', pp)], [('st_nm', pp)])
                                V(lambda e: e.tensor_tensor(out=gc_[:], in0=gv_[:], in1=bc(st4_[:, 4:8].unsqueeze(2), [128, 4, 64]), op=ALU.add),
                                  [('gv', p3), ('st_nm', pp)], [('gc', pp)])
                                G(lambda e: e.tensor_tensor(out=gsq_[:], in0=gc_[:], in1=gc_[:], op=ALU.mult), [('gc', pp)], [('gsq', pp)])
                                V(lambda e: e.tensor_reduce(out=st4_[:, 8:12], in_=gsq_[:], axis=AX.X, op=ALU.add), [('gsq', pp)], [('st_ss', pp)])
                                V(lambda e: e.tensor_scalar(out=st4_[:, 8:12], in0=st4_[:, 8:12], scalar1=1.0 / 64, scalar2=LN_EPS,
                                                            op0=ALU.mult, op1=ALU.add), [('st_ss', pp)], [('st_ss', pp)])
                                G(lambda e: e.tensor_tensor(out=st4_[:, 12:16], in0=st4_[:, 8:12], in1=mhalf[:, 0:4], op=ALU.pow), [('st_ss', pp), 'mhalf'], [('st_rs', pp)])

                            def uv_s2(kt):
                                pp = kt % 2
                                gc_, vln_, st4_ = gcL[pp], vlnL[pp], st4L[pp]
                                V(lambda e: e.tensor_tensor(out=gc_[:], in0=gc_[:], in1=bc(st4_[:, 12:16].unsqueeze(2), [128, 4, 64]), op=ALU.mult),
                                  [('gc', pp), ('st_rs', pp)], [('gc', pp)])
                                G(lambda e: e.tensor_tensor(out=gc_[:].rearrange("p g c -> p (g c)"), in0=gc_[:].rearrange("p g c -> p (g c)"),
                                                            in1=glng[:], op=ALU.mult), [('gc', pp), 'glng'], [('gc', pp)])
                                G(lambda e: e.tensor_tensor(out=vln_[:], in0=gc_[:].rearrange("p g c -> p (g c)"), in1=glnb[:], op=ALU.add),
                                  [('gc', pp), 'glnb'], [('vln', pp)])
                                b2 = nbank((4, 5, 6))
                                uvb[kt] = b2
                                for g in range(4):
                                    T(lambda e, g=g: e.matmul(psb[b2][:, g * 64:(g + 1) * 64], lhsT=wcT[:, g, :], rhs=vln_[:, g * 64:(g + 1) * 64],
                                                              start=True, stop=True), [('vln', pp), 'wcT'], [('ps', b2)])

                            def uv_s3(kt):
                                pp = kt % 2
                                p3 = kt % 3
                                gu_, gsq_ = guL[p3], gsqL[pp]
                                b2 = uvb[kt]
                                V(lambda e: e.tensor_tensor(out=gsq_[:], in0=psb[b2][:, 0:256].rearrange("p (g c) -> p g c", g=4),
                                                            in1=bc(bsT[:, :].unsqueeze(2), [128, 4, 64]), op=ALU.add), [('ps', b2), 'bsT'], [('gsq', pp)])
                                V(lambda e: e.tensor_tensor(out=YGM[:, kt, 0:256], in0=gsq_[:].rearrange("p g c -> p (g c)"), in1=gu_[:], op=ALU.mult),
                                  [('gsq', pp), ('gu', p3)], [('YGMa', kt)])
                            for step in range(NK + 2):
                                if step < NK:
                                    uv_s0(step)
                                if 0 <= step - 2 < NK:
                                    uv_s3(step - 2)
                                if 0 <= step - 1 < NK:
                                    uv_s2(step - 1)
                                if step < NK:
                                    uv_s1(step)
                        elif kind == "T":
                            for kt in range(NK):
                                b = nbank((0, 1, 2, 3))
                                for c in range(8):
                                    T(lambda e, b=b, c=c, kt=kt, w=w, ncols=ncols: e.matmul(
                                        psb[b][:, 0:ncols], lhsT=xT[:, c, kt * 128:(kt + 1) * 128], rhs=w[:, c, 0:ncols],
                                        start=(c == 0), stop=(c == 7)),
                                      wk + [(('xT', kt), 0), (('xT', kt), 1)], [('ps', b)])
                                P = psb[b]
                                pk = ('ps', b)
                                if gname == "uv":
                                    pass
                                elif gname == "zz":
                                    tb = tA[kt % 2]
                                    A(lambda e, P=P, tb=tb: e.activation(out=tb[:, 0:256], in_=P[:, 0:256], func=AF.Silu), [pk], [('tA', kt % 2)])
                                    A(lambda e, P=P, kt=kt: e.activation(out=YGM[:, kt, 256:512], in_=P[:, 256:512], func=AF.Silu), [pk], [('YGMb', kt)])
                                    G(lambda e, kt=kt, tb=tb: e.tensor_tensor(out=YGM[:, kt, 0:256], in0=YGM[:, kt, 0:256], in1=tb[:, 0:256], op=ALU.mult),
                                      [('tA', kt % 2), ('YGMa', kt)], [('YGMa', kt)])
                                elif gname == "nz":
                                    A(lambda e, P=P, kt=kt: e.activation(out=SZ[:, kt, :], in_=P[:, 0:512], func=AF.Silu), [pk], [('SZ', kt)])
                                elif gname == "vg":
                                    for i in range(2):
                                        V(lambda e, P=P, kt=kt, i=i: e.tensor_copy(out=vst[i][:, kt, :, 0:64],
                                                                                 in_=P[:, i * 128:(i + 1) * 128].rearrange("p (g d) -> p g d", g=2)),
                                          [pk, ('vst1', i)], [('vst', i, kt)])
                                    A(lambda e, P=P, kt=kt: e.activation(out=GATES[:, kt, :], in_=P[:, 256:280], func=AF.Sigmoid),
                                      [pk, ('vst', 0, kt), ('vst', 1, kt)], [('GATES', kt)])
                        else:
                            nch = ncols // 128
                            roped = gname[0] in ("q", "k")
                            for tg in range(4):
                                xk = [(('xT', kt), h) for kt in range(tg * 4, tg * 4 + 4) for h in range(2)]
                                bl = []
                                for ch in range(nch):
                                    b = nbank((0, 1, 2, 3))
                                    bl.append(b)
                                    for c in range(8):
                                        T(lambda e, b=b, c=c, ch=ch, tg=tg, w=w: e.matmul(
                                            psb[b][:, :], lhsT=w[:, c, ch * 128:(ch + 1) * 128], rhs=xT[:, c, tg * 512:(tg + 1) * 512],
                                            start=(c == 0), stop=(c == 7)), wk + xk, [('ps', b)])
                                tsl = slice(tg * 512, (tg + 1) * 512)
                                if roped:
                                    b1 = bl[0]
                                    b2 = nbank((0, 1, 2, 3))
                                    ta, tb = tA[tg % 2], tB[tg % 2]
                                    pb_ = pbf[tg % 2]
                                    A(lambda e, b1=b1, pb_=pb_: e.copy(out=pb_[:], in_=psb[b1][:, :]), [('ps', b1)], [('pbf', tg % 2)])
                                    T(lambda e, b2=b2, pb_=pb_: e.matmul(psb[b2][:, :], lhsT=PMB[:], rhs=pb_[:], start=True, stop=True),
                                      [('pbf', tg % 2), 'PMB'], [('ps', b2)])
                                    V(lambda e, b1=b1, ta=ta, tsl=tsl: e.tensor_tensor(out=ta[:], in0=psb[b1][:, :], in1=ropeC[:, tsl], op=ALU.mult),
                                      [('ps', b1), ('pbf', tg % 2), 'ropeC'], [('tA', tg % 2)])
                                    V(lambda e, b2=b2, tb=tb, tsl=tsl: e.tensor_tensor(out=tb[:], in0=psb[b2][:, :], in1=ropeS[:, tsl], op=ALU.mult),
                                      [('ps', b2), 'ropeS'], [('tB', tg % 2)])
                                    if gname[0] == "q":
                                        hg = int(gname[1])
                                        dst = QT[:, tg * 4:(tg + 1) * 4, hg, :]
                                        dkey = ('QT', tg, hg)
                                        G(lambda e, ta=ta, tb=tb, dst=dst: e.tensor_tensor(out=dst, in0=ta[:].rearrange("p (a b) -> p a b", a=4),
                                                                                          in1=tb[:].rearrange("p (a b) -> p a b", a=4), op=ALU.add),
                                          [('tA', tg % 2), ('tB', tg % 2)], [dkey])
                                    else:
                                        kb_ = {"kc": 0, "ks": 1, "kw": 0}[gname]
                                        G(lambda e, ta=ta, tb=tb, kb_=kb_, tsl=tsl: e.tensor_tensor(out=kst[kb_][:, tsl], in0=ta[:], in1=tb[:], op=ALU.add),
                                          [('tA', tg % 2), ('tB', tg % 2)], [('kst', kb_, tg)])
                                else:
                                    A(lambda e, b=bl[0], tsl=tsl: e.copy(out=kst[1][:, tsl], in_=psb[b][:, :]), [('ps', bl[0])], [('kst', 1, tg)])
                                    V(lambda e, b=bl[1], tsl=tsl: e.tensor_copy(out=mqT[:, 0, tsl], in_=psb[b][:, :]), [('ps', bl[1])], [('mqT', 0, tg)])
                                    V(lambda e, b=bl[2], tsl=tsl: e.tensor_copy(out=mqT[:, 1, tsl], in_=psb[b][:, :]), [('ps', bl[2])], [('mqT', 1, tg)])
                        if gname in ("kc", "ks", "kw", "vcmq"):
                            si, kb_, bi, o = {"kc": (0, 0, 0, 0), "vcmq": (1, 1, 0, 2048), "ks": (2, 1, 1, 0), "kw": (3, 0, 1, 2048)}[gname]
                            DMA(lambda e, kb_=kb_, bi=bi, o=o: e.dma_start(out=cc_src[bi].ap()[:, o:o + NT], in_=kst[kb_][:]),
                                [('kst', kb_, tg) for tg in range(4)], [('cc_src', si)])
                        if gname == "vcmq":
                            for i in range(2):
                                DMA(lambda e, i=i: e.dma_start(out=cc_src[2 + i].ap()[:, :], in_=vst[i][:].rearrange("p k g e -> p (k g e)")),
                                    [('vst', i, kt) for kt in range(NK)], [('cc_src', 4 + i)])
                            if not os.environ.get("KNOCC"):
                                ccr = [[('cc_src', 0), ('cc_src', 1)], [('cc_src', 2), ('cc_src', 3)], [('cc_src', 4)], [('cc_src', 5)]]
                                for bi in range(4):
                                    S.op('gpsimd', lambda e, bi=bi: e.collective_compute("AllGather", ALU.bypass, replica_groups=groups,
                                                                                       ins=[cc_src[bi].ap().opt()], outs=[cc_dst[bi].ap().opt()]),
                                         reads=ccr[bi], writes=[('cc_dst', bi)], dma='cc')
                        if gname == "zz":
                            def mem_qk(kt):
                                tg = kt // 4
                                pm_ = pmL[kt % 2]
                                for half, b in ((0, 4), (1, 5)):
                                    rs = slice(half * 64, half * 64 + 64)
                                    for mt in range(2):
                                        for hh in range(2):
                                            T(lambda e, b=b, mt=mt, hh=hh, rs=rs: e.matmul(
                                                psb[b][:, (mt * 2 + hh) * 128:(mt * 2 + hh + 1) * 128],
                                                lhsT=memK[rs, hh, mt * 128:(mt + 1) * 128],
                                                rhs=mqT[rs, hh, kt * 128:(kt + 1) * 128], start=True, stop=True),
                                              [('memK', 0), ('memK', 1), ('mqT', 0, tg), ('mqT', 1, tg)], [('ps', b)])
                                    A(lambda e, b=b, half=half: e.activation(out=pm_[:, half, :, :, :].rearrange("p m h q -> p (m h q)"),
                                                                            in_=psb[b][:, :], func=AF.Exp, scale=SCALE),
                                      [('ps', b)], [('pm', kt % 2, half)])

                            def mem_pv(kt):
                                pm_ = pmL[kt % 2]
                                b = 6
                                for h in range(4):
                                    for mt in range(2):
                                        T(lambda e, h=h, mt=mt: e.matmul(psb[b][:, h * 65:(h + 1) * 65], lhsT=pm_[:, h % 2, mt, h // 2, :],
                                                                        rhs=memV[:, mt, h, :], start=(mt == 0), stop=(mt == 1)),
                                          [('pm', kt % 2, 0), ('pm', kt % 2, 1), ('memV', 0), ('memV', 1)], [('ps', b)])
                                ov = psb[b][:, 0:260].rearrange("p (h e) -> p h e", h=4)
                                V(lambda e: e.reciprocal(out=st4L[0][:, 0:4], in_=ov[:, :, 64]), [('ps', b)], [('st_sum', 0)])
                                V(lambda e: e.tensor_tensor(out=om[:], in0=ov[:, :, 0:64], in1=bc(st4L[0][:, 0:4].unsqueeze(2), [128, 4, 64]), op=ALU.mult),
                                  [('ps', b), ('st_sum', 0)], ['om'])
                                G(lambda e: e.tensor_tensor(out=YGM[:, kt, 256:512], in0=om[:].rearrange("p h d -> p (h d)"),
                                                            in1=YGM[:, kt, 256:512], op=ALU.mult), ['om', ('YGMb', kt)], [('YGMb', kt)])
                            for kt in range(NK + 1):
                                if kt < NK:
                                    mem_qk(kt)
                                if kt >= 1:
                                    mem_pv(kt - 1)
                    if debug and l == 0:
                        DMA(lambda e: e.dma_start(out=dbg["qt"][:, :], in_=QT[:].rearrange("p k h q -> p (k h q)")),
                            [('QT', tg, hg) for tg in range(4) for hg in range(4)], ['dbg_qt'])
                        DMA(lambda e: e.dma_start(out=dbg["ygm"][:, :], in_=YGM[:].rearrange("p k c -> p (k c)")),
                            [('YGMa', kt) for kt in range(NK)] + [('YGMb', kt) for kt in range(NK)], ['dbg_ygm'])
                        DMA(lambda e: e.dma_start(out=dbg["sz"][:, :], in_=SZ[:].rearrange("p k c -> p (k c)")),
                            [('SZ', kt) for kt in range(NK)], ['dbg_sz'])
                        DMA(lambda e: e.dma_start(out=dbg["gates"][:, :], in_=GATES[:].rearrange("p k c -> p (k c)")),
                            [('GATES', kt) for kt in range(NK)], ['dbg_gates'])
                        for bi in range(4):
                            DMA(lambda e, bi=bi: e.dma_start(out=dbg[f"cc{bi}"][:, :], in_=cc_dst[bi].ap()[:, :]), [('cc_dst', bi)], [f'dbg_cc{bi}'])
                except _Stop:
                    pass
                S.barrier()
                S.emit()
            if stop_after == "A":
                break
            with ExitStack() as lb:
                sbL = lambda n, s, d: lb.enter_context(nc.sbuf_tensor(f"{n}_L{l}", list(s), d))
                KcmpT = sbL("KcmpT", [128, 512], BF16)
                RHSc = sbL("RHSc", [128, 4, 2, 193], BF16)
                Wout = sbL("Wout", [128, 8, D], BF16)
                wos = sbL("wos", [128, 8, 128], F32)
                lngt = sbL("lngt", [128, D], F32)
                lnbt = sbL("lnbt", [128, D], F32)
                with ExitStack() as p0:
                    sb = lambda n, s, d: p0.enter_context(nc.sbuf_tensor(f"{n}_B0{l}", list(s), d))
                    tT = [sb("kcT", [128, 4, NT], BF16), sb("vcT", [128, 4, NT], BF16)]
                    w1s = sb("w1s", [128, 32, 128], F32)
                    w1b = [sb("w1k", [128, 32, 128], BF16), sb("w1v", [128, 32, 128], BF16)]
                    posf = sb("posf", [128, 2, 32], F32)
                    posb = sb("posb", [128, 2, 32], BF16)
                    w2f = sb("w2f", [128, 2, 64], F32)
                    w2kp = sb("w2kp", [128, 2, 128], BF16)
                    w2vb = sb("w2vb", [128, 64], BF16)
                    hid = sb("hid", [128, 4, 512], BF16)
                    pos_ins = [posk_in, posv_in]
                    for t in range(2):
                        for hf in range(2):
                            DMA(lambda e, t=t, hf=hf: e.dma_start(out=posf[hf * 64:(hf + 1) * 64, t, :], in_=pos_ins[t][l]), [], [('posf', t, hf)])
                    V(lambda e: e.tensor_copy(out=posb[:], in_=posf[:]), [('posf', t, hf) for t in range(2) for hf in range(2)], ['posb'])
                    DMA(lambda e: e.dma_start(out=w2f[:, 0, :], in_=w2k_in[l]), [], [('w2f', 0)])
                    DMA(lambda e: e.dma_start(out=w2f[:, 1, :], in_=w2v_in[l]), [], [('w2f', 1)])
                    G(lambda e: e.memset(w2kp[:], 0.0), [], ['w2kp'])
                    G(lambda e: e.tensor_copy(out=w2kp[:, 0, 0:64], in_=w2f[:, 0, :]), ['w2kp', ('w2f', 0)], ['w2kp'])
                    G(lambda e: e.tensor_copy(out=w2kp[:, 1, 64:128], in_=w2f[:, 0, :]), ['w2kp', ('w2f', 0)], ['w2kp'])
                    G(lambda e: e.tensor_copy(out=w2vb[:], in_=w2f[:, 1, :]), [('w2f', 1)], ['w2vb'])
                    w1_ins = [w1k_in, w1v_in]
                    for t in range(2):
                        if t == 1:
                            for t2_ in range(2):
                                DMA(lambda e, t2_=t2_: e.dma_start(out=tT[t2_][:], in_=cc_dst[0].ap()[:, t2_ * 2048:(t2_ + 1) * 2048].rearrange("(r p) c -> p r c", r=4)),
                                    [], [('tT', t2_)])
                        for hf in range(2):
                            DMA(lambda e, t=t, hf=hf: e.dma_start(out=w1s[hf * 64:(hf + 1) * 64, :, :],
                                                                 in_=w1_ins[t][l].rearrange("(l d) m -> d l m", d=64)), [], [('w1s', hf)])
                        V(lambda e, t=t: e.tensor_copy(out=w1b[t][:, 0:16, :], in_=w1s[:, 0:16, :]), [('w1s', 0), ('w1s', 1)], [('w1b', t, 0)])
                        A(lambda e, t=t: e.copy(out=w1b[t][:, 16:32, :], in_=w1s[:, 16:32, :]), [('w1s', 0), ('w1s', 1)], [('w1b', t, 1)])
                    DMA(lambda e: e.dma_start(out=lngt[:], in_=lng_in[l].partition_broadcast(128)), [], ['lngt'])
                    DMA(lambda e: e.dma_start(out=lnbt[:], in_=lnb_in[l].partition_broadcast(128)), [], ['lnbt'])
                    for cs in range(8):
                        DMA(lambda e, cs=cs: e.dma_start(out=wos[:], in_=wout_in[l][:, cs * 128:(cs + 1) * 128].rearrange("(c p) n -> p c n", p=128)),
                            [], ['wos'])
                        if cs % 2 == 0:
                            V(lambda e, cs=cs: e.tensor_copy(out=Wout[:, :, cs * 128:(cs + 1) * 128], in_=wos[:]), ['wos'], [('Wout', cs)])
                        else:
                            A(lambda e, cs=cs: e.copy(out=Wout[:, :, cs * 128:(cs + 1) * 128], in_=wos[:]), ['wos'], [('Wout', cs)])
                    biasS = sb("biasS", [128, 4], F32)
                    for t in range(2):
                        for g in range(2):
                            gr = slice(g * 64, g * 64 + 64)
                            bb = 4 + g
                            for li in range(32):
                                T(lambda e, bb=bb, t=t, gr=gr, li=li: e.matmul(psb[bb][:, t * 8:(t + 1) * 8], lhsT=w1b[t][gr, li, :],
                                                                              rhs=bc(posb[gr, t, li:li + 1], [64, 8]), start=(li == 0), stop=(li == 31)),
                                  [('w1b', t, 0), ('w1b', t, 1), 'posb'], [('ps', bb)])
                            V(lambda e, bb=bb, t=t, g=g: e.tensor_copy(out=biasS[:, t * 2 + g:t * 2 + g + 1], in_=psb[bb][:, t * 8:t * 8 + 1]),
                              [('ps', bb)], [('biasS', t * 2 + g)])
                    for t in range(2):
                        for g in range(2):
                            b = t * 2 + g
                            gr = slice(g * 64, g * 64 + 64)
                            rk = [('w1b', t, 0), ('w1b', t, 1), ('tT', t)]
                            pk = [('ps', b)]
                            pf = psb[b]
                            svm = tT[t][gr, :, :].rearrange("p r (k m l) -> p m r k l", k=16, m=8)
                            for li in range(32):
                                if li < 16:
                                    T(lambda e, pf=pf, svm=svm, li=li, t=t, gr=gr: e.matmul(pf[:, :], lhsT=w1b[t][gr, li, :], rhs=svm[:, :, :, :, li],
                                                                                          start=(li == 0), stop=False), rk, pk)
                                else:
                                    T(lambda e, pf=pf, svm=svm, li=li, t=t, gr=gr: e.matmul(pf[:, 0:448], lhsT=w1b[t][gr, li, :],
                                                                                          rhs=svm[:, 1:8, :, :, li - 16], start=False, stop=False), rk, pk)
                                    T(lambda e, pf=pf, svm=svm, li=li, t=t, gr=gr: e.matmul(pf[:, 448:496], lhsT=w1b[t][gr, li, :],
                                                                                          rhs=svm[:, 0, 1:4, :, li - 16], start=False, stop=False), rk, pk)
                                    T(lambda e, pf=pf, svm=svm, li=li, t=t, gr=gr: e.matmul(pf[:, 496:511], lhsT=w1b[t][gr, li, :],
                                                                                          rhs=svm[:, 0, 0, 1:16, li - 16], start=False, stop=(li == 31)), rk, pk)
                            A(lambda e, b=b: e.activation(out=hid[:, b, :].rearrange("p (rk m) -> p m rk", m=8),
                                                          in_=psb[b][:, :].rearrange("p (m rk) -> p m rk", m=8), func=AF.Gelu, bias=biasS[:, b:b + 1]),
                              pk + [('biasS', b)], [('hid', b)])
                    T(lambda e: e.matmul(psb[4][:, :], lhsT=w2kp[:, 0, :], rhs=hid[:, 0, :], start=True, stop=False), ['w2kp', ('hid', 0)], [('ps', 4)])
                    T(lambda e: e.matmul(psb[4][:, :], lhsT=w2kp[:, 1, :], rhs=hid[:, 1, :], start=False, stop=True), ['w2kp', ('hid', 1)], [('ps', 4)])
                    V(lambda e: e.tensor_copy(out=KcmpT[:], in_=psb[4][:, :]), [('ps', 4)], ['KcmpT'])
                    for g in range(2):
                        for rp in range(4):
                            T(lambda e, g=g, rp=rp: e.matmul(psb[5][:, (rp * 2 + g) * 64:(rp * 2 + g + 1) * 64], lhsT=hid[:, 2 + g, rp * 128:(rp + 1) * 128],
                                                             rhs=w2vb[:], start=True, stop=True), ['w2vb', ('hid', 2 + g)], [('ps', 5)])
                    G(lambda e: e.memset(RHSc[:, :, :, 64:65], 1.0), [], ['RHSc1'])
                    V(lambda e: e.tensor_copy(out=RHSc[:, :, :, 0:64], in_=psb[5][:, :].rearrange("p (r g d) -> p r g d", r=4, g=2)),
                      [('ps', 5), 'RHSc1'], ['RHScV'])
                    for g in range(2):
                        DMA(lambda e, g=g: e.dma_start(out=RHSc[:, :, g, 65:193], in_=ov_in[:, :, :]), [], [('RHScO', g)])
                    if debug and l == 0:
                        DMA(lambda e: e.dma_start(out=dbg["kcmp"][:, :], in_=KcmpT[:]), ['KcmpT'], ['dbg_kcmp'])
                        DMA(lambda e: e.dma_start(out=dbg["rhsc"][:, :], in_=RHSc[:].rearrange("p r g e -> p (r g e)")),
                            ['RHScV', 'RHSc1', ('RHScO', 0), ('RHScO', 1)], ['dbg_rhsc'])
                    S.barrier()
                    S.emit()
                if stop_after == "B0":
                    break
                with ExitStack() as p1:
                    sb = lambda n, s, d: p1.enter_context(nc.sbuf_tensor(f"{n}_B1{l}", list(s), d))
                    Kaug = [sb(f"Kaug{g}", [128, 64, 128], BF16) for g in range(2)]
                    Kwin = sb("Kwin", [128, 64, 128], BF16)
                    Vs = sb("Vs", [128, 64, 2, 65], BF16)
                    Vw = sb("Vw", [128, 64, 2, 65], BF16)
                    Qaug = [sb(f"Qaug{g}", [128, 3, 512], BF16) for g in range(2)]
                    Pt = [sb(f"Pt{i}", [128, 512], BF16) for i in range(4)]
                    vmk = [sb(f"vmk{i}", [128, 1, 128], BF16) for i in range(2)]
                    imp = sb("imp", [128, 128], F32)
                    impm = sb("impm", [128, 128], F32)
                    wkt = sb("wkt", [128, 128], F32)
                    selm = sb("selm", [128, 128], F32)
                    m8 = sb("m8", [128, 16], F32)
                    thr = sb("thr", [128, 1], F32)
                    NBt = sb("NBt", [128, 192], BF16)
                    rsA = sb("rsA", [128, 12], F32)
                    riA = sb("riA", [128, 12], F32)
                    fA = sb("fA", [128, 12], F32)
                    oacc = sb("oacc", [128, 4, 64], F32)
                    t2 = sb("t2", [128, 4, 64], F32)
                    t3 = sb("t3", [128, 4, 64], F32)
                    mixn = sb("mixn", [128, 512], BF16)
                    mixT = sb("mixT", [128, 8, 128], BF16)
                    xblk = sb("xblk", [128, D], F32)
                    zb = sb("zb", [128, D], F32)
                    stt = sb("stt", [128, 12], F32)
                    mv = sb("mv", [128, 4], F32)
                    zeroB = sb("zeroB", [128, 386], BF16)
                    G(lambda e: e.memset(zeroB[:], 0.0), [], ['zeroB'])
                    mhalfB = sb("mhalfB", [128, 1], F32)
                    G(lambda e: e.memset(mhalfB[:], -0.5), [], ['mhalfB'])
                    G(lambda e: e.memset(NBt[:], 0.0), [], ['NBa'])
                    accS = sb("accS", [128, 1292], F32)

                    try:
                        ksrc = cc_dst[1].ap()[:, 0:2048].rearrange("(r p) c -> p r c", r=4)
                        DMA(lambda e: e.dma_start(out=Kaug[0][0:64, :, :].rearrange("p (r k) c -> p r (k c)", r=4), in_=ksrc[0:64]), [], [('Kaug', 0, 'k')])
                        DMA(lambda e: e.dma_start(out=Kaug[1][64:128, :, :].rearrange("p (r k) c -> p r (k c)", r=4), in_=ksrc[64:128]), [], [('Kaug', 1, 'k')])
                        DMA(lambda e: e.dma_start(out=Kaug[0][64:128, :, :].rearrange("p s c -> p (s c)"), in_=eind_in[:, :]), [], [('Kaug', 0, 'e')])
                        DMA(lambda e: e.dma_start(out=Kaug[1][0:64, :, :].rearrange("p s c -> p (s c)"), in_=eind_in[:, :]), [], [('Kaug', 1, 'e')], q='gpsimd')
                        DMA(lambda e: e.dma_start(out=Kwin[:].rearrange("p (r k) c -> p r (k c)", r=4),
                                                  in_=cc_dst[1].ap()[:, 2048:4096].rearrange("(r p) c -> p r c", r=4)), [], ['Kwin'], q='gpsimd')
                        DMA(lambda e: e.dma_start(out=Vs[:].rearrange("p (r k) g e -> p r (k g e)", r=4),
                                                  in_=cc_dst[2].ap()[:, :].rearrange("(r p) c -> p r c", r=4)), [], ['Vs'])
                        DMA(lambda e: e.dma_start(out=Vw[:].rearrange("p (r k) g e -> p r (k g e)", r=4),
                                                  in_=cc_dst[3].ap()[:, :].rearrange("(r p) c -> p r c", r=4)), [], ['Vw'], q='gpsimd')
                        Wk = [('Wout', cs) for cs in range(8)]
                        G(lambda e: e.memset(Qaug[0][64:128, 2, :], 0.0), [], [('Qz', 0)])
                        G(lambda e: e.memset(Qaug[1][0:64, 2, :], 0.0), [], [('Qz', 1)])
                        KaugK = [[('Kaug', g, 'k'), ('Kaug', g, 'e')] for g in range(2)]

                        tiles = []

                        def mk_group(k, g):
                            M = 8 * (k + 1)
                            sk = 128 - 8 * k
                            gr = slice(g * 64, g * 64 + 64)
                            mr = slice(64, 128) if g == 0 else slice(0, 64)
                            qk = ('Qq', g)
                            vk = ('vmk', k % 2)
                            vm_ = vmk[k % 2]
                            ob = [psb[i][:, 0:386].rearrange("p (h e) -> p h e", h=2) for i in range(2)]

                            def group_begin():
                                if g == 0:
                                    DMA(lambda e: e.dma_start(out=vmk[k % 2][:], in_=vm_in[:, k, :, :]), [], [('vmk', k % 2)])

                            def q_copy():
                                V(lambda e: e.tensor_copy(out=Qaug[g][gr, :, :],
                                                          in_=bc(QT[gr, k, :, :].rearrange("p h q -> p (h q)").unsqueeze(1), [64, 3, 512])),
                                  [], [qk])
                            qcopies.append(q_copy)

                            def zero_acc(banks):
                                def f():
                                    for b_, n_ in banks:
                                        T(lambda e, b_=b_, n_=n_: e.matmul(psb[b_][:, 0:n_], lhsT=zeroB[:, 0:128], rhs=zeroB[:, 0:n_], start=True, stop=False),
                                          ['zeroB'], [('ps', b_)])
                                return f

                            for rp in range(4):
                                def qk_c(b, rp=rp):
                                    T(lambda e: e.matmul(psb[b][0:M, :], lhsT=KcmpT[:, rp * 128:rp * 128 + M], rhs=Qaug[g][:, 2, :],
                                                         start=True, stop=False), [qk, ('Qz', g)], [('ps', b)])
                                    T(lambda e: e.matmul(psb[b][0:M, :].rearrange("p (h q) -> p h q", h=4), lhsT=ZC[:, sk:sk + M],
                                                         rhs=bc(BCMP[:, rp, :].unsqueeze(1), [128, 4, 128]), start=False, stop=True), [], [('ps', b)])

                                def exp_c(b, pi):
                                    A(lambda e: e.activation(out=Pt[pi][0:M, :], in_=psb[b][0:M, :], func=AF.Exp, scale=SCALE), [('ps', b)], [('Pt', pi)])

                                def pv_c(pi, rp=rp):
                                    for h in range(4):
                                        bo, co = h // 2, (h % 2) * 193
                                        T(lambda e, h=h, bo=bo, co=co: e.matmul(psb[bo][:, co:co + 193], lhsT=Pt[pi][0:M, h * 128:(h + 1) * 128],
                                                                               rhs=RHSc[0:M, rp, g, :], start=False, stop=(rp == 3)),
                                          [('Pt', pi)], [('ps', bo)])
                                t = dict(qk=qk_c, exp=exp_c, pv=pv_c, pre_qk=[], pre_pv=[], post_pv=[])
                                if rp == 0:
                                    t['pre_qk'].append(group_begin)
                                    t['pre_pv'].append(zero_acc([(0, 386), (1, 386)]))
                                tiles.append(t)

                            def imp_chain():
                                for i in range(2):
                                    V(lambda e, i=i: e.tensor_scalar(out=rsA[:, 2 * i:2 * i + 2], in0=ob[i][:, :, 64], scalar1=1e-30, scalar2=None, op0=ALU.max),
                                      [('ps', i)], [('rsA', i)])
                                V(lambda e: e.reciprocal(out=riA[:, 0:4], in_=rsA[:, 0:4]), [('rsA', 0), ('rsA', 1)], ['riAc'])
                                V(lambda e: e.scalar_tensor_tensor(out=imp[:], in0=ob[0][:, 0, 65:193], scalar=riA[:, 0:1], in1=vm_[:, 0, :],
                                                                   op0=ALU.mult, op1=ALU.add), [('ps', 0), 'riAc', vk], ['imp'])
                                for h in range(1, 4):
                                    V(lambda e, h=h: e.scalar_tensor_tensor(out=imp[:], in0=ob[h // 2][:, h % 2, 65:193], scalar=riA[:, h:h + 1], in1=imp[:],
                                                                            op0=ALU.mult, op1=ALU.add), [('ps', h // 2), 'riAc', 'imp'], ['imp'])
                                V(lambda e: e.max(out=m8[:, 0:8], in_=imp[:]), ['imp'], ['m8a'])
                                V(lambda e: e.match_replace(out=wkt[:], in_to_replace=m8[:, 0:8], in_values=imp[:], imm_value=-1e30), ['imp', 'm8a'], ['wkt'])
                                V(lambda e: e.max(out=m8[:, 8:16], in_=wkt[:]), ['wkt'], ['m8b'])
                                V(lambda e: e.tensor_scalar(out=selm[:], in0=imp[:], scalar1=m8[:, 15:16], scalar2=None, op0=ALU.is_ge), ['imp', 'm8b'], ['selm'])
                                V(lambda e: e.tensor_scalar(out=NBt[:, 64:192], in0=selm[:], scalar1=-1.0, scalar2=BIG, op0=ALU.add, op1=ALU.mult), ['selm'], ['NBa'])
                                V(lambda e: e.tensor_copy(out=accS[:, 0:386], in_=psb[0][:, 0:386]), [('ps', 0)], [('accS', 0)])
                                V(lambda e: e.tensor_copy(out=accS[:, 386:772], in_=psb[1][:, 0:386]), [('ps', 1)], [('accS', 1)])
                            tiles[-1]['post_pv'].append(imp_chain)

                            wt = [(rp, dk) for dk in range(2) for rp in range(4) if k - 1 + dk >= 0]
                            for idx, (rp, dk) in enumerate(wt):
                                slot = rp * 16 + k - 1 + dk

                                def qk_w(b, slot=slot, rp=rp, dk=dk):
                                    T(lambda e: e.matmul(psb[b][:, :], lhsT=Kwin[:, slot, :], rhs=Qaug[g][:, 2, :], start=True, stop=False),
                                      [qk, ('Qz', g), 'Kwin'], [('ps', b)])
                                    T(lambda e: e.matmul(psb[b][:, :].rearrange("p (h q) -> p h q", h=4), lhsT=identB[:],
                                                         rhs=bc(WMB[:, rp * 2 + dk, :].unsqueeze(1), [128, 4, 128]), start=False, stop=True), [], [('ps', b)])

                                def exp_f(b, pi):
                                    A(lambda e: e.activation(out=Pt[pi][:], in_=psb[b][:, :], func=AF.Exp, scale=SCALE), [('ps', b)], [('Pt', pi)])

                                def pv_w(pi, slot=slot, last=(idx == len(wt) - 1)):
                                    for h in range(4):
                                        T(lambda e, h=h: e.matmul(psb[3][:, h * 65:(h + 1) * 65], lhsT=Pt[pi][:, h * 128:(h + 1) * 128], rhs=Vw[:, slot, g, :],
                                                                  start=False, stop=last), [('Pt', pi), 'Vw'], [('ps', 3)])
                                t = dict(qk=qk_w, exp=exp_f, pv=pv_w, pre_qk=[], pre_pv=[], post_pv=[])
                                if idx == 0:
                                    t['pre_pv'].append(zero_acc([(3, 260)]))
                                    firstwin.append(t)
                                tiles.append(t)

                            def mask_to_q():
                                if g == 0:
                                    for ver, (w0, w1) in enumerate([(0, 128), (64, 192)]):
                                        T(lambda e, ver=ver, w0=w0, w1=w1: e.transpose(out=psT[:, ver * 128:(ver + 1) * 128], in_=NBt[:, w0:w1], identity=identB[:]),
                                          ['NBa'], ['psT'])
                                else:
                                    for ver, (w0, w1) in enumerate([(64, 128), (128, 192)]):
                                        T(lambda e, ver=ver, w0=w0, w1=w1: e.transpose(out=psT[0:64, ver * 128:(ver + 1) * 128], in_=NBt[:, w0:w1], identity=identB[:]),
                                          ['NBa'], ['psT'])
                                for ver in range(2):
                                    V(lambda e, ver=ver: e.tensor_copy(out=Qaug[g][mr, ver, :].rearrange("p (h q) -> p h q", h=4),
                                                                       in_=bc(psT[mr, ver * 128:(ver + 1) * 128].unsqueeze(1), [64, 4, 128])),
                                      ['psT'], [('Qm', g, ver)])
                            st_ = [(rp, kp) for kp in range(k + 1) for rp in range(4)]
                            for idx, (rp, kp) in enumerate(st_):
                                slot = rp * 16 + kp
                                ver = 0 if rp < 2 else 1
                                diag = (kp == k)

                                def qk_s(b, slot=slot, ver=ver, diag=diag, rp=rp):
                                    T(lambda e: e.matmul(psb[b][:, :], lhsT=Kaug[g][:, slot, :], rhs=Qaug[g][:, ver, :], start=True, stop=(not diag)),
                                      [qk, ('Qm', g, ver)] + KaugK[g], [('ps', b)])
                                    if diag:
                                        T(lambda e: e.matmul(psb[b][:, :].rearrange("p (h q) -> p h q", h=4), lhsT=identB[:],
                                                             rhs=bc(DMB[:, rp, :].unsqueeze(1), [128, 4, 128]), start=False, stop=True), [], [('ps', b)])

                                def pv_s(pi, slot=slot, last=(idx == len(st_) - 1)):
                                    for h in range(4):
                                        T(lambda e, h=h: e.matmul(psb[2][:, h * 65:(h + 1) * 65], lhsT=Pt[pi][:, h * 128:(h + 1) * 128], rhs=Vs[:, slot, g, :],
                                                                  start=False, stop=last), [('Pt', pi), 'Vs'], [('ps', 2)])
                                t = dict(qk=qk_s, exp=exp_f, pv=pv_s, pre_qk=[], pre_pv=[], post_pv=[])
                                if idx == 0:
                                    t['pre_qk'].append(mask_to_q)
                                    t['pre_pv'].append(zero_acc([(2, 260)]))
                                tiles.append(t)

                            def combine():
                                V(lambda e: e.tensor_copy(out=accS[:, 772:1032], in_=psb[2][:, 0:260]), [('ps', 2)], [('accS', 2)])
                                V(lambda e: e.tensor_copy(out=accS[:, 1032:1292], in_=psb[3][:, 0:260]), [('ps', 3)], [('accS', 3)])
                                ocv = accS[:, 0:772].rearrange("p (h e) -> p h e", h=4)
                                osv = accS[:, 772:1032].rearrange("p (h e) -> p h e", h=4)
                                owv = accS[:, 1032:1292].rearrange("p (h e) -> p h e", h=4)
                                V(lambda e: e.tensor_scalar(out=rsA[:, 4:8], in0=osv[:, :, 64], scalar1=1e-30, scalar2=None, op0=ALU.max), [('accS', 2)], [('rsA', 2)])
                                V(lambda e: e.tensor_scalar(out=rsA[:, 8:12], in0=owv[:, :, 64], scalar1=1e-30, scalar2=None, op0=ALU.max), [('accS', 3)], [('rsA', 3)])
                                V(lambda e: e.reciprocal(out=riA[:, 4:12], in_=rsA[:, 4:12]), [('rsA', 2), ('rsA', 3)], ['riAs'])
                                V(lambda e: e.tensor_tensor(out=fA[:, :].rearrange("p (c h) -> p c h", c=3), in0=riA[:, :].rearrange("p (c h) -> p c h", c=3),
                                                            in1=GATES[:, k, g * 12:(g + 1) * 12].rearrange("p (h c) -> p c h", c=3), op=ALU.mult),
                                  ['riAc', 'riAs'], ['fA'])
                                V(lambda e: e.tensor_tensor(out=oacc[:], in0=ocv[:, :, 0:64], in1=bc(fA[:, 0:4].unsqueeze(2), [128, 4, 64]), op=ALU.mult),
                                  [('accS', 0), ('accS', 1), 'fA'], ['oacc'])
                                G(lambda e: e.tensor_tensor(out=t2[:], in0=osv[:, :, 0:64], in1=bc(fA[:, 4:8].unsqueeze(2), [128, 4, 64]), op=ALU.mult),
                                  [('accS', 2), 'fA'], ['t2'])
                                G(lambda e: e.tensor_tensor(out=t3[:], in0=owv[:, :, 0:64], in1=bc(fA[:, 8:12].unsqueeze(2), [128, 4, 64]), op=ALU.mult),
                                  [('accS', 3), 'fA'], ['t3'])
                                G(lambda e: e.tensor_tensor(out=oacc[:], in0=oacc[:], in1=t2[:], op=ALU.add), ['oacc', 't2'], ['oacc'])
                                G(lambda e: e.tensor_tensor(out=oacc[:], in0=oacc[:], in1=t3[:], op=ALU.add), ['oacc', 't3'], ['oacc'])
                                G(lambda e: e.tensor_tensor(out=mixn[:, g * 256:(g + 1) * 256], in0=oacc[:].rearrange("p h d -> p (h d)"),
                                                            in1=SZ[:, k, g * 256:(g + 1) * 256], op=ALU.mult), ['oacc'], [('mixn', g)])

                            def bo_a():
                                for c in range(8):
                                    if c < 2:
                                        src, rk_ = YGM[:, k, c * 128:(c + 1) * 128], []
                                    elif c < 6:
                                        src, rk_ = mixn[:, (c - 2) * 128:(c - 1) * 128], [('mixn', (c - 2) // 2)]
                                    else:
                                        src, rk_ = YGM[:, k, 256 + (c - 6) * 128:256 + (c - 5) * 128], []
                                    T(lambda e, c=c, src=src: e.transpose(out=psT[:, c * 128:(c + 1) * 128], in_=src, identity=identB[:]), rk_, ['psT'])
                                V(lambda e: e.tensor_copy(out=mixT[:].rearrange("p c t -> p (c t)"), in_=psT[:, :]), ['psT'], ['mixT'])

                            def bo_b(j):
                                def f():
                                    half = j // 4
                                    for c in (2 * (j % 4), 2 * (j % 4) + 1):
                                        T(lambda e, half=half, c=c: e.matmul(psb[half][:, :], lhsT=mixT[:, c, :], rhs=Wout[:, c, half * 512:(half + 1) * 512],
                                                                             start=(c == 0), stop=(c == 7)), ['mixT'], [('ps', half)])
                                return f

                            def bo_c():
                                yb = [0, 1]
                                for half in range(2):
                                    hs = slice(half * 512, (half + 1) * 512)
                                    V(lambda e, half=half, hs=hs: e.scalar_tensor_tensor(out=zb[:, hs], in0=xblk[:, hs], scalar=ALPHA, in1=psb[yb[half]][:, :],
                                                                                         op0=ALU.mult, op1=ALU.add), [('ps', yb[half]), 'xblk'], [('zb', half)])
                                    V(lambda e, half=half, hs=hs: e.bn_stats(out=stt[:, half * 6:(half + 1) * 6], in_=zb[:, hs]), [('zb', half)], [('stt', half)])
                                V(lambda e: e.bn_aggr(out=mv[:, 0:2], in_=stt[:, :]), [('stt', 0), ('stt', 1)], ['mv'])
                                V(lambda e: e.tensor_scalar(out=mv[:, 2:3], in0=mv[:, 1:2], scalar1=LN_EPS, scalar2=None, op0=ALU.add), ['mv'], ['mv2'])
                                G(lambda e: e.tensor_tensor(out=mv[:, 3:4], in0=mv[:, 2:3], in1=mhalfB[:, 0:1], op=ALU.pow), ['mv2', 'mhalfB'], ['mv3'])
                                V(lambda e: e.tensor_scalar(out=zb[:], in0=zb[:], scalar1=mv[:, 0:1], scalar2=mv[:, 3:4], op0=ALU.subtract, op1=ALU.mult),
                                  [('zb', 0), ('zb', 1), 'mv', 'mv3'], [('zb', 0), ('zb', 1)])
                                V(lambda e: e.tensor_tensor(out=zb[:], in0=zb[:], in1=lngt[:], op=ALU.mult), [('zb', 0), ('zb', 1), 'lngt'], [('zb', 0), ('zb', 1)])
                                V(lambda e: e.tensor_tensor(out=zb[:], in0=zb[:], in1=lnbt[:], op=ALU.add), [('zb', 0), ('zb', 1), 'lnbt'], [('zb', 0), ('zb', 1)])
                                DMA(lambda e: e.dma_start(out=xdst[k * 128:(k + 1) * 128, :], in_=zb[:]), [('zb', 0), ('zb', 1)], [('xdst', k)])
                                if k + 1 < NK:
                                    DMA(lambda e: e.dma_start(out=xblk[:], in_=xsrc[(k + 1) * 128:(k + 2) * 128, :]), [], ['xblk'])
                            tiles[-1]['post_pv'].append(combine)
                            if g == 1:
                                L_ = len(tiles) - 1
                                D_ = min(8, 12 + 4 * (k + 2) - 10)
                                deferred.append((L_ + D_, bo_a))
                                for j in range(8):
                                    deferred.append((L_ + D_ + 1 + j, bo_b(j)))
                                deferred.append((L_ + D_ + 9, bo_c))

                        DEFER = 14
                        deferred = []
                        DMA(lambda e: e.dma_start(out=xblk[:], in_=xsrc[0:128, :]), [], ['xblk'])
                        qcopies, firstwin = [], []
                        for k in range(NK):
                            for g in range(2):
                                mk_group(k, g)
                        tiles[0]['pre_qk'].insert(0, qcopies[0])
                        for n_ in range(1, len(qcopies)):
                            firstwin[n_ - 1]['pre_qk'].insert(0, qcopies[n_])
                        tail_hooks = []
                        for ti, fn in deferred:
                            if ti < len(tiles):
                                tiles[ti]['post_pv'].append(fn)
                            else:
                                tail_hooks.append(fn)
                        LOOK = 2
                        nt_ = len(tiles)
                        binfo = {}
                        for idx in range(nt_ + LOOK):
                            if idx < nt_:
                                t = tiles[idx]
                                for f in t['pre_qk']:
                                    f()
                                b = nbank((4, 5, 6))
                                binfo[idx] = b
                                t['qk'](b)
                            i = idx - LOOK
                            if i >= 0:
                                t = tiles[i]
                                pi = i % 4
                                t['exp'](binfo[i], pi)
                                for f in t['pre_pv']:
                                    f()
                                t['pv'](pi)
                                for f in t['post_pv']:
                                    f()
                        for f in tail_hooks:
                            f()
                    except _Stop:
                        pass
                    S.barrier()
                    S.emit()
        S.barrier()
        S.emit()
    return nc


def _bf(a):
    return np.asarray(a, dtype=np.float32).astype(ml_dtypes.bfloat16)


def _consts_common():
    eind = np.zeros((64, 64, 128), np.float32)
    for s in range(64):
        for half in range(2):
            eind[(2 * s + half) % 64, s, half * 64:(half + 1) * 64] = 1.0
    zc = np.zeros((128, 256), np.float32)
    for e in range(16):
        zc[e, e + 120] = 1.0
    ov = np.zeros((128, 4, 128), np.float32)
    for rp in range(4):
        for kp in range(16):
            for m in range(8):
                n = 8 * (4 * kp + rp) + m
                for jb in ([n // 4] + ([n // 4 + 1] if n % 4 == 3 else [])):
                    if jb > 127:
                        continue
                    j2, half = jb // 2, jb % 2
                    beta = 2 * ((j2 % 4) * 16 + j2 // 4) + half
                    ov[8 * kp + m, rp, beta] = 1.0
    return _bf(eind.reshape(64, SEQ)), _bf(zc), _bf(ov)


def _consts_core(r):
    p = np.arange(128)
    bcmp = np.zeros((128, 4, 128), np.float32)
    for dk in range(2):
        for m in range(8):
            for rp in range(4):
                dj = 4 * (dk - 1) + rp - r
                ok = (128 * dj + 16 * m + 31) <= p
                bcmp[dk * 8 + m, rp, :] = np.where(ok, 0.0, -BIG)
    c = np.arange(128)[:, None]
    q = np.arange(128)[None, :]
    dmb = np.zeros((128, 4, 128), np.float32)
    for rp in range(4):
        if rp < r:
            dmb[:, rp, :] = 0.0
        elif rp > r:
            dmb[:, rp, :] = -BIG
        else:
            dmb[:, rp, :] = np.where(c <= q, 0.0, -BIG)
    wmb = np.zeros((128, 4, 2, 128), np.float32)
    for rp in range(4):
        for dk in range(2):
            dj = 4 * (dk - 1) + rp - r
            if dj == 0:
                wmb[:, rp, dk, :] = np.where(c <= q, 0.0, -BIG)
            elif dj in (-1, -2, -3):
                wmb[:, rp, dk, :] = 0.0
            elif dj == -4:
                wmb[:, rp, dk, :] = np.where(c > q, 0.0, -BIG)
            else:
                wmb[:, rp, dk, :] = -BIG
    beta = np.arange(128)
    s_ = beta // 2
    jb = 2 * (4 * (s_ % 16) + s_ // 16) + beta % 2
    vm = np.zeros((128, NK, 1, 128), np.float32)
    for k in range(NK):
        i = 4 * k + r
        tblk = (2 * i + (p >= 64))[:, None]
        valid = jb[None, :] <= tblk
        forced = (jb[None, :] == 0) | (valid & (jb[None, :] > tblk - 2))
        vm[:, k, 0, :] = np.where(forced, 100.0, np.where(valid, 0.0, -100.0))
    return _bf(bcmp), _bf(dmb), _bf(wmb.reshape(128, 8, 128)), _bf(vm)


def _rope_tables(r):
    blocks = 4 * np.arange(NK) + r
    pos = (blocks[:, None] * 128 + np.arange(128)[None, :]).reshape(-1).astype(np.float32)
    inv = (np.float32(10000.0) ** (-np.arange(32, dtype=np.float32) * np.float32(2.0) / np.float32(64))).astype(np.float32)
    ang = (pos[None, :] * inv[:, None]).astype(np.float32)
    cos = np.cos(ang).astype(np.float32)
    sin = np.sin(ang).astype(np.float32)
    C = np.concatenate([cos, cos, cos, cos], axis=0)
    Sg = np.concatenate([-sin, sin, -sin, sin], axis=0)
    return np.ascontiguousarray(C), np.ascontiguousarray(Sg)


def make_in_maps(inputs):
    f = lambda k: np.asarray(inputs[k], dtype=np.float32)
    x, mem = f("x"), f("mem")
    w_in = f("w_in")
    allcols = np.concatenate([g[2] for g in COLG])
    shared = {
        "w_in_p": np.ascontiguousarray(w_in[:, :, allcols]),
        "gm_ln_g": f("gm_ln_g").reshape(DEPTH, 1, 256),
        "gm_ln_b": f("gm_ln_b").reshape(DEPTH, 1, 256),
        "gm_ws": f("gm_ws"),
        "gm_bsT": np.ascontiguousarray(f("gm_bs").transpose(0, 2, 1)),
        "cmp_pos_kT": np.ascontiguousarray(f("cmp_pos_k").transpose(0, 2, 1)),
        "cmp_pos_vT": np.ascontiguousarray(f("cmp_pos_v").transpose(0, 2, 1)),
        "cmp_k_w1": f("cmp_k_w1"), "cmp_v_w1": f("cmp_v_w1"),
        "cmp_k_w2": f("cmp_k_w2"), "cmp_v_w2": f("cmp_v_w2"),
        "w_mem_kv": f("w_mem_kv"), "w_out": f("w_out"),
        "ln_g": f("ln_g").reshape(DEPTH, 1, D), "ln_b": f("ln_b").reshape(DEPTH, 1, D),
    }
    eind, zc, ov = _consts_common()
    pm = np.zeros((128, 128), np.float32)
    for pq in range(128):
        pm[(pq // 64) * 64 + (pq % 64 + 32) % 64, pq] = 1.0
    shared.update({"eind": eind, "zc": zc, "ovc": ov, "pswap": _bf(pm)})
    maps = []
    for c in range(8):
        b, r = c // 4, c % 4
        blocks = 4 * np.arange(NK) + r
        m = dict(shared)
        m["x_own"] = np.ascontiguousarray(x[b].reshape(64, 128, D)[blocks].reshape(NT, D))
        m["mem_b"] = np.ascontiguousarray(mem[b])
        C, Sg = _rope_tables(r)
        m["ropeC"], m["ropeS"] = C, Sg
        bcmp, dmb, wmb, vm = _consts_core(r)
        m.update({"bcmp": bcmp, "dmb": dmb, "wmb": wmb, "vmc": vm})
        maps.append(m)
    return maps


def assemble(results):
    out = np.zeros((NB, SEQ, D), np.float32)
    for c in range(8):
        b, r = c // 4, c % 4
        blocks = 4 * np.arange(NK) + r
        o = np.asarray(results[c]["out"], dtype=np.float32).reshape(NK, 128, D)
        out[b].reshape(64, 128, D)[blocks] = o
    return out


_NC_CACHE = {}


def kernel(**inputs):
    if "nc" not in _NC_CACHE:
        _NC_CACHE["nc"] = build()
    nc = _NC_CACHE["nc"]
    maps = make_in_maps(inputs)
    res = run_bass_kernel_spmd(nc, maps, core_ids=list(range(8)))
    return assemble(res.results)
```

```python
import os
import numpy as np
import ml_dtypes
from contextlib import ExitStack
import concourse.bass as bass
import concourse.mybir as mybir
from concourse.bass_utils import run_bass_kernel_spmd

F32 = mybir.dt.float32
BF16 = mybir.dt.bfloat16
AF = mybir.ActivationFunctionType
ALU = mybir.AluOpType
AX = mybir.AxisListType

D = 1024
SEQ = 8192
NB = 2
DEPTH = 2
NK = 16
NT = NK * 128
ALPHA = (2.0 * DEPTH) ** 0.25
LN_EPS = 1e-5
SCALE = 0.125
BIG = 30000.0
CCW = 12352
O_KC, O_VC, O_KS, O_KW, O_VS, O_VW = 0, 2048, 4096, 6144, 8192, 10272

C_U, C_V, C_Z, C_Q, C_KC, C_VC, C_KS, C_VS, C_KW, C_VW, C_G, C_NZ, C_MQ, C_MZ = (
    0, 256, 512, 768, 1280, 1408, 1536, 1664, 1792, 1920, 2048, 2072, 2584, 2840)


def _swap64(cols):
    cols = np.asarray(cols).reshape(-1, 64)
    return np.concatenate([cols[:, 32:], cols[:, :32]], axis=1).reshape(-1)


def _col_groups():
    ar = np.arange
    groups = []
    for nm, c0 in (("kc", C_KC), ("ks", C_KS), ("kw", C_KW)):
        p = ar(c0, c0 + 128)
        groups.append((nm, "F", p))
    groups.append(("vg", "T", np.concatenate([ar(C_VS, C_VS + 128), ar(C_VW, C_VW + 128)] + [ar(C_G, C_G + 24)] * 5 + [ar(C_G, C_G + 8)])))
    groups.append(("vcmq", "F", np.concatenate([ar(C_VC, C_VC + 128), ar(C_MQ, C_MQ + 256)])))
    groups.append(("uv", "T", np.concatenate([ar(C_U, C_U + 256), ar(C_V, C_V + 256)])))
    groups.append(("zz", "T", np.concatenate([ar(C_Z, C_Z + 256), ar(C_MZ, C_MZ + 256)])))
    groups.append(("nz", "T", ar(C_NZ, C_NZ + 512)))
    for hg in range(4):
        p = np.concatenate([ar(C_Q + hg * 64, C_Q + hg * 64 + 64), ar(C_Q + (4 + hg) * 64, C_Q + (4 + hg) * 64 + 64)])
        groups.append((f"q{hg}", "F", p))
    return groups


COLG = _col_groups()
WPC = int(sum(len(g[2]) for g in COLG))


class Sched:
    ENG = ['tensor', 'vector', 'scalar', 'gpsimd', 'sync']
    EPOCH = 30000
    NEPOCH = 4
    NDS = 8

    def __init__(self, nc, es):
        self.nc = nc
        self.es = es
        self.ops = {e: [] for e in self.ENG}
        self.cnt = {e: 0 for e in self.ENG}
        self.sems = {}
        for e in self.ENG:
            for ep in range(self.NEPOCH):
                self.sems[(e, 'e', ep)] = es.enter_context(nc.semaphore(f"s_{e}_{ep}"))
        self.dq = {}
        for q in ['sync', 'gpsimd']:
            self.dq[q] = {'i': 0, 'use': [0] * self.NDS}
            for k in range(self.NDS):
                self.sems[(q, 'd', k)] = es.enter_context(nc.semaphore(f"d_{q}_{k}"))
        self.ncc = 0
        self.waited = {e: {} for e in self.ENG}
        self.lastw = {}
        self.readers = {}

    def op(self, eng, fn, reads=(), writes=(), dma=False):
        deps = []
        for r in reads:
            if r in self.lastw:
                deps.append(self.lastw[r])
        for w in writes:
            if w in self.lastw:
                deps.append(self.lastw[w])
            deps.extend(self.readers.get(w, []))
        if dma == 'cc':
            self.ncc += 1
            sk = ('cc', 'c', self.ncc)
            self.sems[sk] = self.es.enter_context(self.nc.semaphore(f"cc_{self.ncc}"))
            tok = (sk, 1)
        elif dma:
            q = self.dq[eng]
            k = q['i'] % self.NDS
            q['i'] += 1
            q['use'][k] += 1
            sk = (eng, 'd', k)
            val = 16 * q['use'][k]
            if q['use'][k] > 1:
                deps.append((sk, val - 16))
            tok = (sk, val)
        else:
            self.cnt[eng] += 1
            ep = (self.cnt[eng] - 1) // self.EPOCH
            sk = (eng, 'e', ep)
            tok = (sk, self.cnt[eng] - ep * self.EPOCH)
        need = {}
        for (dk, v) in deps:
            if eng == 'tensor' and dk[0] == 'tensor' and dk[1] == 'e':
                continue
            if self.waited[eng].get(dk, 0) >= v:
                continue
            need[dk] = max(need.get(dk, 0), v)
        for dk, v in need.items():
            self.waited[eng][dk] = v
        self.ops[eng].append((list(need.items()), fn, tok))
        for w in writes:
            self.lastw[w] = tok
            self.readers[w] = []
        for r in reads:
            self.readers.setdefault(r, []).append(tok)
        return tok

    def barrier(self):
        toks = set()
        for t in self.lastw.values():
            toks.add(t)
        for rl in self.readers.values():
            toks.update(rl)
        best = {}
        for (dk, v) in toks:
            best[dk] = max(best.get(dk, 0), v)
        for e in self.ENG:
            need = []
            for dk, v in best.items():
                if self.waited[e].get(dk, 0) >= v:
                    continue
                self.waited[e][dk] = v
                need.append((dk, v))
            if need:
                self.ops[e].append((need, None, None))
        self.lastw = {}
        self.readers = {}

    def emit(self):
        with self.nc.Block() as block:
            for e in self.ENG:
                ops = self.ops[e]
                if not ops:
                    continue

                def body(engobj, ops=ops):
                    for waits, fn, tok in ops:
                        for dk, v in waits:
                            engobj.wait_ge(self.sems[dk], v)
                        if fn is None:
                            continue
                        ins = fn(engobj)
                        if tok[0][1] == 'c':
                            ins.then_inc(self.sems[tok[0]])
                        else:
                            ins.then_inc(self.sems[tok[0]], 16 if tok[0][1] == 'd' else 1)
                getattr(block, e)(body)
        self.ops = {e: [] for e in self.ENG}


class _Stop(Exception):
    pass


def _kstop(n):
    import os
    if int(os.environ.get('KSTOP', '99')) <= n:
        raise _Stop()


KBK = int(os.environ.get('KBK', '0'))
KBG = int(os.environ.get('KBG', '0'))


def bc(ap, shape):
    return ap.to_broadcast(list(shape))


def build(nlayers=DEPTH, stop_after=None, debug=False):
    nc = bass.Bass("TRN2", target_bir_lowering=False)
    dt_in = lambda n, s, d=F32: nc.dram_tensor(n, list(s), d, kind="ExternalInput").ap()
    x_in = dt_in("x_own", [NT, D])
    mem_in = dt_in("mem_b", [256, D])
    wp_in = dt_in("w_in_p", [DEPTH, D, WPC])
    gm_ln_g = dt_in("gm_ln_g", [DEPTH, 1, 256])
    gm_ln_b = dt_in("gm_ln_b", [DEPTH, 1, 256])
    gm_ws = dt_in("gm_ws", [DEPTH, 4, 128, 128])
    gm_bsT = dt_in("gm_bsT", [DEPTH, 128, 4])
    posk_in = dt_in("cmp_pos_kT", [DEPTH, 64, 32])
    posv_in = dt_in("cmp_pos_vT", [DEPTH, 64, 32])
    w1k_in = dt_in("cmp_k_w1", [DEPTH, 2048, 128])
    w1v_in = dt_in("cmp_v_w1", [DEPTH, 2048, 128])
    w2k_in = dt_in("cmp_k_w2", [DEPTH, 128, 64])
    w2v_in = dt_in("cmp_v_w2", [DEPTH, 128, 64])
    wmem_in = dt_in("w_mem_kv", [DEPTH, D, 512])
    wout_in = dt_in("w_out", [DEPTH, D, D])
    lng_in = dt_in("ln_g", [DEPTH, 1, D])
    lnb_in = dt_in("ln_b", [DEPTH, 1, D])
    ropeC_in = dt_in("ropeC", [128, NT])
    ropeS_in = dt_in("ropeS", [128, NT])
    eind_in = dt_in("eind", [64, SEQ], BF16)
    zc_in = dt_in("zc", [128, 256], BF16)
    ov_in = dt_in("ovc", [128, 4, 128], BF16)
    bcmp_in = dt_in("bcmp", [128, 4, 128], BF16)
    dmb_in = dt_in("dmb", [128, 4, 128], BF16)
    wmb_in = dt_in("wmb", [128, 8, 128], BF16)
    pm_in = dt_in("pswap", [128, 128], BF16)
    vm_in = dt_in("vmc", [128, NK, 1, 128], BF16)
    out_d = nc.dram_tensor("out", [NT, D], F32, kind="ExternalOutput").ap()
    xmid = nc.dram_tensor("xmid", [NT, D], F32)
    CCN = [4096, 4096, 2080, 2080]
    cc_src = [nc.dram_tensor(f"cc_src{i}", [128, CCN[i]], BF16) for i in range(4)]
    cc_dst = [nc.dram_tensor(f"cc_dst{i}", [512, CCN[i]], BF16) for i in range(4)]
    groups = [[0, 1, 2, 3], [4, 5, 6, 7]]
    dbg = {}
    if debug:
        dbg["qt"] = nc.dram_tensor("dbg_qt", [128, NK * 512], BF16, kind="ExternalOutput").ap()
        for bi, n_ in enumerate([4096, 4096, 2080, 2080]):
            dbg[f"cc{bi}"] = nc.dram_tensor(f"dbg_cc{bi}", [512, n_], BF16, kind="ExternalOutput").ap()
        dbg["ygm"] = nc.dram_tensor("dbg_ygm", [128, NK * 512], BF16, kind="ExternalOutput").ap()
        dbg["sz"] = nc.dram_tensor("dbg_sz", [128, NK * 512], BF16, kind="ExternalOutput").ap()
        dbg["gates"] = nc.dram_tensor("dbg_gates", [128, NK * 24], F32, kind="ExternalOutput").ap()
        dbg["kcmp"] = nc.dram_tensor("dbg_kcmp", [128, 512], BF16, kind="ExternalOutput").ap()
        dbg["rhsc"] = nc.dram_tensor("dbg_rhsc", [128, 8 * 193], BF16, kind="ExternalOutput").ap()
        dbg["mix"] = nc.dram_tensor("dbg_mix", [128, NK * 512], BF16, kind="ExternalOutput").ap()
        dbg["acc"] = nc.dram_tensor("dbg_acc", [128, NK * 2 * 1292], F32, kind="ExternalOutput").ap()
        dbg["nb"] = nc.dram_tensor("dbg_nb", [128, NK * 2 * 128], BF16, kind="ExternalOutput").ap()

    with ExitStack() as es:
        S = Sched(nc, es)
        sbP = lambda n, s, d: es.enter_context(nc.sbuf_tensor(n, list(s), d))
        psb = [es.enter_context(nc.psum_tensor(f"psb{i}", [128, 512], F32)) for i in range(7)]
        psT = es.enter_context(nc.psum_tensor("psT", [128, 1024], BF16))
        rr = {'i': 0}

        def nbank(pool=(4, 5, 6)):
            b = pool[rr['i'] % len(pool)]
            rr['i'] += 1
            return b

        QT = sbP("QT", [128, NK, 4, 128], BF16)
        YGM = sbP("YGM", [128, NK, 512], BF16)
        SZ = sbP("SZ", [128, NK, 512], BF16)
        GATES = sbP("GATES", [128, NK, 24], F32)
        identF = sbP("identF", [128, 128], F32)
        identB = sbP("identB", [128, 128], BF16)
        ZC = sbP("ZC", [128, 256], BF16)
        BCMP = sbP("BCMP", [128, 4, 128], BF16)
        DMB = sbP("DMB", [128, 4, 128], BF16)
        WMB = sbP("WMB", [128, 8, 128], BF16)

        V = lambda fn, r, w: S.op('vector', fn, reads=r, writes=w)
        A = lambda fn, r, w: S.op('scalar', fn, reads=r, writes=w)
        G = lambda fn, r, w: S.op('gpsimd', fn, reads=r, writes=w)
        T = lambda fn, r, w: S.op('tensor', fn, reads=r, writes=w)
        DMA = lambda fn, r, w, q='sync': S.op(q, fn, reads=r, writes=w, dma=True)

        G(lambda e: e.memset(identF[:], 0.0), [], ['identF'])
        G(lambda e: e.affine_select(out=identF[:], in_=identF[:], pattern=[[-1, 128]], compare_op=ALU.not_equal,
                                    fill=1.0, base=0, channel_multiplier=1), ['identF'], ['identF'])
        G(lambda e: e.tensor_copy(out=identB[:], in_=identF[:]), ['identF'], ['identB'])
        PMB = sbP("PMB", [128, 128], BF16)
        DMA(lambda e: e.dma_start(out=PMB[:], in_=pm_in[:, :]), [], ['PMB'])
        DMA(lambda e: e.dma_start(out=ZC[:], in_=zc_in[:, :]), [], ['ZC'])
        DMA(lambda e: e.dma_start(out=BCMP[:], in_=bcmp_in[:, :, :]), [], ['BCMP'])
        DMA(lambda e: e.dma_start(out=DMB[:], in_=dmb_in[:, :, :]), [], ['DMB'])
        DMA(lambda e: e.dma_start(out=WMB[:], in_=wmb_in[:, :, :]), [], ['WMB'])

        for l in range(nlayers):
            xsrc = x_in if l == 0 else xmid.ap()
            xdst = out_d if l == nlayers - 1 else xmid.ap()
            with ExitStack() as pa:
                sb = lambda n, s, d: pa.enter_context(nc.sbuf_tensor(f"{n}_A{l}", list(s), d))
                xT = sb("xT", [128, 8, NT], BF16)
                memT = sb("memT", [128, 8, 256], BF16)
                wst = sb("wst", [128, 8, 512], F32)
                wbf = [sb(f"wbf{i}", [128, 8, 512], BF16) for i in range(2)]
                ropeC = sb("ropeC", [128, NT], F32)
                ropeS = sb("ropeS", [128, NT], F32)
                xin = [sb(f"xin{i}", [128, D], F32) for i in range(3)]
                memK = sb("memK", [128, 2, 256], BF16)
                memV = sb("memV", [128, 2, 4, 65], BF16)
                mqT = sb("mqT", [128, 2, NT], BF16)
                wcT = sb("wcT", [128, 4, 128], BF16)
                wsf = sb("wsf", [128, 4, 128], F32)
                bsT = sb("bsT", [128, 4], F32)
                glng = sb("glng", [128, 256], F32)
                glnb = sb("glnb", [128, 256], F32)
                kst = [sb(f"kst{i}", [128, NT], BF16) for i in range(2)]
                vst = [sb(f"vst{i}", [128, NK, 2, 65], BF16) for i in range(2)]
                guL = [sb(f"gu{i}", [128, 256], BF16) for i in range(3)]
                gvL = [sb(f"gv{i}", [128, 4, 64], F32) for i in range(3)]
                gcL = [sb(f"gc{i}", [128, 4, 64], F32) for i in range(2)]
                gsqL = [sb(f"gsq{i}", [128, 4, 64], F32) for i in range(2)]
                mhalf = sb("mhalf", [128, 4], F32)
                G(lambda e: e.memset(mhalf[:], -0.5), [], ['mhalf'])
                vlnL = [sb(f"vln{i}", [128, 256], BF16) for i in range(2)]
                st4L = [sb(f"st4{i}", [128, 16], F32) for i in range(2)]
                pbf = [sb(f"pbf{i}", [128, 512], BF16) for i in range(2)]
                tA = [sb(f"tA{i}", [128, 512], F32) for i in range(2)]
                tB = [sb(f"tB{i}", [128, 512], F32) for i in range(2)]
                pmL = [sb(f"pm{i}", [128, 2, 2, 2, 128], BF16) for i in range(2)]
                om = sb("om", [128, 4, 64], F32)

                try:
                    DMA(lambda e: e.dma_start(out=ropeC[:], in_=ropeC_in[:, :]), [], ['ropeC'])
                    DMA(lambda e: e.dma_start(out=ropeS[:], in_=ropeS_in[:, :]), [], ['ropeS'])
                    for i in range(2):
                        G(lambda e, i=i: e.memset(vst[i][:, :, :, 64:65], 1.0), [], [('vst1', i)])
                    G(lambda e: e.memset(memV[:, :, :, 64:65], 1.0), [], ['memV1'])

                    cp = 0
                    for kt in range(NK + 2):
                        buf = xin[kt % 3]
                        bk = ('xin', kt % 3)
                        if kt < NK:
                            DMA(lambda e, kt=kt, buf=buf: e.dma_start(out=buf[:], in_=xsrc[kt * 128:(kt + 1) * 128, :]), [], [bk])
                        else:
                            m = kt - NK
                            DMA(lambda e, m=m, buf=buf: e.dma_start(out=buf[:], in_=mem_in[m * 128:(m + 1) * 128, :]), [], [bk])
                        for half in range(2):
                            b = nbank((0, 1, 2, 3))
                            for cc in range(4):
                                c = half * 4 + cc
                                T(lambda e, b=b, cc=cc, c=c, buf=buf: e.transpose(out=psb[b][:, cc * 128:(cc + 1) * 128],
                                                                                   in_=buf[:, c * 128:(c + 1) * 128], identity=identF[:]),
                                  [bk, 'identF'], [('ps', b)])
                            if kt < NK:
                                dst = xT[:, half * 4:(half + 1) * 4, kt * 128:(kt + 1) * 128]
                                dk = ('xT', kt)
                            else:
                                dst = memT[:, half * 4:(half + 1) * 4, (kt - NK) * 128:(kt - NK + 1) * 128]
                                dk = ('memT', kt - NK, half)
                            src = psb[b][:, :].rearrange("p (a b) -> p a b", a=4)
                            if cp % 2 == 0:
                                V(lambda e, dst=dst, src=src: e.tensor_copy(out=dst, in_=src), [('ps', b)], [(dk, half)])
                            else:
                                A(lambda e, dst=dst, src=src: e.copy(out=dst, in_=src), [('ps', b)], [(dk, half)])
                            cp += 1
                    _kstop(1)
                    xT_keys = [(('xT', kt), h) for kt in range(NK) for h in range(2)]
                    memT_keys = [(('memT', m, h), h) for m in range(2) for h in range(2)]

                    wi = {'i': 0}

                    def load_w(src_ap, ncols):
                        i = wi['i'] % 2
                        wi['i'] += 1
                        DMA(lambda e: e.dma_start(out=wst[:, :, 0:ncols], in_=src_ap.rearrange("(c p) n -> p c n", p=128)),
                            [], ['wst'])
                        V(lambda e: e.tensor_copy(out=wbf[i][:, 0:4, 0:ncols], in_=wst[:, 0:4, 0:ncols]), ['wst'], [('wbf', i, 0)])
                        A(lambda e: e.copy(out=wbf[i][:, 4:8, 0:ncols], in_=wst[:, 4:8, 0:ncols]), ['wst'], [('wbf', i, 1)])
                        return wbf[i], [('wbf', i, 0), ('wbf', i, 1)]

                    wm, wmk = load_w(wmem_in[l], 512)
                    for pr in range(2):
                        b = nbank((0, 1, 2, 3))
                        for c in range(8):
                            T(lambda e, b=b, c=c, pr=pr: e.matmul(psb[b][:, 0:256], lhsT=wm[:, c, pr * 128:(pr + 1) * 128],
                                                                 rhs=memT[:, c, :], start=(c == 0), stop=(c == 7)),
                              wmk + memT_keys, [('ps', b)])
                        V(lambda e, b=b, pr=pr: e.tensor_copy(out=memK[:, pr, :], in_=psb[b][:, 0:256]), [('ps', b)], [('memK', pr)])
                    for mt in range(2):
                        b = nbank((0, 1, 2, 3))
                        for c in range(8):
                            T(lambda e, b=b, c=c, mt=mt: e.matmul(psb[b][:, 0:256], lhsT=memT[:, c, mt * 128:(mt + 1) * 128],
                                                                 rhs=wm[:, c, 256:512], start=(c == 0), stop=(c == 7)),
                              wmk + memT_keys, [('ps', b)])
                        V(lambda e, b=b, mt=mt: e.tensor_copy(out=memV[:, mt, :, 0:64],
                                                             in_=psb[b][:, 0:256].rearrange("p (h d) -> p h d", h=4)),
                          [('ps', b), 'memV1'], [('memV', mt)])

                    _kstop(2)
                    DMA(lambda e: e.dma_start(out=wsf[:], in_=gm_ws[l].rearrange("g i j -> i g j")), [], ['wsf'])
                    DMA(lambda e: e.dma_start(out=bsT[:], in_=gm_bsT[l]), [], ['bsT'])
                    DMA(lambda e: e.dma_start(out=glng[:], in_=gm_ln_g[l].partition_broadcast(128)), [], ['glng'])
                    DMA(lambda e: e.dma_start(out=glnb[:], in_=gm_ln_b[l].partition_broadcast(128)), [], ['glnb'])
                    for g in range(4):
                        G(lambda e, g=g: e.affine_select(out=wsf[:, g, :], in_=wsf[:, g, :], pattern=[[-1, 128]],
                                                         compare_op=ALU.is_ge, fill=0.0, base=0, channel_multiplier=1),
                          ['wsf'], ['wsf'])
                    b = nbank((0, 1, 2, 3))
                    for g in range(4):
                        T(lambda e, g=g, b=b: e.transpose(out=psb[b][:, g * 128:(g + 1) * 128], in_=wsf[:, g, :], identity=identF[:]),
                          ['wsf', 'identF'], [('ps', b)])
                    V(lambda e, b=b: e.tensor_copy(out=wcT[:], in_=psb[b][:, :].rearrange("p (g i) -> p g i", g=4)), [('ps', b)], ['wcT'])

                    _kstop(3)
                    col0 = 0
                    for gi_, (gname, kind, cols) in enumerate(COLG):
                        _kstop(4 + gi_)
                        ncols = len(cols)
                        w, wk = load_w(wp_in[l][:, col0:col0 + ncols], ncols)
                        col0 += ncols
                        if gname == "uv":
                            uvb = {}

                            def uv_s0(kt):
                                p3 = kt % 3
                                gu_, gv_ = guL[p3], gvL[p3]
                                b = nbank((0, 1, 2, 3))
                                for c in range(8):
                                    T(lambda e, b=b, c=c, w=w, ncols=ncols: e.matmul(psb[b][:, 0:ncols], lhsT=xT[:, c, kt * 128:(kt + 1) * 128], rhs=w[:, c, 0:ncols],
                                                                   start=(c == 0), stop=(c == 7)), wk + [(('xT', kt), 0), (('xT', kt), 1)], [('ps', b)])
                                P = psb[b]
                                pk = ('ps', b)
                                A(lambda e: e.activation(out=gu_[:], in_=P[:, 0:256], func=AF.Gelu), [pk], [('gu', p3)])
                                A(lambda e: e.activation(out=gv_[:].rearrange("p g c -> p (g c)"), in_=P[:, 256:512], func=AF.Gelu), [pk], [('gv', p3)])

                            def uv_s1(kt):
                                pp = kt % 2
                                p3 = kt % 3
                                gv_, gc_, gsq_, st4_ = gvL[p3], gcL[pp], gsqL[pp], st4L[pp]
                                V(lambda e: e.tensor_reduce(out=st4_[:, 0:4], in_=gv_[:], axis=AX.X, op=ALU.add), [('gv', p3)], [('st_sum', pp)])
                                V(lambda e: e.tensor_scalar(out=st4_[:, 4:8], in0=st4_[:, 0:4], scalar1=-1.0 / 64, scalar2=None, op0=ALU.mult),
                                  [('st_sum', pp)], [('st_nm', pp)])
                                V(lambda e: e.tensor_tensor(out=gc_[:], in0=gv_[:], in1=bc(st4_[:, 4:8].unsqueeze(2), [128, 4, 64]), op=ALU.add),
                                  [('gv', p3), ('st_nm', pp)], [('gc', pp)])
                                G(lambda e: e.tensor_tensor(out=gsq_[:], in0=gc_[:], in1=gc_[:], op=ALU.mult), [('gc', pp)], [('gsq', pp)])
                                V(lambda e: e.tensor_reduce(out=st4_[:, 8:12], in_=gsq_[:], axis=AX.X, op=ALU.add), [('gsq', pp)], [('st_ss', pp)])
                                V(lambda e: e.tensor_scalar(out=st4_[:, 8:12], in0=st4_[:, 8:12], scalar1=1.0 / 64, scalar2=LN_EPS,
                                                            op0=ALU.mult, op1=ALU.add), [('st_ss', pp)], [('st_ss', pp)])
                                G(lambda e: e.tensor_tensor(out=st4_[:, 12:16], in0=st4_[:, 8:12], in1=mhalf[:, 0:4], op=ALU.pow), [('st_ss', pp), 'mhalf'], [('st_rs', pp)])

                            def uv_s2(kt):
                                pp = kt % 2
                                gc_, vln_, st4_ = gcL[pp], vlnL[pp], st4L[pp]
                                V(lambda e: e.tensor_tensor(out=gc_[:], in0=gc_[:], in1=bc(st4_[:, 12:16].unsqueeze(2), [128, 4, 64]), op=ALU.mult),
                                  [('gc', pp), ('st_rs', pp)], [('gc', pp)])
                                G(lambda e: e.tensor_tensor(out=gc_[:].rearrange("p g c -> p (g c)"), in0=gc_[:].rearrange("p g c -> p (g c)"),
                                                            in1=glng[:], op=ALU.mult), [('gc', pp), 'glng'], [('gc', pp)])
                                G(lambda e: e.tensor_tensor(out=vln_[:], in0=gc_[:].rearrange("p g c -> p (g c)"), in1=glnb[:], op=ALU.add),
                                  [('gc', pp), 'glnb'], [('vln', pp)])
                                b2 = nbank((4, 5, 6))
                                uvb[kt] = b2
                                for g in range(4):
                                    T(lambda e, g=g: e.matmul(psb[b2][:, g * 64:(g + 1) * 64], lhsT=wcT[:, g, :], rhs=vln_[:, g * 64:(g + 1) * 64],
                                                              start=True, stop=True), [('vln', pp), 'wcT'], [('ps', b2)])

                            def uv_s3(kt):
                                pp = kt % 2
                                p3 = kt % 3
                                gu_, gsq_ = guL[p3], gsqL[pp]
                                b2 = uvb[kt]
                                V(lambda e: e.tensor_tensor(out=gsq_[:], in0=psb[b2][:, 0:256].rearrange("p (g c) -> p g c", g=4),
                                                            in1=bc(bsT[:, :].unsqueeze(2), [128, 4, 64]), op=ALU.add), [('ps', b2), 'bsT'], [('gsq', pp)])
                                V(lambda e: e.tensor_tensor(out=YGM[:, kt, 0:256], in0=gsq_[:].rearrange("p g c -> p (g c)"), in1=gu_[:], op=ALU.mult),
                                  [('gsq', pp), ('gu', p3)], [('YGMa', kt)])
                            for step in range(NK + 2):
                                if step < NK:
                                    uv_s0(step)
                                if 0 <= step - 2 < NK:
                                    uv_s3(step - 2)
                                if 0 <= step - 1 < NK:
                                    uv_s2(step - 1)
                                if step < NK:
                                    uv_s1(step)
                        elif kind == "T":
                            for kt in range(NK):
                                b = nbank((0, 1, 2, 3))
                                for c in range(8):
                                    T(lambda e, b=b, c=c, kt=kt, w=w, ncols=ncols: e.matmul(
                                        psb[b][:, 0:ncols], lhsT=xT[:, c, kt * 128:(kt + 1) * 128], rhs=w[:, c, 0:ncols],
                                        start=(c == 0), stop=(c == 7)),
                                      wk + [(('xT', kt), 0), (('xT', kt), 1)], [('ps', b)])
                                P = psb[b]
                                pk = ('ps', b)
                                if gname == "uv":
                                    pass
                                elif gname == "zz":
                                    tb = tA[kt % 2]
                                    A(lambda e, P=P, tb=tb: e.activation(out=tb[:, 0:256], in_=P[:, 0:256], func=AF.Silu), [pk], [('tA', kt % 2)])
                                    A(lambda e, P=P, kt=kt: e.activation(out=YGM[:, kt, 256:512], in_=P[:, 256:512], func=AF.Silu), [pk], [('YGMb', kt)])
                                    G(lambda e, kt=kt, tb=tb: e.tensor_tensor(out=YGM[:, kt, 0:256], in0=YGM[:, kt, 0:256], in1=tb[:, 0:256], op=ALU.mult),
                                      [('tA', kt % 2), ('YGMa', kt)], [('YGMa', kt)])
                                elif gname == "nz":
                                    A(lambda e, P=P, kt=kt: e.activation(out=SZ[:, kt, :], in_=P[:, 0:512], func=AF.Silu), [pk], [('SZ', kt)])
                                elif gname == "vg":
                                    for i in range(2):
                                        V(lambda e, P=P, kt=kt, i=i: e.tensor_copy(out=vst[i][:, kt, :, 0:64],
                                                                                 in_=P[:, i * 128:(i + 1) * 128].rearrange("p (g d) -> p g d", g=2)),
                                          [pk, ('vst1', i)], [('vst', i, kt)])
                                    A(lambda e, P=P, kt=kt: e.activation(out=GATES[:, kt, :], in_=P[:, 256:280], func=AF.Sigmoid),
                                      [pk, ('vst', 0, kt), ('vst', 1, kt)], [('GATES', kt)])
                        else:
                            nch = ncols // 128
                            roped = gname[0] in ("q", "k")
                            for tg in range(4):
                                xk = [(('xT', kt), h) for kt in range(tg * 4, tg * 4 + 4) for h in range(2)]
                                bl = []
                                for ch in range(nch):
                                    b = nbank((0, 1, 2, 3))
                                    bl.append(b)
                                    for c in range(8):
                                        T(lambda e, b=b, c=c, ch=ch, tg=tg, w=w: e.matmul(
                                            psb[b][:, :], lhsT=w[:, c, ch * 128:(ch + 1) * 128], rhs=xT[:, c, tg * 512:(tg + 1) * 512],
                                            start=(c == 0), stop=(c == 7)), wk + xk, [('ps', b)])
                                tsl = slice(tg * 512, (tg + 1) * 512)
                                if roped:
                                    b1 = bl[0]
                                    b2 = nbank((0, 1, 2, 3))
                                    ta, tb = tA[tg % 2], tB[tg % 2]
                                    pb_ = pbf[tg % 2]
                                    A(lambda e, b1=b1, pb_=pb_: e.copy(out=pb_[:], in_=psb[b1][:, :]), [('ps', b1)], [('pbf', tg % 2)])
                                    T(lambda e, b2=b2, pb_=pb_: e.matmul(psb[b2][:, :], lhsT=PMB[:], rhs=pb_[:], start=True, stop=True),
                                      [('pbf', tg % 2), 'PMB'], [('ps', b2)])
                                    V(lambda e, b1=b1, ta=ta, tsl=tsl: e.tensor_tensor(out=ta[:], in0=psb[b1][:, :], in1=ropeC[:, tsl], op=ALU.mult),
                                      [('ps', b1), ('pbf', tg % 2), 'ropeC'], [('tA', tg % 2)])
                                    V(lambda e, b2=b2, tb=tb, tsl=tsl: e.tensor_tensor(out=tb[:], in0=psb[b2][:, :], in1=ropeS[:, tsl], op=ALU.mult),
                                      [('ps', b2), 'ropeS'], [('tB', tg % 2)])
                                    if gname[0] == "q":
                                        hg = int(gname[1])
                                        dst = QT[:, tg * 4:(tg + 1) * 4, hg, :]
                                        dkey = ('QT', tg, hg)
                                        G(lambda e, ta=ta, tb=tb, dst=dst: e.tensor_tensor(out=dst, in0=ta[:].rearrange("p (a b) -> p a b", a=4),
                                                                                          in1=tb[:].rearrange("p (a b) -> p a b", a=4), op=ALU.add),
                                          [('tA', tg % 2), ('tB', tg % 2)], [dkey])
                                    else:
                                        kb_ = {"kc": 0, "ks": 1, "kw": 0}[gname]
                                        G(lambda e, ta=ta, tb=tb, kb_=kb_, tsl=tsl: e.tensor_tensor(out=kst[kb_][:, tsl], in0=ta[:], in1=tb[:], op=ALU.add),
                                          [('tA', tg % 2), ('tB', tg % 2)], [('kst', kb_, tg)])
                                else:
                                    A(lambda e, b=bl[0], tsl=tsl: e.copy(out=kst[1][:, tsl], in_=psb[b][:, :]), [('ps', bl[0])], [('kst', 1, tg)])
                                    V(lambda e, b=bl[1], tsl=tsl: e.tensor_copy(out=mqT[:, 0, tsl], in_=psb[b][:, :]), [('ps', bl[1])], [('mqT', 0, tg)])
                                    V(lambda e, b=bl[2], tsl=tsl: e.tensor_copy(out=mqT[:, 1, tsl], in_=psb[b][:, :]), [('ps', bl[2])], [('mqT', 1, tg)])
                        if gname in ("kc", "ks", "kw", "vcmq"):
                            si, kb_, bi, o = {"kc": (0, 0, 0, 0), "vcmq": (1, 1, 0, 2048), "ks": (2, 1, 1, 0), "kw": (3, 0, 1, 2048)}[gname]
                            DMA(lambda e, kb_=kb_, bi=bi, o=o: e.dma_start(out=cc_src[bi].ap()[:, o:o + NT], in_=kst[kb_][:]),
                                [('kst', kb_, tg) for tg in range(4)], [('cc_src', si)])
                        if gname == "vcmq":
                            for i in range(2):
                                DMA(lambda e, i=i: e.dma_start(out=cc_src[2 + i].ap()[:, :], in_=vst[i][:].rearrange("p k g e -> p (k g e)")),
                                    [('vst', i, kt) for kt in range(NK)], [('cc_src', 4 + i)])
                            if not os.environ.get("KNOCC"):
                                ccr = [[('cc_src', 0), ('cc_src', 1)], [('cc_src', 2), ('cc_src', 3)], [('cc_src', 4)], [('cc_src', 5)]]
                                for bi in range(4):
                                    S.op('gpsimd', lambda e, bi=bi: e.collective_compute("AllGather", ALU.bypass, replica_groups=groups,
                                                                                       ins=[cc_src[bi].ap().opt()], outs=[cc_dst[bi].ap().opt()]),
                                         reads=ccr[bi], writes=[('cc_dst', bi)], dma='cc')
                        if gname == "zz":
                            def mem_qk(kt):
                                tg = kt // 4
                                pm_ = pmL[kt % 2]
                                for half, b in ((0, 4), (1, 5)):
                                    rs = slice(half * 64, half * 64 + 64)
                                    for mt in range(2):
                                        for hh in range(2):
                                            T(lambda e, b=b, mt=mt, hh=hh, rs=rs: e.matmul(
                                                psb[b][:, (mt * 2 + hh) * 128:(mt * 2 + hh + 1) * 128],
                                                lhsT=memK[rs, hh, mt * 128:(mt + 1) * 128],
                                                rhs=mqT[rs, hh, kt * 128:(kt + 1) * 128], start=True, stop=True),
                                              [('memK', 0), ('memK', 1), ('mqT', 0, tg), ('mqT', 1, tg)], [('ps', b)])
                                    A(lambda e, b=b, half=half: e.activation(out=pm_[:, half, :, :, :].rearrange("p m h q -> p (m h q)"),
                                                                            in_=psb[b][:, :], func=AF.Exp, scale=SCALE),
                                      [('ps', b)], [('pm', kt % 2, half)])

                            def mem_pv(kt):
                                pm_ = pmL[kt % 2]
                                b = 6
                                for h in range(4):
                                    for mt in range(2):
                                        T(lambda e, h=h, mt=mt: e.matmul(psb[b][:, h * 65:(h + 1) * 65], lhsT=pm_[:, h % 2, mt, h // 2, :],
                                                                        rhs=memV[:, mt, h, :], start=(mt == 0), stop=(mt == 1)),
                                          [('pm', kt % 2, 0), ('pm', kt % 2, 1), ('memV', 0), ('memV', 1)], [('ps', b)])
                                ov = psb[b][:, 0:260].rearrange("p (h e) -> p h e", h=4)
                                V(lambda e: e.reciprocal(out=st4L[0][:, 0:4], in_=ov[:, :, 64]), [('ps', b)], [('st_sum', 0)])
                                V(lambda e: e.tensor_tensor(out=om[:], in0=ov[:, :, 0:64], in1=bc(st4L[0][:, 0:4].unsqueeze(2), [128, 4, 64]), op=ALU.mult),
                                  [('ps', b), ('st_sum', 0)], ['om'])
                                G(lambda e: e.tensor_tensor(out=YGM[:, kt, 256:512], in0=om[:].rearrange("p h d -> p (h d)"),
                                                            in1=YGM[:, kt, 256:512], op=ALU.mult), ['om', ('YGMb', kt)], [('YGMb', kt)])
                            for kt in range(NK + 1):
                                if kt < NK:
                                    mem_qk(kt)
                                if kt >= 1:
                                    mem_pv(kt - 1)
                    if debug and l == 0:
                        DMA(lambda e: e.dma_start(out=dbg["qt"][:, :], in_=QT[:].rearrange("p k h q -> p (k h q)")),
                            [('QT', tg, hg) for tg in range(4) for hg in range(4)], ['dbg_qt'])
                        DMA(lambda e: e.dma_start(out=dbg["ygm"][:, :], in_=YGM[:].rearrange("p k c -> p (k c)")),
                            [('YGMa', kt) for kt in range(NK)] + [('YGMb', kt) for kt in range(NK)], ['dbg_ygm'])
                        DMA(lambda e: e.dma_start(out=dbg["sz"][:, :], in_=SZ[:].rearrange("p k c -> p (k c)")),
                            [('SZ', kt) for kt in range(NK)], ['dbg_sz'])
                        DMA(lambda e: e.dma_start(out=dbg["gates"][:, :], in_=GATES[:].rearrange("p k c -> p (k c)")),
                            [('GATES', kt) for kt in range(NK)], ['dbg_gates'])
                        for bi in range(4):
                            DMA(lambda e, bi=bi: e.dma_start(out=dbg[f"cc{bi}"][:, :], in_=cc_dst[bi].ap()[:, :]), [('cc_dst', bi)], [f'dbg_cc{bi}'])
                except _Stop:
                    pass
                S.barrier()
                S.emit()
            if stop_after == "A":
                break
            with ExitStack() as lb:
                sbL = lambda n, s, d: lb.enter_context(nc.sbuf_tensor(f"{n}_L{l}", list(s), d))
                KcmpT = sbL("KcmpT", [128, 512], BF16)
                RHSc = sbL("RHSc", [128, 4, 2, 193], BF16)
                Wout = sbL("Wout", [128, 8, D], BF16)
                wos = sbL("wos", [128, 8, 128], F32)
                lngt = sbL("lngt", [128, D], F32)
                lnbt = sbL("lnbt", [128, D], F32)
                with ExitStack() as p0:
                    sb = lambda n, s, d: p0.enter_context(nc.sbuf_tensor(f"{n}_B0{l}", list(s), d))
                    tT = [sb("kcT", [128, 4, NT], BF16), sb("vcT", [128, 4, NT], BF16)]
                    w1s = sb("w1s", [128, 32, 128], F32)
                    w1b = [sb("w1k", [128, 32, 128], BF16), sb("w1v", [128, 32, 128], BF16)]
                    posf = sb("posf", [128, 2, 32], F32)
                    posb = sb("posb", [128, 2, 32], BF16)
                    w2f = sb("w2f", [128, 2, 64], F32)
                    w2kp = sb("w2kp", [128, 2, 128], BF16)
                    w2vb = sb("w2vb", [128, 64], BF16)
                    hid = sb("hid", [128, 4, 512], BF16)
                    pos_ins = [posk_in, posv_in]
                    for t in range(2):
                        for hf in range(2):
                            DMA(lambda e, t=t, hf=hf: e.dma_start(out=posf[hf * 64:(hf + 1) * 64, t, :], in_=pos_ins[t][l]), [], [('posf', t, hf)])
                    V(lambda e: e.tensor_copy(out=posb[:], in_=posf[:]), [('posf', t, hf) for t in range(2) for hf in range(2)], ['posb'])
                    DMA(lambda e: e.dma_start(out=w2f[:, 0, :], in_=w2k_in[l]), [], [('w2f', 0)])
                    DMA(lambda e: e.dma_start(out=w2f[:, 1, :], in_=w2v_in[l]), [], [('w2f', 1)])
                    G(lambda e: e.memset(w2kp[:], 0.0), [], ['w2kp'])
                    G(lambda e: e.tensor_copy(out=w2kp[:, 0, 0:64], in_=w2f[:, 0, :]), ['w2kp', ('w2f', 0)], ['w2kp'])
                    G(lambda e: e.tensor_copy(out=w2kp[:, 1, 64:128], in_=w2f[:, 0, :]), ['w2kp', ('w2f', 0)], ['w2kp'])
                    G(lambda e: e.tensor_copy(out=w2vb[:], in_=w2f[:, 1, :]), [('w2f', 1)], ['w2vb'])
                    w1_ins = [w1k_in, w1v_in]
                    for t in range(2):
                        if t == 1:
                            for t2_ in range(2):
                                DMA(lambda e, t2_=t2_: e.dma_start(out=tT[t2_][:], in_=cc_dst[0].ap()[:, t2_ * 2048:(t2_ + 1) * 2048].rearrange("(r p) c -> p r c", r=4)),
                                    [], [('tT', t2_)])
                        for hf in range(2):
                            DMA(lambda e, t=t, hf=hf: e.dma_start(out=w1s[hf * 64:(hf + 1) * 64, :, :],
                                                                 in_=w1_ins[t][l].rearrange("(l d) m -> d l m", d=64)), [], [('w1s', hf)])
                        V(lambda e, t=t: e.tensor_copy(out=w1b[t][:, 0:16, :], in_=w1s[:, 0:16, :]), [('w1s', 0), ('w1s', 1)], [('w1b', t, 0)])
                        A(lambda e, t=t: e.copy(out=w1b[t][:, 16:32, :], in_=w1s[:, 16:32, :]), [('w1s', 0), ('w1s', 1)], [('w1b', t, 1)])
                    DMA(lambda e: e.dma_start(out=lngt[:], in_=lng_in[l].partition_broadcast(128)), [], ['lngt'])
                    DMA(lambda e: e.dma_start(out=lnbt[:], in_=lnb_in[l].partition_broadcast(128)), [], ['lnbt'])
                    for cs in range(8):
                        DMA(lambda e, cs=cs: e.dma_start(out=wos[:], in_=wout_in[l][:, cs * 128:(cs + 1) * 128].rearrange("(c p) n -> p c n", p=128)),
                            [], ['wos'])
                        if cs % 2 == 0:
                            V(lambda e, cs=cs: e.tensor_copy(out=Wout[:, :, cs * 128:(cs + 1) * 128], in_=wos[:]), ['wos'], [('Wout', cs)])
                        else:
                            A(lambda e, cs=cs: e.copy(out=Wout[:, :, cs * 128:(cs + 1) * 128], in_=wos[:]), ['wos'], [('Wout', cs)])
                    biasS = sb("biasS", [128, 4], F32)
                    for t in range(2):
                        for g in range(2):
                            gr = slice(g * 64, g * 64 + 64)
                            bb = 4 + g
                            for li in range(32):
                                T(lambda e, bb=bb, t=t, gr=gr, li=li: e.matmul(psb[bb][:, t * 8:(t + 1) * 8], lhsT=w1b[t][gr, li, :],
                                                                              rhs=bc(posb[gr, t, li:li + 1], [64, 8]), start=(li == 0), stop=(li == 31)),
                                  [('w1b', t, 0), ('w1b', t, 1), 'posb'], [('ps', bb)])
                            V(lambda e, bb=bb, t=t, g=g: e.tensor_copy(out=biasS[:, t * 2 + g:t * 2 + g + 1], in_=psb[bb][:, t * 8:t * 8 + 1]),
                              [('ps', bb)], [('biasS', t * 2 + g)])
                    for t in range(2):
                        for g in range(2):
                            b = t * 2 + g
                            gr = slice(g * 64, g * 64 + 64)
                            rk = [('w1b', t, 0), ('w1b', t, 1), ('tT', t)]
                            pk = [('ps', b)]
                            pf = psb[b]
                            svm = tT[t][gr, :, :].rearrange("p r (k m l) -> p m r k l", k=16, m=8)
                            for li in range(32):
                                if li < 16:
                                    T(lambda e, pf=pf, svm=svm, li=li, t=t, gr=gr: e.matmul(pf[:, :], lhsT=w1b[t][gr, li, :], rhs=svm[:, :, :, :, li],
                                                                                          start=(li == 0), stop=False), rk, pk)
                                else:
                                    T(lambda e, pf=pf, svm=svm, li=li, t=t, gr=gr: e.matmul(pf[:, 0:448], lhsT=w1b[t][gr, li, :],
                                                                                          rhs=svm[:, 1:8, :, :, li - 16], start=False, stop=False), rk, pk)
                                    T(lambda e, pf=pf, svm=svm, li=li, t=t, gr=gr: e.matmul(pf[:, 448:496], lhsT=w1b[t][gr, li, :],
                                                                                          rhs=svm[:, 0, 1:4, :, li - 16], start=False, stop=False), rk, pk)
                                    T(lambda e, pf=pf, svm=svm, li=li, t=t, gr=gr: e.matmul(pf[:, 496:511], lhsT=w1b[t][gr, li, :],
                                                                                          rhs=svm[:, 0, 0, 1:16, li - 16], start=False, stop=(li == 31)), rk, pk)
                            A(lambda e, b=b: e.activation(out=hid[:, b, :].rearrange("p (rk m) -> p m rk", m=8),
                                                          in_=psb[b][:, :].rearrange("p (m rk) -> p m rk", m=8), func=AF.Gelu, bias=biasS[:, b:b + 1]),
                              pk + [('biasS', b)], [('hid', b)])
                    T(lambda e: e.matmul(psb[4][:, :], lhsT=w2kp[:, 0, :], rhs=hid[:, 0, :], start=True, stop=False), ['w2kp', ('hid', 0)], [('ps', 4)])
                    T(lambda e: e.matmul(psb[4][:, :], lhsT=w2kp[:, 1, :], rhs=hid[:, 1, :], start=False, stop=True), ['w2kp', ('hid', 1)], [('ps', 4)])
                    V(lambda e: e.tensor_copy(out=KcmpT[:], in_=psb[4][:, :]), [('ps', 4)], ['KcmpT'])
                    for g in range(2):
                        for rp in range(4):
                            T(lambda e, g=g, rp=rp: e.matmul(psb[5][:, (rp * 2 + g) * 64:(rp * 2 + g + 1) * 64], lhsT=hid[:, 2 + g, rp * 128:(rp + 1) * 128],
                                                             rhs=w2vb[:], start=True, stop=True), ['w2vb', ('hid', 2 + g)], [('ps', 5)])
                    G(lambda e: e.memset(RHSc[:, :, :, 64:65], 1.0), [], ['RHSc1'])
                    V(lambda e: e.tensor_copy(out=RHSc[:, :, :, 0:64], in_=psb[5][:, :].rearrange("p (r g d) -> p r g d", r=4, g=2)),
                      [('ps', 5), 'RHSc1'], ['RHScV'])
                    for g in range(2):
                        DMA(lambda e, g=g: e.dma_start(out=RHSc[:, :, g, 65:193], in_=ov_in[:, :, :]), [], [('RHScO', g)])
                    if debug and l == 0:
                        DMA(lambda e: e.dma_start(out=dbg["kcmp"][:, :], in_=KcmpT[:]), ['KcmpT'], ['dbg_kcmp'])
                        DMA(lambda e: e.dma_start(out=dbg["rhsc"][:, :], in_=RHSc[:].rearrange("p r g e -> p (r g e)")),
                            ['RHScV', 'RHSc1', ('RHScO', 0), ('RHScO', 1)], ['dbg_rhsc'])
                    S.barrier()
                    S.emit()
                if stop_after == "B0":
                    break
                with ExitStack() as p1:
                    sb = lambda n, s, d: p1.enter_context(nc.sbuf_tensor(f"{n}_B1{l}", list(s), d))
                    Kaug = [sb(f"Kaug{g}", [128, 64, 128], BF16) for g in range(2)]
                    Kwin = sb("Kwin", [128, 64, 128], BF16)
                    Vs = sb("Vs", [128, 64, 2, 65], BF16)
                    Vw = sb("Vw", [128, 64, 2, 65], BF16)
                    Qaug = [sb(f"Qaug{g}", [128, 3, 512], BF16) for g in range(2)]
                    Pt = [sb(f"Pt{i}", [128, 512], BF16) for i in range(4)]
                    vmk = [sb(f"vmk{i}", [128, 1, 128], BF16) for i in range(2)]
                    imp = sb("imp", [128, 128], F32)
                    impm = sb("impm", [128, 128], F32)
                    wkt = sb("wkt", [128, 128], F32)
                    selm = sb("selm", [128, 128], F32)
                    m8 = sb("m8", [128, 16], F32)
                    thr = sb("thr", [128, 1], F32)
                    NBt = sb("NBt", [128, 192], BF16)
                    rsA = sb("rsA", [128, 12], F32)
                    riA = sb("riA", [128, 12], F32)
                    fA = sb("fA", [128, 12], F32)
                    oacc = sb("oacc", [128, 4, 64], F32)
                    t2 = sb("t2", [128, 4, 64], F32)
                    t3 = sb("t3", [128, 4, 64], F32)
                    mixn = sb("mixn", [128, 512], BF16)
                    mixT = sb("mixT", [128, 8, 128], BF16)
                    xblk = sb("xblk", [128, D], F32)
                    zb = sb("zb", [128, D], F32)
                    stt = sb("stt", [128, 12], F32)
                    mv = sb("mv", [128, 4], F32)
                    zeroB = sb("zeroB", [128, 386], BF16)
                    G(lambda e: e.memset(zeroB[:], 0.0), [], ['zeroB'])
                    mhalfB = sb("mhalfB", [128, 1], F32)
                    G(lambda e: e.memset(mhalfB[:], -0.5), [], ['mhalfB'])
                    G(lambda e: e.memset(NBt[:], 0.0), [], ['NBa'])
                    accS = sb("accS", [128, 1292], F32)

                    try:
                        ksrc = cc_dst[1].ap()[:, 0:2048].rearrange("(r p) c -> p r c", r=4)
                        DMA(lambda e: e.dma_start(out=Kaug[0][0:64, :, :].rearrange("p (r k) c -> p r (k c)", r=4), in_=ksrc[0:64]), [], [('Kaug', 0, 'k')])
                        DMA(lambda e: e.dma_start(out=Kaug[1][64:128, :, :].rearrange("p (r k) c -> p r (k c)", r=4), in_=ksrc[64:128]), [], [('Kaug', 1, 'k')])
                        DMA(lambda e: e.dma_start(out=Kaug[0][64:128, :, :].rearrange("p s c -> p (s c)"), in_=eind_in[:, :]), [], [('Kaug', 0, 'e')])
                        DMA(lambda e: e.dma_start(out=Kaug[1][0:64, :, :].rearrange("p s c -> p (s c)"), in_=eind_in[:, :]), [], [('Kaug', 1, 'e')], q='gpsimd')
                        DMA(lambda e: e.dma_start(out=Kwin[:].rearrange("p (r k) c -> p r (k c)", r=4),
                                                  in_=cc_dst[1].ap()[:, 2048:4096].rearrange("(r p) c -> p r c", r=4)), [], ['Kwin'], q='gpsimd')
                        DMA(lambda e: e.dma_start(out=Vs[:].rearrange("p (r k) g e -> p r (k g e)", r=4),
                                                  in_=cc_dst[2].ap()[:, :].rearrange("(r p) c -> p r c", r=4)), [], ['Vs'])
                        DMA(lambda e: e.dma_start(out=Vw[:].rearrange("p (r k) g e -> p r (k g e)", r=4),
                                                  in_=cc_dst[3].ap()[:, :].rearrange("(r p) c -> p r c", r=4)), [], ['Vw'], q='gpsimd')
                        Wk = [('Wout', cs) for cs in range(8)]
                        G(lambda e: e.memset(Qaug[0][64:128, 2, :], 0.0), [], [('Qz', 0)])
                        G(lambda e: e.memset(Qaug[1][0:64, 2, :], 0.0), [], [('Qz', 1)])
                        KaugK = [[('Kaug', g, 'k'), ('Kaug', g, 'e')] for g in range(2)]

                        tiles = []

                        def mk_group(k, g):
                            M = 8 * (k + 1)
                            sk = 128 - 8 * k
                            gr = slice(g * 64, g * 64 + 64)
                            mr = slice(64, 128) if g == 0 else slice(0, 64)
                            qk = ('Qq', g)
                            vk = ('vmk', k % 2)
                            vm_ = vmk[k % 2]
                            ob = [psb[i][:, 0:386].rearrange("p (h e) -> p h e", h=2) for i in range(2)]

                            def group_begin():
                                if g == 0:
                                    DMA(lambda e: e.dma_start(out=vmk[k % 2][:], in_=vm_in[:, k, :, :]), [], [('vmk', k % 2)])

                            def q_copy():
                                V(lambda e: e.tensor_copy(out=Qaug[g][gr, :, :],
                                                          in_=bc(QT[gr, k, :, :].rearrange("p h q -> p (h q)").unsqueeze(1), [64, 3, 512])),
                                  [], [qk])
                            qcopies.append(q_copy)

                            def zero_acc(banks):
                                def f():
                                    for b_, n_ in banks:
                                        T(lambda e, b_=b_, n_=n_: e.matmul(psb[b_][:, 0:n_], lhsT=zeroB[:, 0:128], rhs=zeroB[:, 0:n_], start=True, stop=False),
                                          ['zeroB'], [('ps', b_)])
                                return f

                            for rp in range(4):
                                def qk_c(b, rp=rp):
                                    T(lambda e: e.matmul(psb[b][0:M, :], lhsT=KcmpT[:, rp * 128:rp * 128 + M], rhs=Qaug[g][:, 2, :],
                                                         start=True, stop=False), [qk, ('Qz', g)], [('ps', b)])
                                    T(lambda e: e.matmul(psb[b][0:M, :].rearrange("p (h q) -> p h q", h=4), lhsT=ZC[:, sk:sk + M],
                                                         rhs=bc(BCMP[:, rp, :].unsqueeze(1), [128, 4, 128]), start=False, stop=True), [], [('ps', b)])

                                def exp_c(b, pi):
                                    A(lambda e: e.activation(out=Pt[pi][0:M, :], in_=psb[b][0:M, :], func=AF.Exp, scale=SCALE), [('ps', b)], [('Pt', pi)])

                                def pv_c(pi, rp=rp):
                                    for h in range(4):
                                        bo, co = h // 2, (h % 2) * 193
                                        T(lambda e, h=h, bo=bo, co=co: e.matmul(psb[bo][:, co:co + 193], lhsT=Pt[pi][0:M, h * 128:(h + 1) * 128],
                                                                               rhs=RHSc[0:M, rp, g, :], start=False, stop=(rp == 3)),
                                          [('Pt', pi)], [('ps', bo)])
                                t = dict(qk=qk_c, exp=exp_c, pv=pv_c, pre_qk=[], pre_pv=[], post_pv=[])
                                if rp == 0:
                                    t['pre_qk'].append(group_begin)
                                    t['pre_pv'].append(zero_acc([(0, 386), (1, 386)]))
                                tiles.append(t)

                            def imp_chain():
                                for i in range(2):
                                    V(lambda e, i=i: e.tensor_scalar(out=rsA[:, 2 * i:2 * i + 2], in0=ob[i][:, :, 64], scalar1=1e-30, scalar2=None, op0=ALU.max),
                                      [('ps', i)], [('rsA', i)])
                                V(lambda e: e.reciprocal(out=riA[:, 0:4], in_=rsA[:, 0:4]), [('rsA', 0), ('rsA', 1)], ['riAc'])
                                V(lambda e: e.scalar_tensor_tensor(out=imp[:], in0=ob[0][:, 0, 65:193], scalar=riA[:, 0:1], in1=vm_[:, 0, :],
                                                                   op0=ALU.mult, op1=ALU.add), [('ps', 0), 'riAc', vk], ['imp'])
                                for h in range(1, 4):
                                    V(lambda e, h=h: e.scalar_tensor_tensor(out=imp[:], in0=ob[h // 2][:, h % 2, 65:193], scalar=riA[:, h:h + 1], in1=imp[:],
                                                                            op0=ALU.mult, op1=ALU.add), [('ps', h // 2), 'riAc', 'imp'], ['imp'])
                                V(lambda e: e.max(out=m8[:, 0:8], in_=imp[:]), ['imp'], ['m8a'])
                                V(lambda e: e.match_replace(out=wkt[:], in_to_replace=m8[:, 0:8], in_values=imp[:], imm_value=-1e30), ['imp', 'm8a'], ['wkt'])
                                V(lambda e: e.max(out=m8[:, 8:16], in_=wkt[:]), ['wkt'], ['m8b'])
                                V(lambda e: e.tensor_scalar(out=selm[:], in0=imp[:], scalar1=m8[:, 15:16], scalar2=None, op0=ALU.is_ge), ['imp', 'm8b'], ['selm'])
                                V(lambda e: e.tensor_scalar(out=NBt[:, 64:192], in0=selm[:], scalar1=-1.0, scalar2=BIG, op0=ALU.add, op1=ALU.mult), ['selm'], ['NBa'])
                                V(lambda e: e.tensor_copy(out=accS[:, 0:386], in_=psb[0][:, 0:386]), [('ps', 0)], [('accS', 0)])
                                V(lambda e: e.tensor_copy(out=accS[:, 386:772], in_=psb[1][:, 0:386]), [('ps', 1)], [('accS', 1)])
                            tiles[-1]['post_pv'].append(imp_chain)

                            wt = [(rp, dk) for dk in range(2) for rp in range(4) if k - 1 + dk >= 0]
                            for idx, (rp, dk) in enumerate(wt):
                                slot = rp * 16 + k - 1 + dk

                                def qk_w(b, slot=slot, rp=rp, dk=dk):
                                    T(lambda e: e.matmul(psb[b][:, :], lhsT=Kwin[:, slot, :], rhs=Qaug[g][:, 2, :], start=True, stop=False),
                                      [qk, ('Qz', g), 'Kwin'], [('ps', b)])
                                    T(lambda e: e.matmul(psb[b][:, :].rearrange("p (h q) -> p h q", h=4), lhsT=identB[:],
                                                         rhs=bc(WMB[:, rp * 2 + dk, :].unsqueeze(1), [128, 4, 128]), start=False, stop=True), [], [('ps', b)])

                                def exp_f(b, pi):
                                    A(lambda e: e.activation(out=Pt[pi][:], in_=psb[b][:, :], func=AF.Exp, scale=SCALE), [('ps', b)], [('Pt', pi)])

                                def pv_w(pi, slot=slot, last=(idx == len(wt) - 1)):
                                    for h in range(4):
                                        T(lambda e, h=h: e.matmul(psb[3][:, h * 65:(h + 1) * 65], lhsT=Pt[pi][:, h * 128:(h + 1) * 128], rhs=Vw[:, slot, g, :],
                                                                  start=False, stop=last), [('Pt', pi), 'Vw'], [('ps', 3)])
                                t = dict(qk=qk_w, exp=exp_f, pv=pv_w, pre_qk=[], pre_pv=[], post_pv=[])
                                if idx == 0:
                                    t['pre_pv'].append(zero_acc([(3, 260)]))
                                    firstwin.append(t)
                                tiles.append(t)

                            def mask_to_q():
                                if g == 0:
                                    for ver, (w0, w1) in enumerate([(0, 128), (64, 192)]):
                                        T(lambda e, ver=ver, w0=w0, w1=w1: e.transpose(out=psT[:, ver * 128:(ver + 1) * 128], in_=NBt[:, w0:w1], identity=identB[:]),
                                          ['NBa'], ['psT'])
                                else:
                                    for ver, (w0, w1) in enumerate([(64, 128), (128, 192)]):
                                        T(lambda e, ver=ver, w0=w0, w1=w1: e.transpose(out=psT[0:64, ver * 128:(ver + 1) * 128], in_=NBt[:, w0:w1], identity=identB[:]),
                                          ['NBa'], ['psT'])
                                for ver in range(2):
                                    V(lambda e, ver=ver: e.tensor_copy(out=Qaug[g][mr, ver, :].rearrange("p (h q) -> p h q", h=4),
                                                                       in_=bc(psT[mr, ver * 128:(ver + 1) * 128].unsqueeze(1), [64, 4, 128])),
                                      ['psT'], [('Qm', g, ver)])
                            st_ = [(rp, kp) for kp in range(k + 1) for rp in range(4)]
                            for idx, (rp, kp) in enumerate(st_):
                                slot = rp * 16 + kp
                                ver = 0 if rp < 2 else 1
                                diag = (kp == k)

                                def qk_s(b, slot=slot, ver=ver, diag=diag, rp=rp):
                                    T(lambda e: e.matmul(psb[b][:, :], lhsT=Kaug[g][:, slot, :], rhs=Qaug[g][:, ver, :], start=True, stop=(not diag)),
                                      [qk, ('Qm', g, ver)] + KaugK[g], [('ps', b)])
                                    if diag:
                                        T(lambda e: e.matmul(psb[b][:, :].rearrange("p (h q) -> p h q", h=4), lhsT=identB[:],
                                                             rhs=bc(DMB[:, rp, :].unsqueeze(1), [128, 4, 128]), start=False, stop=True), [], [('ps', b)])

                                def pv_s(pi, slot=slot, last=(idx == len(st_) - 1)):
                                    for h in range(4):
                                        T(lambda e, h=h: e.matmul(psb[2][:, h * 65:(h + 1) * 65], lhsT=Pt[pi][:, h * 128:(h + 1) * 128], rhs=Vs[:, slot, g, :],
                                                                  start=False, stop=last), [('Pt', pi), 'Vs'], [('ps', 2)])
                                t = dict(qk=qk_s, exp=exp_f, pv=pv_s, pre_qk=[], pre_pv=[], post_pv=[])
                                if idx == 0:
                                    t['pre_qk'].append(mask_to_q)
                                    t['pre_pv'].append(zero_acc([(2, 260)]))
                                tiles.append(t)

                            def combine():
                                V(lambda e: e.tensor_copy(out=accS[:, 772:1032], in_=psb[2][:, 0:260]), [('ps', 2)], [('accS', 2)])
                                V(lambda e: e.tensor_copy(out=accS[:, 1032:1292], in_=psb[3][:, 0:260]), [('ps', 3)], [('accS', 3)])
                                ocv = accS[:, 0:772].rearrange("p (h e) -> p h e", h=4)
                                osv = accS[:, 772:1032].rearrange("p (h e) -> p h e", h=4)
                                owv = accS[:, 1032:1292].rearrange("p (h e) -> p h e", h=4)
                                V(lambda e: e.tensor_scalar(out=rsA[:, 4:8], in0=osv[:, :, 64], scalar1=1e-30, scalar2=None, op0=ALU.max), [('accS', 2)], [('rsA', 2)])
                                V(lambda e: e.tensor_scalar(out=rsA[:, 8:12], in0=owv[:, :, 64], scalar1=1e-30, scalar2=None, op0=ALU.max), [('accS', 3)], [('rsA', 3)])
                                V(lambda e: e.reciprocal(out=riA[:, 4:12], in_=rsA[:, 4:12]), [('rsA', 2), ('rsA', 3)], ['riAs'])
                                V(lambda e: e.tensor_tensor(out=fA[:, :].rearrange("p (c h) -> p c h", c=3), in0=riA[:, :].rearrange("p (c h) -> p c h", c=3),
                                                            in1=GATES[:, k, g * 12:(g + 1) * 12].rearrange("p (h c) -> p c h", c=3), op=ALU.mult),
                                  ['riAc', 'riAs'], ['fA'])
                                V(lambda e: e.tensor_tensor(out=oacc[:], in0=ocv[:, :, 0:64], in1=bc(fA[:, 0:4].unsqueeze(2), [128, 4, 64]), op=ALU.mult),
                                  [('accS', 0), ('accS', 1), 'fA'], ['oacc'])
                                G(lambda e: e.tensor_tensor(out=t2[:], in0=osv[:, :, 0:64], in1=bc(fA[:, 4:8].unsqueeze(2), [128, 4, 64]), op=ALU.mult),
                                  [('accS', 2), 'fA'], ['t2'])
                                G(lambda e: e.tensor_tensor(out=t3[:], in0=owv[:, :, 0:64], in1=bc(fA[:, 8:12].unsqueeze(2), [128, 4, 64]), op=ALU.mult),
                                  [('accS', 3), 'fA'], ['t3'])
                                G(lambda e: e.tensor_tensor(out=oacc[:], in0=oacc[:], in1=t2[:], op=ALU.add), ['oacc', 't2'], ['oacc'])
                                G(lambda e: e.tensor_tensor(out=oacc[:], in0=oacc[:], in1=t3[:], op=ALU.add), ['oacc', 't3'], ['oacc'])
                                G(lambda e: e.tensor_tensor(out=mixn[:, g * 256:(g + 1) * 256], in0=oacc[:].rearrange("p h d -> p (h d)"),
                                                            in1=SZ[:, k, g * 256:(g + 1) * 256], op=ALU.mult), ['oacc'], [('mixn', g)])

                            def bo_a():
                                for c in range(8):
                                    if c < 2:
                                        src, rk_ = YGM[:, k, c * 128:(c + 1) * 128], []
                                    elif c < 6:
                                        src, rk_ = mixn[:, (c - 2) * 128:(c - 1) * 128], [('mixn', (c - 2) // 2)]
                                    else:
                                        src, rk_ = YGM[:, k, 256 + (c - 6) * 128:256 + (c - 5) * 128], []
                                    T(lambda e, c=c, src=src: e.transpose(out=psT[:, c * 128:(c + 1) * 128], in_=src, identity=identB[:]), rk_, ['psT'])
                                V(lambda e: e.tensor_copy(out=mixT[:].rearrange("p c t -> p (c t)"), in_=psT[:, :]), ['psT'], ['mixT'])

                            def bo_b(j):
                                def f():
                                    half = j // 4
                                    for c in (2 * (j % 4), 2 * (j % 4) + 1):
                                        T(lambda e, half=half, c=c: e.matmul(psb[half][:, :], lhsT=mixT[:, c, :], rhs=Wout[:, c, half * 512:(half + 1) * 512],
                                                                             start=(c == 0), stop=(c == 7)), ['mixT'], [('ps', half)])
                                return f

                            def bo_c():
                                yb = [0, 1]
                                for half in range(2):
                                    hs = slice(half * 512, (half + 1) * 512)
                                    V(lambda e, half=half, hs=hs: e.scalar_tensor_tensor(out=zb[:, hs], in0=xblk[:, hs], scalar=ALPHA, in1=psb[yb[half]][:, :],
                                                                                         op0=ALU.mult, op1=ALU.add), [('ps', yb[half]), 'xblk'], [('zb', half)])
                                    V(lambda e, half=half, hs=hs: e.bn_stats(out=stt[:, half * 6:(half + 1) * 6], in_=zb[:, hs]), [('zb', half)], [('stt', half)])
                                V(lambda e: e.bn_aggr(out=mv[:, 0:2], in_=stt[:, :]), [('stt', 0), ('stt', 1)], ['mv'])
                                V(lambda e: e.tensor_scalar(out=mv[:, 2:3], in0=mv[:, 1:2], scalar1=LN_EPS, scalar2=None, op0=ALU.add), ['mv'], ['mv2'])
                                G(lambda e: e.tensor_tensor(out=mv[:, 3:4], in0=mv[:, 2:3], in1=mhalfB[:, 0:1], op=ALU.pow), ['mv2', 'mhalfB'], ['mv3'])
                                V(lambda e: e.tensor_scalar(out=zb[:], in0=zb[:], scalar1=mv[:, 0:1], scalar2=mv[:, 3:4], op0=ALU.subtract, op1=ALU.mult),
                                  [('zb', 0), ('zb', 1), 'mv', 'mv3'], [('zb', 0), ('zb', 1)])
                                V(lambda e: e.tensor_tensor(out=zb[:], in0=zb[:], in1=lngt[:], op=ALU.mult), [('zb', 0), ('zb', 1), 'lngt'], [('zb', 0), ('zb', 1)])
                                V(lambda e: e.tensor_tensor(out=zb[:], in0=zb[:], in1=lnbt[:], op=ALU.add), [('zb', 0), ('zb', 1), 'lnbt'], [('zb', 0), ('zb', 1)])
                                DMA(lambda e: e.dma_start(out=xdst[k * 128:(k + 1) * 128, :], in_=zb[:]), [('zb', 0), ('zb', 1)], [('xdst', k)])
                                if k + 1 < NK:
                                    DMA(lambda e: e.dma_start(out=xblk[:], in_=xsrc[(k + 1) * 128:(k + 2) * 128, :]), [], ['xblk'])
                            tiles[-1]['post_pv'].append(combine)
                            if g == 1:
                                L_ = len(tiles) - 1
                                D_ = min(14, 12 + 4 * (k + 2) - 10)
                                deferred.append((L_ + D_, bo_a))
                                for j in range(8):
                                    deferred.append((L_ + D_ + 1 + j, bo_b(j)))
                                deferred.append((L_ + D_ + 9, bo_c))

                        DEFER = 14
                        deferred = []
                        DMA(lambda e: e.dma_start(out=xblk[:], in_=xsrc[0:128, :]), [], ['xblk'])
                        qcopies, firstwin = [], []
                        for k in range(NK):
                            for g in range(2):
                                mk_group(k, g)
                        tiles[0]['pre_qk'].insert(0, qcopies[0])
                        for n_ in range(1, len(qcopies)):
                            firstwin[n_ - 1]['pre_qk'].insert(0, qcopies[n_])
                        tail_hooks = []
                        for ti, fn in deferred:
                            if ti < len(tiles):
                                tiles[ti]['post_pv'].append(fn)
                            else:
                                tail_hooks.append(fn)
                        LOOK = 2
                        nt_ = len(tiles)
                        binfo = {}
                        for idx in range(nt_ + LOOK):
                            if idx < nt_:
                                t = tiles[idx]
                                for f in t['pre_qk']:
                                    f()
                                b = nbank((4, 5, 6))
                                binfo[idx] = b
                                t['qk'](b)
                            i = idx - LOOK
                            if i >= 0:
                                t = tiles[i]
                                pi = i % 4
                                t['exp'](binfo[i], pi)
                                for f in t['pre_pv']:
                                    f()
                                t['pv'](pi)
                                for f in t['post_pv']:
                                    f()
                        for f in tail_hooks:
                            f()
                    except _Stop:
                        pass
                    S.barrier()
                    S.emit()
        S.barrier()
        S.emit()
    return nc


def _bf(a):
    return np.asarray(a, dtype=np.float32).astype(ml_dtypes.bfloat16)


def _consts_common():
    eind = np.zeros((64, 64, 128), np.float32)
    for s in range(64):
        for half in range(2):
            eind[(2 * s + half) % 64, s, half * 64:(half + 1) * 64] = 1.0
    zc = np.zeros((128, 256), np.float32)
    for e in range(16):
        zc[e, e + 120] = 1.0
    ov = np.zeros((128, 4, 128), np.float32)
    for rp in range(4):
        for kp in range(16):
            for m in range(8):
                n = 8 * (4 * kp + rp) + m
                for jb in ([n // 4] + ([n // 4 + 1] if n % 4 == 3 else [])):
                    if jb > 127:
                        continue
                    j2, half = jb // 2, jb % 2
                    beta = 2 * ((j2 % 4) * 16 + j2 // 4) + half
                    ov[8 * kp + m, rp, beta] = 1.0
    return _bf(eind.reshape(64, SEQ)), _bf(zc), _bf(ov)


def _consts_core(r):
    p = np.arange(128)
    bcmp = np.zeros((128, 4, 128), np.float32)
    for dk in range(2):
        for m in range(8):
            for rp in range(4):
                dj = 4 * (dk - 1) + rp - r
                ok = (128 * dj + 16 * m + 31) <= p
                bcmp[dk * 8 + m, rp, :] = np.where(ok, 0.0, -BIG)
    c = np.arange(128)[:, None]
    q = np.arange(128)[None, :]
    dmb = np.zeros((128, 4, 128), np.float32)
    for rp in range(4):
        if rp < r:
            dmb[:, rp, :] = 0.0
        elif rp > r:
            dmb[:, rp, :] = -BIG
        else:
            dmb[:, rp, :] = np.where(c <= q, 0.0, -BIG)
    wmb = np.zeros((128, 4, 2, 128), np.float32)
    for rp in range(4):
        for dk in range(2):
            dj = 4 * (dk - 1) + rp - r
            if dj == 0:
                wmb[:, rp, dk, :] = np.where(c <= q, 0.0, -BIG)
            elif dj in (-1, -2, -3):
                wmb[:, rp, dk, :] = 0.0
            elif dj == -4:
                wmb[:, rp, dk, :] = np.where(c > q, 0.0, -BIG)
            else:
                wmb[:, rp, dk, :] = -BIG
    beta = np.arange(128)
    s_ = beta // 2
    jb = 2 * (4 * (s_ % 16) + s_ // 16) + beta % 2
    vm = np.zeros((128, NK, 1, 128), np.float32)
    for k in range(NK):
        i = 4 * k + r
        tblk = (2 * i + (p >= 64))[:, None]
        valid = jb[None, :] <= tblk
        forced = (jb[None, :] == 0) | (valid & (jb[None, :] > tblk - 2))
        vm[:, k, 0, :] = np.where(forced, 100.0, np.where(valid, 0.0, -100.0))
    return _bf(bcmp), _bf(dmb), _bf(wmb.reshape(128, 8, 128)), _bf(vm)


def _rope_tables(r):
    blocks = 4 * np.arange(NK) + r
    pos = (blocks[:, None] * 128 + np.arange(128)[None, :]).reshape(-1).astype(np.float32)
    inv = (np.float32(10000.0) ** (-np.arange(32, dtype=np.float32) * np.float32(2.0) / np.float32(64))).astype(np.float32)
    ang = (pos[None, :] * inv[:, None]).astype(np.float32)
    cos = np.cos(ang).astype(np.float32)
    sin = np.sin(ang).astype(np.float32)
    C = np.concatenate([cos, cos, cos, cos], axis=0)
    Sg = np.concatenate([-sin, sin, -sin, sin], axis=0)
    return np.ascontiguousarray(C), np.ascontiguousarray(Sg)


def make_in_maps(inputs):
    f = lambda k: np.asarray(inputs[k], dtype=np.float32)
    x, mem = f("x"), f("mem")
    w_in = f("w_in")
    allcols = np.concatenate([g[2] for g in COLG])
    shared = {
        "w_in_p": np.ascontiguousarray(w_in[:, :, allcols]),
        "gm_ln_g": f("gm_ln_g").reshape(DEPTH, 1, 256),
        "gm_ln_b": f("gm_ln_b").reshape(DEPTH, 1, 256),
        "gm_ws": f("gm_ws"),
        "gm_bsT": np.ascontiguousarray(f("gm_bs").transpose(0, 2, 1)),
        "cmp_pos_kT": np.ascontiguousarray(f("cmp_pos_k").transpose(0, 2, 1)),
        "cmp_pos_vT": np.ascontiguousarray(f("cmp_pos_v").transpose(0, 2, 1)),
        "cmp_k_w1": f("cmp_k_w1"), "cmp_v_w1": f("cmp_v_w1"),
        "cmp_k_w2": f("cmp_k_w2"), "cmp_v_w2": f("cmp_v_w2"),
        "w_mem_kv": f("w_mem_kv"), "w_out": f("w_out"),
        "ln_g": f("ln_g").reshape(DEPTH, 1, D), "ln_b": f("ln_b").reshape(DEPTH, 1, D),
    }
    eind, zc, ov = _consts_common()
    pm = np.zeros((128, 128), np.float32)
    for pq in range(128):
        pm[(pq // 64) * 64 + (pq % 64 + 32) % 64, pq] = 1.0
    shared.update({"eind": eind, "zc": zc, "ovc": ov, "pswap": _bf(pm)})
    maps = []
    for c in range(8):
        b, r = c // 4, c % 4
        blocks = 4 * np.arange(NK) + r
        m = dict(shared)
        m["x_own"] = np.ascontiguousarray(x[b].reshape(64, 128, D)[blocks].reshape(NT, D))
        m["mem_b"] = np.ascontiguousarray(mem[b])
        C, Sg = _rope_tables(r)
        m["ropeC"], m["ropeS"] = C, Sg
        bcmp, dmb, wmb, vm = _consts_core(r)
        m.update({"bcmp": bcmp, "dmb": dmb, "wmb": wmb, "vmc": vm})
        maps.append(m)
    return maps


def assemble(results):
    out = np.zeros((NB, SEQ, D), np.float32)
    for c in range(8):
        b, r = c // 4, c % 4
        blocks = 4 * np.arange(NK) + r
        o = np.asarray(results[c]["out"], dtype=np.float32).reshape(NK, 128, D)
        out[b].reshape(64, 128, D)[blocks] = o
    return out


_NC_CACHE = {}


def kernel(**inputs):
    if "nc" not in _NC_CACHE:
        _NC_CACHE["nc"] = build()
    nc = _NC_CACHE["nc"]
    maps = make_in_maps(inputs)
    res = run_bass_kernel_spmd(nc, maps, core_ids=list(range(8)))
    return assemble(res.results)
```
